# Optimizing a Trainium2 kernel written in Bass

```python
import math
import jax
import jax.numpy as jnp
from jax import lax
import numpy as np

D_MODEL = 2048
BATCH = 4
SEQ = 2048
DEPTH = 4

GRID_W = 64
CTX_LEN = 256
HEAD_DIM = 128
HY_DIM = D_MODEL // 2
HY_ORDER = 2
HY_SHORT = 3
HY_BANDS = 16
HY_EMB = 2 * HY_BANDS + 1
HY_FILT_HID = 64
HY_FILT_INNER = 2
HY_FILT_SCALE = 0.1
HY_FAST_DECAY = 0.3
HY_SLOW_DECAY = 1.5
HY_TARGET = 1e-2
SWA_HEADS = (D_MODEL - HY_DIM) // HEAD_DIM
SWA_KV_HEADS = SWA_HEADS // 4
SWA_WINDOW = 128
SWA_BLOCK = 128
NA_HEADS = D_MODEL // HEAD_DIM
NA_KH = 8
NA_KW = 16
ROPE_BASE = 10000.0
N_EXPERTS = 16
EC_CAPACITY = 2
EXPERT_FF = D_MODEL // 2
N_EVEN = (DEPTH + 1) // 2
N_ODD = DEPTH // 2
DN_ALPHA = (2 * DEPTH) ** 0.25
DN_BETA = (8 * DEPTH) ** -0.25
LN_EPS = 1e-5
EV_Q0 = 3 * HY_DIM
EV_K0 = EV_Q0 + SWA_HEADS * HEAD_DIM
EV_V0 = EV_K0 + SWA_KV_HEADS * HEAD_DIM
EV_IN = EV_V0 + SWA_KV_HEADS * HEAD_DIM
EV_CAT = HY_DIM + SWA_HEADS * HEAD_DIM
F32 = jnp.float32

kernel_name = 'hybrid_hyena_swa_natten_ec_dit'


def layer_norm(x, g, b):
    xf = x.astype(F32)
    mu = jnp.mean(xf, axis=-1, keepdims=True)
    var = jnp.mean(jnp.square(xf - mu), axis=-1, keepdims=True)
    y = (xf - mu) * lax.rsqrt(var + LN_EPS) * g.astype(F32) + b.astype(F32)
    return y.astype(x.dtype)


def axial_rope(x):
    L, d = x.shape[1], x.shape[-1]
    t = jnp.arange(L)
    half = d // 2
    nf = half // 2
    inv = ROPE_BASE ** (-2.0 * jnp.arange(nf, dtype=F32) / half)
    parts = []
    for a, pos in enumerate((t // GRID_W, t % GRID_W)):
        xa = x[..., a * half:(a + 1) * half].astype(F32)
        ang = pos.astype(F32)[:, None] * inv[None, :]
        cos = jnp.cos(ang)[None, :, None, :]
        sin = jnp.sin(ang)[None, :, None, :]
        x1, x2 = xa[..., :nf], xa[..., nf:]
        parts.append(jnp.concatenate([x1 * cos - x2 * sin, x2 * cos + x1 * sin], axis=-1))
    return jnp.concatenate(parts, axis=-1).astype(x.dtype)


def short_conv(u, w, b):
    L = u.shape[1]
    p = HY_SHORT // 2
    up = jnp.pad(u, ((0, 0), (p, HY_SHORT - 1 - p), (0, 0)))
    y = b
    for j in range(HY_SHORT):
        y = y + up[:, j:j + L] * w[j]
    return y


def hyena_filters(L, fw1, fb1, fw2, fb2, fw3, ffreq):
    t = jnp.linspace(0.0, 1.0, L, dtype=F32)[:, None]
    w = (2.0 * math.pi / L) * jnp.arange(L, dtype=F32)[:, None]
    f = jnp.linspace(1e-4, HY_BANDS - 1, HY_BANDS, dtype=F32)[None, :]
    z = jnp.concatenate([t, jnp.cos(f * w), -jnp.sin(f * w)], axis=-1)
    fr = ffreq.astype(F32)
    h = jnp.sin(fr * (z @ fw1.astype(F32) + fb1.astype(F32)))
    for i in range(HY_FILT_INNER):
        h = jnp.sin(fr * (h @ fw2[i].astype(F32) + fb2[i].astype(F32)))
    h = (h @ fw3.astype(F32)).reshape(L, HY_ORDER, 2, HY_DIM)
    deltas = jnp.abs(jnp.linspace(math.log(HY_TARGET) / HY_SLOW_DECAY, math.log(HY_TARGET) / HY_FAST_DECAY, HY_DIM, dtype=F32))
    decay = jnp.exp(-t * deltas[None, :])
    return h * decay[:, None, None, :]


def bidir_long_conv(z, hf, hb, bias):
    B, L, C = z.shape
    kc = jnp.concatenate([hf.at[0].add(hb[0]), jnp.zeros((1, C), F32), hb[:0:-1]], axis=0)
    zf = jnp.fft.rfft(z.astype(F32), n=2 * L, axis=1)
    kf = jnp.fft.rfft(kc, n=2 * L, axis=0)
    y = jnp.fft.irfft(zf * kf[None], n=2 * L, axis=1)[:, :L]
    return (y + z.astype(F32) * bias.astype(F32)).astype(z.dtype)


def hyena_mix(u, conv_w, conv_b, filt, hy_bias):
    L = u.shape[1]
    u = short_conv(u, conv_w, conv_b)
    parts = jnp.split(u, HY_ORDER + 1, axis=-1)
    h = hyena_filters(L, *filt)
    z = parts[-1]
    for n in range(HY_ORDER):
        z = parts[n] * bidir_long_conv(z, h[:, n, 0], h[:, n, 1], hy_bias[n])
    return z


def window_gqa_attn(q, k, v, kc, vc, sink):
    B, L, Hq, d = q.shape
    Hkv = k.shape[2]
    G = Hq // Hkv
    W = SWA_BLOCK
    nb = L // W
    scale = d ** -0.5
    qb = q.reshape(B, nb, W, Hkv, G, d)

    def band(a):
        ap = jnp.pad(a, ((0, 0), (W, W), (0, 0), (0, 0))).reshape(B, nb + 2, W, Hkv, d)
        return jnp.concatenate([ap[:, :-2], ap[:, 1:-1], ap[:, 2:]], axis=2)

    kb, vb = band(k), band(v)
    s_loc = jnp.einsum('bnqhgd,bnkhd->bnhgqk', qb, kb).astype(F32) * scale
    qpos = jnp.arange(nb)[:, None] * W + jnp.arange(W)[None, :]
    kpos = jnp.arange(nb)[:, None] * W - W + jnp.arange(3 * W)[None, :]
    ok = (jnp.abs(qpos[:, :, None] - kpos[:, None, :]) <= SWA_WINDOW) & (kpos[:, None, :] >= 0) & (kpos[:, None, :] < L)
    s_loc = jnp.where(ok[None, :, None, None], s_loc, -jnp.inf)
    s_ctx = jnp.einsum('bnqhgd,bchd->bnhgqc', qb, kc).astype(F32) * scale
    s_snk = jnp.broadcast_to(sink.astype(F32).reshape(Hkv, G)[None, None, :, :, None, None], s_loc.shape[:-1] + (1,))
    p = jax.nn.softmax(jnp.concatenate([s_loc, s_ctx, s_snk], axis=-1), axis=-1).astype(v.dtype)
    nk = 3 * W
    nc = kc.shape[1]
    o = jnp.einsum('bnhgqk,bnkhd->bnqhgd', p[..., :nk], vb) + jnp.einsum('bnhgqc,bchd->bnqhgd', p[..., nk:nk + nc], vc)
    return o.reshape(B, L, Hq * d)


def ctx_attn(q, k, v, sink):
    B, Lc, Hq, d = q.shape
    Hkv = k.shape[2]
    G = Hq // Hkv
    qg = q.reshape(B, Lc, Hkv, G, d)
    s = jnp.einsum('bqhgd,bkhd->bhgqk', qg, k).astype(F32) * d ** -0.5
    if sink is not None:
        snk = jnp.broadcast_to(sink.astype(F32).reshape(Hkv, G)[None, :, :, None, None], s.shape[:-1] + (1,))
        s = jnp.concatenate([s, snk], axis=-1)
    p = jax.nn.softmax(s, axis=-1)[..., :Lc].astype(v.dtype)
    return jnp.einsum('bhgqk,bkhd->bqhgd', p, v).reshape(B, Lc, Hq * d)


def neighbourhood_attn(q, k, v, kc, vc, rpb):
    B, L, H, d = q.shape
    rows = L // GRID_W
    kh = min(NA_KH, rows)
    kw = min(NA_KW, GRID_W)
    scale = d ** -0.5
    qg = jnp.moveaxis(q.reshape(B, rows, GRID_W, H, d), 1, 0)
    kg = k.reshape(B, rows, GRID_W, H, d)
    vg = v.reshape(B, rows, GRID_W, H, d)
    col = jnp.arange(GRID_W)
    cs = jnp.clip(col - kw // 2, 0, GRID_W - kw)
    col_ok = (col[None, :] >= cs[:, None]) & (col[None, :] < cs[:, None] + kw)
    dc_idx = jnp.clip(col[None, :] - col[:, None] + NA_KW - 1, 0, 2 * NA_KW - 2)
    rpb32 = rpb.astype(F32)
    n = kh * GRID_W

    def row_block(args):
        r, qr = args
        rs = jnp.clip(r - kh // 2, 0, rows - kh)
        kb = lax.dynamic_slice_in_dim(kg, rs, kh, axis=1)
        vb = lax.dynamic_slice_in_dim(vg, rs, kh, axis=1).reshape(B, n, H, d)
        dr_idx = rs - r + jnp.arange(kh) + NA_KH - 1
        bias = rpb32[:, dr_idx[None, :, None], dc_idx[:, None, :]]
        s = jnp.einsum('bqhd,brkhd->bhqrk', qr, kb).astype(F32) * scale + bias[None]
        s = jnp.where(col_ok[:, None, :], s, -jnp.inf).reshape(B, H, GRID_W, n)
        sc = jnp.einsum('bqhd,bchd->bhqc', qr, kc).astype(F32) * scale
        p = jax.nn.softmax(jnp.concatenate([s, sc], axis=-1), axis=-1).astype(v.dtype)
        return jnp.einsum('bhqk,bkhd->bqhd', p[..., :n], vb) + jnp.einsum('bhqc,bchd->bqhd', p[..., n:], vc)

    out = lax.map(row_block, (jnp.arange(rows), qg))
    return jnp.moveaxis(out, 0, 1).reshape(B, L, H * d)


def expert_choice_ffn(h, w_router, w1, w3, w2):
    B, N, D = h.shape
    cap = EC_CAPACITY * N // N_EXPERTS
    aff = jax.nn.softmax((h @ w_router).astype(F32), axis=-1)
    gate, idx = lax.top_k(jnp.swapaxes(aff, 1, 2), cap)
    bidx = jnp.arange(B)[:, None, None]
    xs = h[bidx, idx]
    a = jnp.einsum('becd,edf->becf', xs, w1)
    u = jnp.einsum('becd,edf->becf', xs, w3)
    y = jnp.einsum('becf,efd->becd', jax.nn.silu(a) * u, w2) * gate[..., None].astype(h.dtype)
    return jnp.zeros_like(h).at[bidx, idx].add(y)


def even_mixer(h_lat, h_ctx, w_in, w_out, conv_w, conv_b, filt, hy_bias, sink, need_ctx):
    B, L, _ = h_lat.shape
    Lc = h_ctx.shape[1]
    p = h_lat @ w_in
    q = axial_rope(p[..., EV_Q0:EV_K0].reshape(B, L, SWA_HEADS, HEAD_DIM))
    k = axial_rope(p[..., EV_K0:EV_V0].reshape(B, L, SWA_KV_HEADS, HEAD_DIM))
    v = p[..., EV_V0:].reshape(B, L, SWA_KV_HEADS, HEAD_DIM)
    off = 0 if need_ctx else EV_K0
    pc = h_ctx @ w_in[:, off:]
    kc = pc[..., EV_K0 - off:EV_V0 - off].reshape(B, Lc, SWA_KV_HEADS, HEAD_DIM)
    vc = pc[..., EV_V0 - off:].reshape(B, Lc, SWA_KV_HEADS, HEAD_DIM)
    a = window_gqa_attn(q, k, v, kc, vc, sink)
    y_lat = jnp.concatenate([hyena_mix(p[..., :EV_Q0], conv_w, conv_b, filt, hy_bias), a], axis=-1) @ w_out
    if not need_ctx:
        return y_lat, None
    qc = pc[..., EV_Q0:EV_K0].reshape(B, Lc, SWA_HEADS, HEAD_DIM)
    ac = ctx_attn(qc, kc, vc, sink)
    y_ctx = jnp.concatenate([hyena_mix(pc[..., :EV_Q0], conv_w, conv_b, filt, hy_bias), ac], axis=-1) @ w_out
    return y_lat, y_ctx


def odd_mixer(h_lat, h_ctx, w_in, w_out, rpb, need_ctx):
    B, L, _ = h_lat.shape
    Lc = h_ctx.shape[1]
    hd = NA_HEADS * HEAD_DIM
    p = (h_lat @ w_in).reshape(B, L, 3, NA_HEADS, HEAD_DIM)
    off = 0 if need_ctx else hd
    pc = (h_ctx @ w_in[:, off:]).reshape(B, Lc, -1, NA_HEADS, HEAD_DIM)
    kc, vc = pc[:, :, -2], pc[:, :, -1]
    y_lat = neighbourhood_attn(p[:, :, 0], p[:, :, 1], p[:, :, 2], kc, vc, rpb) @ w_out
    if not need_ctx:
        return y_lat, None
    y_ctx = ctx_attn(pc[:, :, 0], kc, vc, None) @ w_out
    return y_lat, y_ctx


def setup_inputs(seed: int = 0) -> dict:
    key = jax.random.key(seed)
    keys = jax.random.split(key, 32)
    counter = iter(range(32))

    def nrm(shape, scale):
        return scale * jax.random.normal(keys[next(counter)], shape, dtype=F32)

    D, E, F = D_MODEL, N_EXPERTS, EXPERT_FF
    return {
        'x': nrm((BATCH, SEQ, D), 1.0),
        'c': nrm((BATCH, D), 1.0),
        'ctx': nrm((BATCH, CTX_LEN, D), 1.0),
        'c_ctx': nrm((D,), 1.0),
        'ada_w': nrm((DEPTH, D, 6 * D), 0.5 * D ** -0.5),
        'ada_b': nrm((DEPTH, 6 * D), 0.01),
        'ln_g': 1.0 + nrm((DEPTH, 2, D), 0.02),
        'ln_b': nrm((DEPTH, 2, D), 0.02),
        'ev_w_in': nrm((N_EVEN, D, EV_IN), D ** -0.5),
        'ev_w_out': nrm((N_EVEN, EV_CAT, D), DN_BETA * EV_CAT ** -0.5),
        'hy_conv_w': nrm((N_EVEN, HY_SHORT, (HY_ORDER + 1) * HY_DIM), HY_SHORT ** -0.5),
        'hy_conv_b': nrm((N_EVEN, (HY_ORDER + 1) * HY_DIM), 0.02),
        'hy_f_w1': nrm((N_EVEN, HY_EMB, HY_FILT_HID), HY_EMB ** -0.5),
        'hy_f_b1': nrm((N_EVEN, HY_FILT_HID), 0.1),
        'hy_f_w2': nrm((N_EVEN, HY_FILT_INNER, HY_FILT_HID, HY_FILT_HID), HY_FILT_HID ** -0.5),
        'hy_f_b2': nrm((N_EVEN, HY_FILT_INNER, HY_FILT_HID), 0.1),
        'hy_f_w3': nrm((N_EVEN, HY_FILT_HID, HY_ORDER * 2 * HY_DIM), HY_FILT_SCALE * HY_FILT_HID ** -0.5),
        'hy_f_freq': 1.0 + nrm((N_EVEN, HY_FILT_HID), 0.1),
        'hy_bias': nrm((N_EVEN, HY_ORDER, HY_DIM), 0.1),
        'swa_sink': nrm((N_EVEN, SWA_HEADS), 0.5),
        'od_w_in': nrm((N_ODD, D, 3 * NA_HEADS * HEAD_DIM), D ** -0.5),
        'od_w_out': nrm((N_ODD, NA_HEADS * HEAD_DIM, D), DN_BETA * (NA_HEADS * HEAD_DIM) ** -0.5),
        'na_rpb': nrm((N_ODD, NA_HEADS, 2 * NA_KH - 1, 2 * NA_KW - 1), 0.02),
        'moe_w_router': nrm((DEPTH, D, E), D ** -0.5),
        'moe_w1': nrm((DEPTH, E, D, F), D ** -0.5),
        'moe_w3': nrm((DEPTH, E, D, F), D ** -0.5),
        'moe_w2': nrm((DEPTH, E, F, D), DN_BETA * F ** -0.5),
    }


def reference(x, c, ctx, c_ctx, ada_w, ada_b, ln_g, ln_b, ev_w_in, ev_w_out, hy_conv_w, hy_conv_b,
              hy_f_w1, hy_f_b1, hy_f_w2, hy_f_b2, hy_f_w3, hy_f_freq, hy_bias, swa_sink,
              od_w_in, od_w_out, na_rpb, moe_w_router, moe_w1, moe_w3, moe_w2):
    B, L, D = x.shape
    s_lat = jax.nn.silu(c)
    s_ctx = jax.nn.silu(c_ctx)
    x_lat, x_ctx = x, ctx
    for l in range(DEPTH):
        need_ctx = l < DEPTH - 1
        m_lat = (s_lat @ ada_w[l] + ada_b[l]).reshape(B, 6, 1, D)
        m_ctx = (s_ctx @ ada_w[l] + ada_b[l]).reshape(6, 1, 1, D)
        h_lat = x_lat * (1.0 + m_lat[:, 1]) + m_lat[:, 0]
        h_ctx = x_ctx * (1.0 + m_ctx[1]) + m_ctx[0]
        i = l // 2
        if l % 2 == 0:
            filt = (hy_f_w1[i], hy_f_b1[i], hy_f_w2[i], hy_f_b2[i], hy_f_w3[i], hy_f_freq[i])
            y_lat, y_ctx = even_mixer(h_lat, h_ctx, ev_w_in[i], ev_w_out[i], hy_conv_w[i], hy_conv_b[i],
                                      filt, hy_bias[i], swa_sink[i], need_ctx)
        else:
            y_lat, y_ctx = odd_mixer(h_lat, h_ctx, od_w_in[i], od_w_out[i], na_rpb[i], need_ctx)
        x_lat = layer_norm(DN_ALPHA * x_lat + m_lat[:, 2] * y_lat, ln_g[l, 0], ln_b[l, 0])
        f_lat = expert_choice_ffn(x_lat * (1.0 + m_lat[:, 4]) + m_lat[:, 3], moe_w_router[l], moe_w1[l], moe_w3[l], moe_w2[l])
        x_lat = layer_norm(DN_ALPHA * x_lat + m_lat[:, 5] * f_lat, ln_g[l, 1], ln_b[l, 1])
        if need_ctx:
            x_ctx = layer_norm(DN_ALPHA * x_ctx + m_ctx[2] * y_ctx, ln_g[l, 0], ln_b[l, 0])
            f_ctx = expert_choice_ffn(x_ctx * (1.0 + m_ctx[4]) + m_ctx[3], moe_w_router[l], moe_w1[l], moe_w3[l], moe_w2[l])
            x_ctx = layer_norm(DN_ALPHA * x_ctx + m_ctx[5] * f_ctx, ln_g[l, 1], ln_b[l, 1])
    return x_lat
```

```python
import math
from contextlib import ExitStack
import numpy as np
import concourse.bass as bass
import concourse.mybir as mybir
from concourse.bass_utils import run_bass_kernel_spmd

F32 = mybir.dt.float32
F32R = mybir.dt.float32r
FAST_MM = True


def f32(ap):
    return ap.bitcast(F32) if ap.dtype == F32R else ap
I32 = mybir.dt.int32
AF = mybir.ActivationFunctionType
ALU = mybir.AluOpType
AX = mybir.AxisListType

D = 2048
L = 2048
LC = 256
T = L + LC
NT = T // 128
DEPTH = 4
GRID_W = 64
HY_DIM = 1024
NEXP = 16
EFF = 1024
DN_ALPHA = (2 * DEPTH) ** 0.25
LN_EPS = 1e-5
NEG = -30000.0


class Trk:
    __slots__ = ("name", "lw", "rd", "dsem")

    def __init__(self, name):
        self.name = name
        self.lw = None
        self.rd = []
        self.dsem = None


class Tile:
    def __init__(self, name, t, space):
        self.name = name
        self.t = t
        self.space = space
        self.trk = Trk(name)
        if space == "dram":
            self.trk.lw = {}
            self.trk.rd = {}

    def __getitem__(self, idx):
        return self.t[idx]


class KB:
    ENG = ("pe", "dve", "act", "pool", "sp")

    def __init__(self, nc, stack):
        self.nc = nc
        self.stack = stack
        self.prog = {e: [] for e in self.ENG}
        self.sems = {}
        self.cnt = {}
        self.seen = {e: {} for e in self.ENG}
        for e in ("pe", "dve", "act", "pool"):
            self._mksem("E_" + e)
        self.free_dsems = []
        self.pending = {}
        self.ndsem = 0
        self.ninst = 0
        self.rr = 0

    def _mksem(self, key):
        h = self.stack.enter_context(self.nc.semaphore(key))
        self.sems[key] = h
        self.cnt[key] = 0
        return key

    def _dsem_for(self, trk):
        if trk.dsem is None:
            if self.free_dsems:
                trk.dsem = self.free_dsems.pop()
            else:
                self.ndsem += 1
                trk.dsem = self._mksem("D%d" % self.ndsem)
        return trk.dsem

    def sbuf(self, st, name, shape, dtype=F32):
        self.uid = getattr(self, "uid", 0) + 1
        t = st.enter_context(self.nc.sbuf_tensor("s%d_%s" % (self.uid, name), list(shape), dtype))
        tl = Tile(name, t, "sbuf")
        tl.trk.rd = list(self.pending.items())
        st.callback(self._release, tl)
        return tl

    def psum(self, st, name, shape, dtype=F32):
        self.uid = getattr(self, "uid", 0) + 1
        t = st.enter_context(self.nc.psum_tensor("p%d_%s" % (self.uid, name), list(shape), dtype))
        tl = Tile(name, t, "psum")
        tl.trk.rd = list(self.pending.items())
        st.callback(self._release, tl)
        return tl

    def _release(self, tile):
        for d in [tile.trk.lw] + list(tile.trk.rd):
            if d is not None and d[1] > self.pending.get(d[0], 0):
                self.pending[d[0]] = d[1]
        if tile.trk.dsem is not None:
            self.free_dsems.append(tile.trk.dsem)
            tile.trk.dsem = None

    def dram(self, name, shape, dtype=F32, kind="Internal"):
        t = self.nc.dram_tensor(name, list(shape), dtype, kind=kind)
        return Tile(name, t.ap(), "dram")

    def _waits(self, e, reads, writes, skip_self=False):
        deps = {}

        def add(d):
            if d is None:
                return
            k, c = d
            if c > deps.get(k, 0):
                deps[k] = c
        for r in reads:
            if r.space == "dram":
                for d in r.trk.lw.items():
                    add(d)
            else:
                add(r.trk.lw)
        for w in writes:
            if w.space == "dram":
                for d in w.trk.rd.items():
                    add(d)
                if e not in ("sp", "pool"):
                    for d in w.trk.lw.items():
                        add(d)
            else:
                add(w.trk.lw)
                for d in w.trk.rd:
                    add(d)
        out = []
        seen = self.seen[e]
        for k, c in deps.items():
            if skip_self and k == "E_" + e:
                continue
            if k[0] == "D":
                c = self.cnt[k]
            if seen.get(k, 0) >= c:
                continue
            seen[k] = c
            out.append((k, c))
        return out

    def _commit(self, mark, reads, writes):
        for r in reads:
            if r.space == "dram":
                if mark[1] > r.trk.rd.get(mark[0], 0):
                    r.trk.rd[mark[0]] = mark[1]
                continue
            r.trk.rd.append(mark)
            if len(r.trk.rd) > 64:
                mx = {}
                for k, c in r.trk.rd:
                    if c > mx.get(k, 0):
                        mx[k] = c
                r.trk.rd = list(mx.items())
        for w in writes:
            if w.space == "dram":
                if mark[1] > w.trk.lw.get(mark[0], 0):
                    w.trk.lw[mark[0]] = mark[1]
                continue
            w.trk.lw = mark
            w.trk.rd = []

    def op(self, e, fn, reads=(), writes=()):
        waits = self._waits(e, reads, writes, skip_self=(e == "pe"))
        key = "E_" + e
        self.cnt[key] += 1
        c = self.cnt[key]
        sems = self.sems

        def emit(eng, waits=waits, fn=fn, key=key):
            for k, v in waits:
                eng.wait_ge(sems[k], v)
            fn(eng).then_inc(sems[key], 1)
        self.prog[e].append(emit)
        self._commit((key, c), reads, writes)
        self.ninst += 1

    def dma(self, q, out, in_, reads=(), writes=(), **kw):
        sb = [t for t in list(writes) + list(reads) if t.space != "dram"]
        assert len(sb) >= 1
        key = self._dsem_for(sb[0].trk)
        waits = self._waits(q, reads, writes)
        self.cnt[key] += 16
        c = self.cnt[key]
        sems = self.sems

        def emit(eng, waits=waits, key=key, out=out, in_=in_, kw=kw):
            for k, v in waits:
                eng.wait_ge(sems[k], v)
            eng.dma_start(out=out, in_=in_, **kw).then_inc(sems[key], 16)
        self.prog[q].append(emit)
        self._commit((key, c), reads, writes)
        self.ninst += 1

    def gather(self, dst, n, src, idx):
        key = self._dsem_for(dst.trk)
        waits = self._waits("pool", [idx, src], [dst])
        self.cnt[key] += 16
        c = self.cnt[key]
        sems = self.sems

        def emit(eng, waits=waits, key=key):
            for k, v in waits:
                eng.wait_ge(sems[k], v)
            eng.indirect_dma_start(out=dst[0:n, :], out_offset=None, in_=src[:, :],
                                   in_offset=bass.IndirectOffsetOnAxis(ap=idx[0:n, :], axis=0)).then_inc(sems[key], 16)
        self.prog["pool"].append(emit)
        self._commit((key, c), [idx, src], [dst])
        self.ninst += 1

    def q(self):
        self.rr += 1
        return ("sp", "pool")[self.rr % 2]

    def wait_all(self, e, tiles):
        waits = self._waits(e, tiles, ())
        sems = self.sems

        def emit(eng, waits=waits):
            for k, v in waits:
                eng.wait_ge(sems[k], v)
        self.prog[e].append(emit)

    def finalize(self):
        nc = self.nc
        prog = self.prog
        with nc.Block() as block:
            @block.sync
            def _(eng):
                for f in prog["sp"]:
                    f(eng)

            @block.tensor
            def _(eng):
                for f in prog["pe"]:
                    f(eng)

            @block.vector
            def _(eng):
                for f in prog["dve"]:
                    f(eng)

            @block.scalar
            def _(eng):
                for f in prog["act"]:
                    f(eng)

            @block.gpsimd
            def _(eng):
                for f in prog["pool"]:
                    f(eng)

    def mm(self, out, lhsT, rhs, start, stop, reads, writes):
        fast = (FAST_MM and lhsT.dtype == F32R and rhs.dtype == F32R and tuple(lhsT.shape) == (128, 128)
                and len(rhs.shape) == 2 and rhs.shape[1] % 2 == 0 and rhs.shape[1] >= 32)
        if not fast:
            lhsT = f32(lhsT)
            rhs = f32(rhs)
        self.op("pe", lambda e: e.matmul(out, lhsT=lhsT, rhs=rhs, start=start, stop=stop), reads=reads, writes=writes)

    def tr(self, out, in_, ident, reads, writes):
        in_ = f32(in_)
        self.op("pe", lambda e: e.transpose(out, in_, ident), reads=reads, writes=writes)

    def copy(self, e, out, in_, reads, writes):
        if e == "act":
            self.op("act", lambda g: g.activation(out=out, in_=in_, func=AF.Copy), reads=reads, writes=writes)
        else:
            self.op(e, lambda g: g.tensor_copy(out=out, in_=in_), reads=reads, writes=writes)

    def ts(self, e, out, in0, s1, s2, op0, op1, reads, writes):
        if s2 is None:
            self.op(e, lambda g: g.tensor_scalar(out=out, in0=in0, scalar1=s1, scalar2=None, op0=op0), reads=reads, writes=writes)
        else:
            self.op(e, lambda g: g.tensor_scalar(out=out, in0=in0, scalar1=s1, scalar2=s2, op0=op0, op1=op1), reads=reads, writes=writes)

    def tt(self, e, out, in0, in1, op, reads, writes):
        self.op(e, lambda g: g.tensor_tensor(out=out, in0=in0, in1=in1, op=op), reads=reads, writes=writes)

    def stt(self, e, out, in0, scalar, in1, op0, op1, reads, writes):
        e = "dve"
        self.op(e, lambda g: g.scalar_tensor_tensor(out=out, in0=in0, scalar=scalar, in1=in1, op0=op0, op1=op1),
                reads=reads, writes=writes)

    def actf(self, out, in_, func, reads, writes, scale=None, bias=None, accum_out=None):
        kw = {}
        if scale is not None:
            kw["scale"] = scale
        if bias is not None:
            kw["bias"] = bias
        if accum_out is not None:
            kw["accum_out"] = accum_out
        self.op("act", lambda g: g.activation(out=out, in_=in_, func=func, **kw), reads=reads, writes=writes)


class Prog:
    def __init__(self, dbg=(), nlayers=DEPTH, stop_after=None, l0=0):
        self.dbg = set(dbg)
        self.l0 = l0
        self.nlayers = nlayers
        self.stop_after = stop_after
        self.nc = bass.Bass("TRN2", target_bir_lowering=False)
        self.inputs = {}

    def inp(self, name, shape):
        t = self.nc.dram_tensor(name, list(shape), F32, kind="ExternalInput")
        tl = Tile(name, t.ap(), "dram")
        self.inputs[name] = tl
        return tl

    def scratch(self, name, shape):
        kind = "ExternalOutput" if name in self.dbg else "Internal"
        return self.kb.dram(name, shape, F32, kind=kind)

    def stage_mod(self, l):
        kb = self.kb
        Ml = self.M[l]
        with ExitStack() as st:
            wb = [kb.sbuf(st, "modw%d" % i, [128, 16, 512]) for i in range(2)]
            br = [kb.sbuf(st, "modb%d" % i, [1, 512]) for i in range(2)]
            ms = [kb.sbuf(st, "mods%d" % i, [2, 512]) for i in range(2)]
            ps = [kb.psum(st, "modp%d" % i, [2, 512]) for i in range(2)]
            aw = self.inputs["ada_w"]
            ab = self.inputs["ada_b"]
            for cb in range(24):
                w = wb[cb % 2]
                b = br[cb % 2]
                p = ps[cb % 2]
                m = ms[cb % 2]
                cs = slice(cb * 512, (cb + 1) * 512)
                kb.dma("pool", w[:], aw[l, :, cs].rearrange("(k p) n -> p k n", p=128), reads=[aw], writes=[w])
                kb.dma("pool", b[:], ab[l:l + 1, cs], reads=[ab], writes=[b])
                for k in range(16):
                    kb.mm(p[:], self.sT[:, k, :], w[:, k, :], k == 0, False, [self.sT, w], [p])
                kb.mm(p[:], self.ones[0:1, 0:2], b[:], False, True, [self.ones, b], [p])
                kb.copy("dve", m[:], p[:], [p], [m])
                kb.dma("sp", Ml[:, cs], m[:], reads=[m], writes=[Ml])
            mr = kb.sbuf(st, "modr", [96, 128])
            pc = kb.psum(st, "modpc", [128, 96])
            for r in range(2):
                kb.dma("sp", mr[:], Ml[r, :].rearrange("(c p) -> c p", p=128), reads=[Ml], writes=[mr])
                kb.tr(pc[:], mr[:], self.ident[0:96, 0:96], [mr, self.ident], [pc])
                kb.copy("dve", self.mcol[:, r, :], pc[:], [pc], [self.mcol])
                kb.ts("dve", self.mcol1[:, r, :], pc[:], 1.0, None, ALU.add, None, [pc], [self.mcol1])

    def stage_modT(self, X, HT, s_shift, s_scale):
        kb = self.kb
        with ExitStack() as st:
            xb = [kb.sbuf(st, "mtx%d" % i, [128, D]) for i in range(2)]
            hb = [kb.sbuf(st, "mth%d" % i, [128, 16, 128]) for i in range(2)]
            ps = [kb.psum(st, "mtp%d" % i, [128, 4, 128]) for i in range(2)]
            for i in range(NT):
                r = 0 if i < 16 else 1
                xt = xb[i % 2]
                ht = hb[i % 2]
                kb.dma("sp", xt[:], X[i * 128:(i + 1) * 128, :], reads=[X], writes=[xt])
                for g in range(4):
                    p = ps[g % 2]
                    for j in range(4):
                        c = g * 4 + j
                        kb.tr(p[:, j, :], xt[:, c * 128:(c + 1) * 128], self.ident[:], [xt, self.ident], [p])
                    for j in range(4):
                        c = g * 4 + j
                        sc = self.mcol1[:, r, s_scale * 16 + c:s_scale * 16 + c + 1]
                        sh = self.mcol[:, r, s_shift * 16 + c:s_shift * 16 + c + 1]
                        if j % 2 == 0:
                            kb.actf(ht[:, c, :], p[:, j, :], AF.Identity, [p, self.mcol, self.mcol1], [ht], scale=sc, bias=sh)
                        else:
                            kb.ts("dve", ht[:, c, :], p[:, j, :], sc, sh, ALU.mult, ALU.add, [p, self.mcol, self.mcol1], [ht])
                kb.dma("pool", HT[:, :, i * 128:(i + 1) * 128], ht[:], reads=[ht], writes=[HT])

    def stage_proj(self, HT, W, wsel, specs):
        kb = self.kb
        TB = T // 2
        mv = [(0, 512), (512, 512), (1024, 128)]
        with ExitStack() as st:
            hblk = kb.sbuf(st, "pjh", [128, 16, TB], F32R)
            wpan = [kb.sbuf(st, "pjw%d" % i, [128, 16, 128], F32R) for i in range(2)]
            wp2 = [kb.sbuf(st, "pjv%d" % i, [128, 16, 512], F32R) for i in range(2)]
            ot = [kb.sbuf(st, "pjo%d" % i, [128, TB]) for i in range(2)]
            ot2 = [kb.sbuf(st, "pjq%d" % i, [128, 512]) for i in range(2)]
            ps = [kb.psum(st, "pjp%d" % i, [128, 512]) for i in range(6)]
            n = 0
            n2 = 0
            for tb in range(2):
                t0 = tb * TB
                kb.dma("pool", hblk[:, 0:8, :], HT[:, 0:8, t0:t0 + TB], reads=[HT], writes=[hblk])
                kb.dma("pool", hblk[:, 8:16, :], HT[:, 8:16, t0:t0 + TB], reads=[HT], writes=[hblk])
                for kind, col0, ncols, OUT in specs:
                    if kind == "fm":
                        for cc in range(ncols // 128):
                            wp = wpan[n % 2]
                            o = ot[n % 2]
                            pp = ps[(n % 2) * 3:(n % 2) * 3 + 3]
                            c0 = col0 + cc * 128
                            kb.dma("pool", wp[:], W[wsel, :, c0:c0 + 128].rearrange("(k p) n -> p k n", p=128),
                                   reads=[W], writes=[wp])
                            for k in range(16):
                                for mi, (m0, mn) in enumerate(mv):
                                    kb.mm(pp[mi][:, :mn], wp[:, k, :], hblk[:, k, m0:m0 + mn], k == 0, k == 15,
                                          [wp, hblk], [pp[mi]])
                            for mi, (m0, mn) in enumerate(mv):
                                kb.copy("act" if mi != 1 else "dve", o[:, m0:m0 + mn], pp[mi][:, :mn], [pp[mi]], [o])
                            kb.dma("sp", OUT[:, cc, t0:t0 + TB], o[:], reads=[o], writes=[OUT])
                            n += 1
                    else:
                        PW = 512 if ncols % 512 == 0 else 256
                        for cb in range(ncols // PW):
                            wp = wp2[n2 % 2]
                            c0 = col0 + cb * PW
                            kb.dma("pool", wp[:, :, 0:PW], W[wsel, :, c0:c0 + PW].rearrange("(k p) n -> p k n", p=128),
                                   reads=[W], writes=[wp])
                            n2 += 1
                            for ti in range(TB // 128):
                                p = ps[n % 6]
                                o = ot2[n % 2]
                                for k in range(16):
                                    kb.mm(p[:, :PW], hblk[:, k, ti * 128:(ti + 1) * 128], wp[:, k, 0:PW], k == 0, k == 15,
                                          [hblk, wp], [p])
                                kb.copy("act" if n % 2 == 0 else "dve", o[:, 0:PW], p[:, :PW], [p], [o])
                                kb.dma("sp", OUT[t0 + ti * 128:t0 + (ti + 1) * 128, cb * PW:(cb + 1) * PW], o[:, 0:PW],
                                       reads=[o], writes=[OUT])
                                n += 1

    def stage_attn(self, QT, KT, V, CATT, cat_c0, nheads, gq, plans, BT, nuniq, bt_per_head, rope, sink_l):
        kb = self.kb
        scale = 128.0 ** -0.5
        with ExitStack() as st:
            qh = kb.sbuf(st, "aq", [128, T], F32R)
            kh = kb.sbuf(st, "ak", [128, T], F32R)
            va = kb.sbuf(st, "av", [128, NT, 132], F32R)
            oth = kb.sbuf(st, "ao", [128, T])
            bt = kb.sbuf(st, "abt", [128, nuniq, 128], F32R)
            identR = kb.sbuf(st, "aidr", [128, 128], F32R)
            kb.copy("dve", identR[:], self.ident[:], [self.ident], [identR])
            et = [kb.sbuf(st, "aet%d" % i, [128, 7, 128], F32R) for i in range(2)]
            osb = [kb.sbuf(st, "aos%d" % i, [128, 128]) for i in range(2)]
            rd = [kb.sbuf(st, "ard%d" % i, [128, 2]) for i in range(2)]
            psS = [[kb.psum(st, "aps%d%d" % (i, j), [128, 512]) for j in range(2)] for i in range(2)]
            psO = [kb.psum(st, "apo%d" % i, [128, 512]) for i in range(2)]
            psT = kb.psum(st, "apt", [128, 128])
            psR = kb.psum(st, "apr", [128, 512])
            if rope:
                rt1 = kb.sbuf(st, "art1", [128, 512])
                rt2 = kb.sbuf(st, "art2", [128, 512])
            if sink_l is not None:
                esk = kb.sbuf(st, "aesk", [128, 8])
                sk = self.inputs["swa_sink"]
                kb.dma("sp", esk[:], sk[sink_l:sink_l + 1, :].to_broadcast([128, 8]), reads=[sk], writes=[esk])
                kb.actf(esk[:], esk[:], AF.Exp, [esk], [esk])
            kb.copy("pool", va[:, :, 128:132], self.ones[:, 0:NT * 4].rearrange("p (a b) -> p a b", b=4), [self.ones], [va])
            def load_bt(hh):
                kb.dma("pool", bt[:], BT[hh], reads=[BT], writes=[bt])
                kb.ts("dve", bt[:].rearrange("p a b -> p (a b)"), f32(bt[:].rearrange("p a b -> p (a b)")), float(1.0 / scale), None,
                      ALU.mult, None, [bt], [bt])
            if not bt_per_head:
                load_bt(0)

            def do_rope(tl):
                for blk in range(4):
                    sl = slice(blk * 512, (blk + 1) * 512)
                    kb.mm(psR[:], self.ropeR[:], tl[:, sl], True, True, [self.ropeR, tl], [psR])
                    kb.tt("pool", rt1[:], f32(tl[:, sl]), self.ropeC[:, sl], ALU.mult, [tl, self.ropeC], [rt1])
                    kb.tt("dve", rt2[:], psR[:], self.ropeS[:, sl], ALU.mult, [psR, self.ropeS], [rt2])
                    kb.tt("dve", (tl[:, sl]), rt1[:], rt2[:], ALU.add, [rt1, rt2], [tl])

            cur_kv = -1
            it = 0
            for h in range(nheads):
                kv = h // gq
                kb.dma("pool", qh[:, 0:T // 2], QT[:, h, 0:T // 2], reads=[QT], writes=[qh])
                kb.dma("pool", qh[:, T // 2:T], QT[:, h, T // 2:T], reads=[QT], writes=[qh])
                if bt_per_head:
                    load_bt(h)
                if kv != cur_kv:
                    cur_kv = kv
                    kb.dma("pool", kh[:, 0:T // 2], KT[:, kv, 0:T // 2], reads=[KT], writes=[kh])
                    kb.dma("pool", kh[:, T // 2:T], KT[:, kv, T // 2:T], reads=[KT], writes=[kh])
                    kb.dma("pool", va[:, :, 0:128], V[:, kv * 128:(kv + 1) * 128].rearrange("(i p) d -> p i d", p=128),
                           reads=[V], writes=[va])
                    if rope:
                        do_rope(kh)
                if rope:
                    do_rope(qh)
                def emit_S(n, b):
                    blocks = plans[n]
                    pS = psS[b]
                    for jj, (j, bk) in enumerate(blocks):
                        p = pS[jj // 4]
                        cols = slice((jj % 4) * 128, (jj % 4 + 1) * 128)
                        kb.mm(p[:, cols], kh[:, j * 128:(j + 1) * 128], qh[:, n * 128:(n + 1) * 128],
                              True, bk is None, [kh, qh], [p])
                        if bk is not None:
                            kb.mm(p[:, cols], identR[:], bt[:, bk, :], False, True, [identR, bt], [p])

                def emit_exp(n, b):
                    nb = len(plans[n])
                    e_t = et[b]
                    pS = psS[b]
                    for bank in range((nb + 3) // 4):
                        w = min(4, nb - bank * 4)
                        kb.actf(e_t[:, bank * 4:bank * 4 + w, :], pS[bank][:, 0:w * 128].rearrange("p (a b) -> p a b", b=128),
                                AF.Exp, [pS[bank]], [e_t], scale=scale)

                def emit_PV(n, b):
                    blocks = plans[n]
                    nb = len(blocks)
                    for jj, (j, bk) in enumerate(blocks):
                        kb.mm(psO[b][:, 0:130], et[b][:, jj, :], va[:, j, 0:130], jj == 0, jj == nb - 1, [et[b], va], [psO[b]])

                def emit_norm(n, b):
                    r = rd[b]
                    o = osb[b]
                    pO = psO[b]
                    if sink_l is not None:
                        kb.tt("dve", r[:, 0:1], pO[:, 128:129], esk[:, h:h + 1], ALU.add, [pO, esk], [r])
                        kb.op("dve", lambda e, r=r: e.reciprocal(out=r[:, 1:2], in_=r[:, 0:1]), reads=[r], writes=[r])
                    else:
                        kb.op("dve", lambda e, r=r, pO=pO: e.reciprocal(out=r[:, 1:2], in_=pO[:, 128:129]), reads=[pO], writes=[r])
                    kb.ts("dve", o[:], pO[:, 0:128], r[:, 1:2], None, ALU.mult, None, [pO, r], [o])

                def emit_tr(n, b):
                    kb.tr(psT[:], osb[b][:], self.ident[:], [osb[b], self.ident], [psT])
                    kb.copy("act", oth[:, n * 128:(n + 1) * 128], psT[:], [psT], [oth])

                b0 = it % 2
                emit_S(0, b0)
                for n in range(NT):
                    b = (b0 + n) % 2
                    emit_exp(n, b)
                    if n + 1 < NT:
                        emit_S(n + 1, 1 - b)
                    emit_PV(n, b)
                    emit_norm(n, b)
                    if n >= 1:
                        emit_tr(n - 1, 1 - b)
                emit_tr(NT - 1, (b0 + NT - 1) % 2)
                it += NT
                kb.dma("sp", CATT[:, cat_c0 + h, :], oth[:], reads=[oth], writes=[CATT])

    def stage_ln(self, l, which, s_gate, X, Y, Xout, H2=None, AFF=None):
        kb = self.kb
        Ml = self.M[l]
        lg = self.inputs["ln_g"]
        lb = self.inputs["ln_b"]
        with ExitStack() as st:
            mg = [kb.sbuf(st, "lmg%d" % r, [128, D]) for r in range(2)]
            gb = kb.sbuf(st, "lgb", [128, D])
            bb = kb.sbuf(st, "lbb", [128, D])
            for r in range(2):
                kb.dma("sp", mg[r][:], Ml[r:r + 1, s_gate * D:(s_gate + 1) * D].to_broadcast([128, D]), reads=[Ml], writes=[mg[r]])
            kb.dma("sp", gb[:], lg[l, which:which + 1, :].to_broadcast([128, D]), reads=[lg], writes=[gb])
            kb.dma("sp", bb[:], lb[l, which:which + 1, :].to_broadcast([128, D]), reads=[lb], writes=[bb])
            if H2 is not None:
                m3 = [kb.sbuf(st, "lm3%d" % r, [128, D]) for r in range(2)]
                m4 = [kb.sbuf(st, "lm4%d" % r, [128, D]) for r in range(2)]
                for r in range(2):
                    kb.dma("sp", m3[r][:], Ml[r:r + 1, 3 * D:4 * D].to_broadcast([128, D]), reads=[Ml], writes=[m3[r]])
                    kb.dma("sp", m4[r][:], Ml[r:r + 1, 4 * D:5 * D].to_broadcast([128, D]), reads=[Ml], writes=[m4[r]])
                    kb.ts("pool", m4[r][:], m4[r][:], 1.0, None, ALU.add, None, [m4[r]], [m4[r]])
                wr = kb.sbuf(st, "lwr", [128, 16, NEXP])
                wrin = self.inputs["moe_w_router"]
                kb.dma("sp", wr[:], wrin[l].rearrange("(k p) e -> p k e", p=128), reads=[wrin], writes=[wr])
                h2b = [kb.sbuf(st, "lh2%d" % i, [128, D]) for i in range(2)]
                h2T = kb.sbuf(st, "lh2T", [128, 16, 128])
                psT = [kb.psum(st, "lpt%d" % i, [128, 4, 128]) for i in range(2)]
                psL = kb.psum(st, "lpl", [128, NEXP])
                lgt = kb.sbuf(st, "llg", [128, NEXP])
                sm = kb.sbuf(st, "lsm", [128, 4])
            xb = [kb.sbuf(st, "lx%d" % i, [128, D]) for i in range(2)]
            yb = [kb.sbuf(st, "ly%d" % i, [128, D]) for i in range(2)]
            stt_ = kb.sbuf(st, "lst", [128, 4, 6])
            mv = kb.sbuf(st, "lmv", [128, 4])
            def router(i):
                h2 = h2b[i % 2]
                for g in range(4):
                    p = psT[g % 2]
                    for j in range(4):
                        c = g * 4 + j
                        kb.tr(p[:, j, :], h2[:, c * 128:(c + 1) * 128], self.ident[:], [h2, self.ident], [p])
                    kb.copy("act", h2T[:, g * 4:(g + 1) * 4, :], p[:], [p], [h2T])
                for c in range(16):
                    kb.mm(psL[:], h2T[:, c, :], wr[:, c, :], c == 0, c == 15, [h2T, wr], [psL])
                kb.op("dve", lambda e: e.reduce_max(out=sm[:, 0:1], in_=psL[:], axis=AX.X), reads=[psL], writes=[sm])
                kb.ts("dve", sm[:, 1:2], sm[:, 0:1], -1.0, None, ALU.mult, None, [sm], [sm])
                kb.actf(lgt[:], psL[:], AF.Exp, [psL, sm], [lgt, sm], bias=sm[:, 1:2], accum_out=sm[:, 2:3])
                kb.op("dve", lambda e: e.reciprocal(out=sm[:, 3:4], in_=sm[:, 2:3]), reads=[sm], writes=[sm])
                kb.ts("dve", AFF[:, i, :], lgt[:], sm[:, 3:4], None, ALU.mult, None, [lgt, sm], [AFF])

            for i in range(NT):
                r = 0 if i < 16 else 1
                x = xb[i % 2]
                y = yb[i % 2]
                rows = slice(i * 128, (i + 1) * 128)
                if i == 0:
                    kb.dma("sp", x[:], X[rows, :], reads=[X], writes=[x])
                    kb.dma("sp", y[:], Y[rows, :], reads=[Y], writes=[y])
                if i + 1 < NT:
                    rn = slice((i + 1) * 128, (i + 2) * 128)
                    kb.dma("sp", xb[(i + 1) % 2][:], X[rn, :], reads=[X], writes=[xb[(i + 1) % 2]])
                    kb.dma("sp", yb[(i + 1) % 2][:], Y[rn, :], reads=[Y], writes=[yb[(i + 1) % 2]])
                kb.tt("dve", y[:], y[:], mg[r][:], ALU.mult, [y, mg[r]], [y])
                kb.stt("pool", y[:], x[:], float(DN_ALPHA), y[:], ALU.mult, ALU.add, [x, y], [y])
                for c in range(4):
                    kb.op("dve", lambda e, c=c, y=y: e.bn_stats(out=stt_[:, c, :], in_=y[:, c * 512:(c + 1) * 512]), reads=[y], writes=[stt_])
                kb.op("dve", lambda e: e.bn_aggr(out=mv[:, 0:2], in_=stt_[:].rearrange("p a b -> p (a b)")), reads=[stt_], writes=[mv])
                kb.ts("dve", mv[:, 2:3], mv[:, 1:2], float(LN_EPS), None, ALU.add, None, [mv], [mv])
                kb.actf(mv[:, 2:3], mv[:, 2:3], AF.Sqrt, [mv], [mv])
                kb.op("dve", lambda e: e.reciprocal(out=mv[:, 3:4], in_=mv[:, 2:3]), reads=[mv], writes=[mv])
                kb.ts("dve", y[:], y[:], mv[:, 0:1], mv[:, 3:4], ALU.subtract, ALU.mult, [y, mv], [y])
                kb.tt("pool", y[:], y[:], gb[:], ALU.mult, [y, gb], [y])
                kb.tt("dve", x[:], y[:], bb[:], ALU.add, [y, bb], [x])
                kb.dma("sp", Xout[rows, :], x[:], reads=[x], writes=[Xout])
                if H2 is not None:
                    h2 = h2b[i % 2]
                    kb.tt("pool", h2[:], x[:], m4[r][:], ALU.mult, [x, m4[r]], [h2])
                    kb.tt("pool", h2[:], h2[:], m3[r][:], ALU.add, [h2, m3[r]], [h2])
                    kb.dma("sp", H2[rows, :], h2[:], reads=[h2], writes=[H2])
                    if i >= 1:
                        router(i - 1)
            if H2 is not None:
                router(NT - 1)

    def stage_moe(self, l, H2, AFF, Fo):
        kb = self.kb
        W1 = self.inputs["moe_w1"]
        W3 = self.inputs["moe_w3"]
        W2 = self.inputs["moe_w2"]
        SEGS = [(0, 16, 256, 0), (16, 2, 32, 256)]
        CAPT = 288
        YS = self.YSm
        with ExitStack() as st:
            masks = [kb.sbuf(st, "gmask%d" % g, [128, SEGS[g][1], 16]) for g in range(2)]
            poss = [kb.sbuf(st, "gpos%d" % g, [128, SEGS[g][1], 16]) for g in range(2)]
            gate = kb.sbuf(st, "ggate", [128, 8])
            psAs = [kb.psum(st, "gpa%d" % i, [128, 512]) for i in range(2)]
            psUs = [kb.psum(st, "gpu%d" % i, [128, 512]) for i in range(2)]
            psYs = [kb.psum(st, "gpy%d" % i, [128, 512]) for i in range(2)]
            psM = kb.psum(st, "gpm", [128, 512])
            psG1 = kb.psum(st, "gpg", [128, 512])
            psG = psAs + psUs
            for g, (tile0, ntile, cap, soff) in enumerate(SEGS):
                N = ntile * 128
                mask, pos = masks[g], poss[g]
                st2 = ExitStack()
                afT = kb.sbuf(st2, "gafT", [16, N])
                bc = kb.sbuf(st2, "gbc", [128, N])
                junk = kb.sbuf(st2, "gjunk", [128, N])
                rank = kb.sbuf(st2, "grank", [128, ntile, 16])
                for i in range(ntile):
                    kb.tr(psM[0:16, 0:128], AFF[:, tile0 + i, :], self.ident[:], [AFF, self.ident], [psM])
                    kb.copy("dve", afT[:, i * 128:(i + 1) * 128], psM[0:16, 0:128], [psM], [afT])
                npb = 0
                for e_ in range(NEXP):
                    for blk in range(0, N, 512):
                        n_ = min(512, N - blk)
                        pb = psG[npb % 4]
                        npb += 1
                        kb.mm(pb[:, 0:n_], self.sel16[:, e_, :], afT[:, blk:blk + n_], True, True, [self.sel16, afT], [pb])
                        kb.copy("act", bc[:, blk:blk + n_], pb[:, 0:n_], [pb], [bc])
                    for i in range(ntile):
                        kb.op("dve", lambda g_, i=i, e_=e_, tile0=tile0, junk=junk, bc=bc, rank=rank: g_.tensor_scalar(
                            out=junk[:], in0=bc[:], scalar1=AFF[:, tile0 + i, e_:e_ + 1], scalar2=None, op0=ALU.is_gt, op1=ALU.add,
                            accum_out=rank[:, i, e_:e_ + 1]), reads=[bc, AFF], writes=[junk, rank])
                kb.ts("dve", mask[:].rearrange("p a b -> p (a b)"), rank[:].rearrange("p a b -> p (a b)"), float(cap), None, ALU.is_lt, None,
                      [rank], [mask])
                for i in range(ntile):
                    for i2 in range(i):
                        kb.mm(psM[:, 0:16], self.ones[:], mask[:, i2, :], i2 == 0, False, [self.ones, mask], [psM])
                    kb.mm(psM[:, 0:16], self.ustrict[:], mask[:, i, :], i == 0, True, [self.ustrict, mask], [psM])
                    kb.copy("dve", pos[:, i, :], psM[:, 0:16], [psM], [pos])
                st2.close()
            sels = [kb.sbuf(st, "gsel%d" % g, [128, SEGS[g][1], SEGS[g][2]], F32R) for g in range(2)]
            xsT = kb.sbuf(st, "gxsT", [128, 16, CAPT], F32R)
            gT = kb.sbuf(st, "ggT", [128, 8, CAPT], F32R)
            stg = [kb.sbuf(st, "gstg%d" % i, [128, 512]) for i in range(2)]
            xtok = [kb.sbuf(st, "gxk%d" % i, [128, D]) for i in range(3)]
            idxs = [kb.sbuf(st, "gix%d" % i, [128, 1], I32) for i in range(6)]
            ngp = 0
            wbuf = [kb.sbuf(st, "gw%d" % i, [128, 8192], F32R) for i in range(3)]
            sas = [kb.sbuf(st, "gsa%d" % i, [128, CAPT]) for i in range(2)]
            chunks = [(0, 0, 128, 0, 0), (0, 128, 128, 128, 1), (1, 0, 32, 256, 2)]
            cnt = {"nst": 0, "nw": 0, "ngp": 0, "ny": 0}

            def prepA(e_):
                for g, (tile0, ntile, cap, soff) in enumerate(SEGS):
                    for i in range(ntile):
                        kb.ts("dve", sels[g][:, i, :], self.iota[:, 0:cap], poss[g][:, i, e_:e_ + 1], masks[g][:, i, e_:e_ + 1],
                              ALU.is_equal, ALU.mult, [self.iota, poss[g], masks[g]], [sels[g]])
                for ci, (g, s0, ns, gs0, gc) in enumerate(chunks):
                    tile0, ntile, cap, soff = SEGS[g]
                    for i in range(ntile):
                        kb.mm(psM[0:ns, 0:1], sels[g][:, i, s0:s0 + ns], self.tokidx[:, tile0 + i:tile0 + i + 1], i == 0, i == ntile - 1,
                              [sels[g], self.tokidx], [psM])
                    ix = idxs[(e_ * 3 + ci) % 6]
                    kb.copy("dve", ix[0:ns, :], psM[0:ns, 0:1], [psM], [ix])
                    kb.gather(xtok[ci], ns, H2, ix)
                for (g, s0, ns, gs0, gc) in chunks:
                    tile0, ntile, cap, soff = SEGS[g]
                    nsc = (cap + 127) // 128
                    sc = s0 // 128
                    for ig in range(0, ntile, 4):
                        ng = min(4, ntile - ig)
                        for i in range(ig, ig + ng):
                            kb.tr(psM[0:ns, (i - ig) * 128:(i - ig + 1) * 128], sels[g][:, i, s0:s0 + ns], self.ident[:],
                                  [sels[g], self.ident], [psM])
                        sg = stg[cnt["nst"] % 2]
                        cnt["nst"] += 1
                        kb.copy("act", sg[0:ns, 0:ng * 128], psM[0:ns, 0:ng * 128], [psM], [sg])
                        kb.dma("sp", self.SELT[g][ig:ig + ng, 0:ns, e_ * nsc + sc, :].rearrange("i p t -> p i t"),
                               sg[0:ns, 0:ng * 128].rearrange("p (i t) -> p i t", t=128), reads=[sg], writes=[self.SELT[g]])
                for (g, s0, ns, gs0, gc) in chunks:
                    tile0, ntile, cap, soff = SEGS[g]
                    for i in range(ntile):
                        kb.mm(psM[0:ns, 0:1], sels[g][:, i, s0:s0 + ns], AFF[:, tile0 + i, e_:e_ + 1], i == 0, i == ntile - 1,
                              [sels[g], AFF], [psM])
                    gcol = (e_ % 2) * 4 + gc
                    kb.copy("dve", gate[0:ns, gcol:gcol + 1], psM[0:ns, 0:1], [psM], [gate])

            def prepB(e_):
                for ci, (g, s0, ns, gs0, gc) in enumerate(chunks):
                    xk = xtok[ci]
                    for dg in range(4):
                        p = psG1
                        for j in range(4):
                            dc = dg * 4 + j
                            kb.tr(p[:, j * 128:j * 128 + ns], xk[0:ns, dc * 128:(dc + 1) * 128], self.ident[0:ns, 0:ns], [xk, self.ident], [p])
                        kb.copy("act" if dg % 2 == 0 else "dve", xsT[:, dg * 4:(dg + 1) * 4, gs0:gs0 + ns],
                                p[:].rearrange("p (a b) -> p a b", b=128)[:, :, 0:ns], [p], [xsT])

            def ffn_up(e_):
                for fh in range(2):
                    w1 = wbuf[cnt["nw"] % 3]
                    cnt["nw"] += 1
                    w3 = wbuf[cnt["nw"] % 3]
                    cnt["nw"] += 1
                    kb.dma("pool", w1[:].rearrange("p (k n) -> p k n", n=512),
                           W1[l, e_, :, fh * 512:(fh + 1) * 512].rearrange("(k p) n -> p k n", p=128), reads=[W1], writes=[w1])
                    kb.dma("pool", w3[:].rearrange("p (k n) -> p k n", n=512),
                           W3[l, e_, :, fh * 512:(fh + 1) * 512].rearrange("(k p) n -> p k n", p=128), reads=[W3], writes=[w3])
                    for fc in range(4):
                        psA = psAs[fc % 2]
                        psU = psUs[fc % 2]
                        sa_ = sas[fc % 2]
                        for dc in range(16):
                            kb.mm(psA[:, 0:CAPT], w1[:, dc * 512 + fc * 128:dc * 512 + (fc + 1) * 128], xsT[:, dc, :], dc == 0, dc == 15,
                                  [w1, xsT], [psA])
                        for dc in range(16):
                            kb.mm(psU[:, 0:CAPT], w3[:, dc * 512 + fc * 128:dc * 512 + (fc + 1) * 128], xsT[:, dc, :], dc == 0, dc == 15,
                                  [w3, xsT], [psU])
                        kb.actf(sa_[:], psA[:, 0:CAPT], AF.Silu, [psA], [sa_])
                        kb.tt("dve", gT[:, fh * 4 + fc, :], sa_[:], psU[:, 0:CAPT], ALU.mult, [sa_, psU], [gT])

            def ffn_down(e_):
                for dh in range(2):
                    w2 = wbuf[cnt["nw"] % 3]
                    cnt["nw"] += 1
                    kb.dma("pool", w2[:].rearrange("p (k n) -> p k n", n=1024),
                           W2[l, e_, :, dh * 1024:(dh + 1) * 1024].rearrange("(k p) n -> p k n", p=128), reads=[W2], writes=[w2])
                    for (g, s0, ns, gs0, gc) in chunks:
                        gcol = (e_ % 2) * 4 + gc
                        for db in range(2):
                            psY = psYs[cnt["ny"] % 2]
                            cnt["ny"] += 1
                            for fc in range(8):
                                kb.mm(psY[0:ns, :], gT[:, fc, gs0:gs0 + ns], w2[:, fc * 1024 + db * 512:fc * 1024 + (db + 1) * 512],
                                      fc == 0, fc == 7, [gT, w2], [psY])
                            sg = stg[cnt["nst"] % 2]
                            cnt["nst"] += 1
                            kb.actf(sg[0:ns, :], psY[0:ns, :], AF.Identity, [psY, gate], [sg], scale=gate[0:ns, gcol:gcol + 1])
                            c0 = dh * 1024 + db * 512
                            kb.dma("sp", YS[e_, gs0:gs0 + ns, c0:c0 + 512], sg[0:ns, :], reads=[sg], writes=[YS])

            prepA(0)
            prepB(0)
            for e_ in range(NEXP):
                if e_ + 1 < NEXP:
                    prepA(e_ + 1)
                ffn_up(e_)
                if e_ + 1 < NEXP:
                    prepB(e_ + 1)
                ffn_down(e_)
        for g, (tile0, ntile, cap, soff) in enumerate(SEGS):
            nsc = (cap + 127) // 128
            sl_ = min(cap, 128)
            with ExitStack() as st:
                ysall = kb.sbuf(st, "cys", [128, NEXP * nsc, 512], F32R)
                selt = [kb.sbuf(st, "cse%d" % i, [128, NEXP * nsc, 128], F32R) for i in range(2)]
                ob = [kb.sbuf(st, "cob%d" % i, [128, 512]) for i in range(2)]
                ps = [kb.psum(st, "cps%d" % i, [128, 512]) for i in range(2)]
                n = 0
                for db in range(4):
                    for e_ in range(NEXP):
                        kb.dma("pool", ysall[0:sl_, e_ * nsc:(e_ + 1) * nsc, :],
                               YS[e_, soff:soff + cap, db * 512:(db + 1) * 512].rearrange("(s p) d -> p s d", p=sl_), reads=[YS], writes=[ysall])
                    for i in range(ntile):
                        se = selt[n % 2]
                        p = ps[n % 2]
                        o = ob[n % 2]
                        n += 1
                        kb.dma("pool", se[0:sl_, :, :], self.SELT[g][i, 0:sl_, :, :], reads=[self.SELT[g]], writes=[se])
                        for q_ in range(NEXP * nsc):
                            kb.mm(p[:], se[0:sl_, q_, :], ysall[0:sl_, q_, :], q_ == 0, q_ == NEXP * nsc - 1, [se, ysall], [p])
                        kb.copy("act", o[:], p[:], [p], [o])
                        kb.dma("sp", Fo[(tile0 + i) * 128:(tile0 + i + 1) * 128, db * 512:(db + 1) * 512], o[:], reads=[o], writes=[Fo])

    def hy_filters(self, i, Ls, seg):
        kb = self.kb
        nt = Ls // 128
        zT = self.inputs["zembT%d" % seg]
        dec = self.inputs["decay%d" % seg]
        fw1 = self.inputs["hy_f_w1"]
        fw2 = self.inputs["hy_f_w2"]
        fw3 = self.inputs["hy_f_w3"]
        fcol = self.inputs["hy_fcol"]
        KP, KM = self.KP[seg], self.KM[seg]
        with ExitStack() as st:
            z = kb.sbuf(st, "fz", [33, Ls])
            w1 = kb.sbuf(st, "fw1", [33, 64])
            w2 = kb.sbuf(st, "fw2", [64, 2, 64])
            w3 = kb.sbuf(st, "fw3", [64, 4096])
            fc = kb.sbuf(st, "fc", [64, 8])
            ha = kb.sbuf(st, "fha", [64, Ls])
            hb_ = kb.sbuf(st, "fhb", [64, Ls])
            wr_ = kb.sbuf(st, "fwr", [64, 512])
            wa_ = kb.sbuf(st, "fwa", [64, 512])
            wb_ = kb.sbuf(st, "fwb", [64, 512])
            dt = [kb.sbuf(st, "fdt%d" % j, [128, 1024]) for j in range(2)]
            ft = [kb.sbuf(st, "fft%d" % j, [128, 4096]) for j in range(2)]
            kp = [kb.sbuf(st, "fkp%d" % j, [128, 2048]) for j in range(2)]
            km = [kb.sbuf(st, "fkm%d" % j, [128, 2048]) for j in range(2)]
            ps = [kb.psum(st, "fps%d" % j, [128, 512]) for j in range(4)]
            kb.dma("sp", z[:], zT[:], reads=[zT], writes=[z])
            kb.dma("sp", w1[:], fw1[i], reads=[fw1], writes=[w1])
            kb.dma("sp", w2[:], fw2[i].rearrange("a k n -> k a n"), reads=[fw2], writes=[w2])
            kb.dma("pool", w3[:], fw3[i], reads=[fw3], writes=[w3])
            kb.dma("sp", fc[:, 0:4], fcol[i], reads=[fcol], writes=[fc])
            for j in range(3):
                kb.tt("dve", fc[:, 4 + j:5 + j], fc[:, 1 + j:2 + j], fc[:, 0:1], ALU.mult, [fc], [fc])
            src = z
            srcK = 33
            cur = ha
            for layer in range(3):
                wl = w1[:, :] if layer == 0 else w2[:, layer - 1, :]
                for blk in range(0, Ls, 512):
                    n_ = min(512, Ls - blk)
                    p = ps[(blk // 512) % 4]
                    kb.mm(p[0:64, 0:n_], wl, src[0:srcK, blk:blk + n_], True, True, [w1, w2, src], [p])
                    kb.ts("dve", wr_[:, 0:n_], p[0:64, 0:n_], fc[:, 0:1], fc[:, 4 + layer:5 + layer], ALU.mult, ALU.add, [p, fc], [wr_])
                    kb.ts("dve", wa_[:, 0:n_], wr_[:, 0:n_], -math.pi, 2 * math.pi, ALU.is_lt, ALU.mult, [wr_], [wa_])
                    kb.ts("pool", wb_[:, 0:n_], wr_[:, 0:n_], math.pi, 2 * math.pi, ALU.is_gt, ALU.mult, [wr_], [wb_])
                    kb.tt("dve", wr_[:, 0:n_], wr_[:, 0:n_], wa_[:, 0:n_], ALU.add, [wr_, wa_], [wr_])
                    kb.tt("dve", wr_[:, 0:n_], wr_[:, 0:n_], wb_[:, 0:n_], ALU.subtract, [wr_, wb_], [wr_])
                    kb.actf(cur[:, blk:blk + n_], wr_[:, 0:n_], AF.Sin, [wr_], [cur])
                src = cur
                srcK = 64
                cur = hb_ if cur is ha else ha
            hfin = src
            for tc in range(nt):
                d_ = dt[tc % 2]
                f_ = ft[tc % 2]
                kb.dma("sp", d_[:], dec[tc * 128:(tc + 1) * 128, :], reads=[dec], writes=[d_])
                for cb in range(8):
                    p = ps[cb % 4]
                    kb.mm(p[:], hfin[:, tc * 128:(tc + 1) * 128], w3[:, cb * 512:(cb + 1) * 512], True, True, [hfin, w3], [p])
                    kb.tt("dve", f_[:, cb * 512:(cb + 1) * 512], p[:], d_[:, (cb % 2) * 512:(cb % 2 + 1) * 512], ALU.mult, [p, d_], [f_])
                a = kp[tc % 2]
                m = km[tc % 2]
                for o in range(2):
                    kb.tt("pool", a[:, o * 1024:(o + 1) * 1024], f_[:, o * 2048:o * 2048 + 1024], f_[:, o * 2048 + 1024:(o + 1) * 2048],
                          ALU.add, [f_], [a])
                    kb.tt("pool", m[:, o * 1024:(o + 1) * 1024], f_[:, o * 2048:o * 2048 + 1024], f_[:, o * 2048 + 1024:(o + 1) * 2048],
                          ALU.subtract, [f_], [m])
                    kb.dma("sp", KP[o, tc * 128:(tc + 1) * 128, :], a[:, o * 1024:(o + 1) * 1024], reads=[a], writes=[KP])
                    kb.dma("sp", KM[o, tc * 128:(tc + 1) * 128, :], m[:, o * 1024:(o + 1) * 1024], reads=[m], writes=[KM])
        for o in range(2):
            self.hy_fwd(seg, Ls, [(self.KP[seg], o)], None, self.SA[seg], self.SB[seg], o, 1.0 / Ls, parts="C")
            self.hy_fwd(seg, Ls, [(self.KM[seg], o)], None, self.SA[seg], self.SB[seg], o, 1.0 / Ls, parts="S")

    def hy_fwd(self, seg, Ls, srcs, spec, OA, OB, o, scale, parts="CS"):
        kb = self.kb
        nt = Ls // 128
        Cm = self.inputs["dftC%d" % seg]
        Sm = self.inputs["dftS%d" % seg]
        with ExitStack() as st:
            zr = kb.sbuf(st, "dz0", [128, nt, 1024], F32R)
            zi = zr
            kb.dma("pool", zr[:, 0:nt // 2, :], srcs[0][0][srcs[0][1], 0:Ls // 2, :].rearrange("(k p) c -> p k c", p=128), reads=[srcs[0][0]], writes=[zr])
            kb.dma("pool", zr[:, nt // 2:nt, :], srcs[0][0][srcs[0][1], Ls // 2:Ls, :].rearrange("(k p) c -> p k c", p=128), reads=[srcs[0][0]], writes=[zr])
            cp = [kb.sbuf(st, "dcp%d" % j, [128, nt, 128], F32R) for j in range(2)]
            sp = [kb.sbuf(st, "dsp%d" % j, [128, nt, 128], F32R) for j in range(2)]
            ps = [kb.psum(st, "dps%d" % j, [128, 512]) for j in range(8)]
            ra = [kb.sbuf(st, "dra%d" % j, [128, 1024]) for j in range(2)]
            rb = [kb.sbuf(st, "drb%d" % j, [128, 1024]) for j in range(2)]
            if spec is not None:
                ka = [kb.sbuf(st, "dka%d" % j, [128, 1024]) for j in range(2)]
                kbb = [kb.sbuf(st, "dkb%d" % j, [128, 1024]) for j in range(2)]
                t1 = kb.sbuf(st, "dt1", [128, 1024])
                t2 = kb.sbuf(st, "dt2", [128, 1024])
                zrs = kb.sbuf(st, "dzr", [128, 1024])
                zis = kb.sbuf(st, "dzi", [128, 1024])
            for fc in range(nt):
                c_ = cp[fc % 2]
                s_ = sp[fc % 2]
                pp = ps[(fc % 2) * 4:(fc % 2) * 4 + 4]
                if "C" in parts:
                    kb.dma("pool", c_[:], Cm[:, fc * 128:(fc + 1) * 128].rearrange("(k p) f -> p k f", p=128), reads=[Cm], writes=[c_])
                    for k in range(nt):
                        for hcol in range(2):
                            kb.mm(pp[hcol][:], c_[:, k, :], zr[:, k, hcol * 512:(hcol + 1) * 512], k == 0, k == nt - 1, [c_, zr], [pp[hcol]])
                if "S" in parts:
                    kb.dma("pool", s_[:], Sm[:, fc * 128:(fc + 1) * 128].rearrange("(k p) f -> p k f", p=128), reads=[Sm], writes=[s_])
                    for k in range(nt):
                        for hcol in range(2):
                            kb.mm(pp[2 + hcol][:], s_[:, k, :], zi[:, k, hcol * 512:(hcol + 1) * 512], k == 0, k == nt - 1, [s_, zi], [pp[2 + hcol]])
                a_ = ra[fc % 2]
                b_ = rb[fc % 2]
                rows = slice(fc * 128, (fc + 1) * 128)
                if spec is None:
                    for hcol in range(2):
                        if "C" in parts:
                            kb.actf(a_[:, hcol * 512:(hcol + 1) * 512], pp[hcol][:], AF.Copy, [pp[hcol]], [a_], scale=float(scale))
                        if "S" in parts:
                            kb.ts("dve", b_[:, hcol * 512:(hcol + 1) * 512], pp[2 + hcol][:], float(scale), None, ALU.mult, None, [pp[2 + hcol]], [b_])
                else:
                    A, B = spec
                    ka_ = ka[fc % 2]
                    kb_ = kbb[fc % 2]
                    kb.dma("pool", ka_[:], A[o, rows, :], reads=[A], writes=[ka_])
                    kb.dma("pool", kb_[:], B[o, rows, :], reads=[B], writes=[kb_])
                    for hcol in range(2):
                        kb.copy("act", zrs[:, hcol * 512:(hcol + 1) * 512], pp[hcol][:], [pp[hcol]], [zrs])
                        kb.copy("act", zis[:, hcol * 512:(hcol + 1) * 512], pp[2 + hcol][:], [pp[2 + hcol]], [zis])
                    kb.tt("dve", t1[:], zrs[:], ka_[:], ALU.mult, [zrs, ka_], [t1])
                    kb.tt("dve", t2[:], zis[:], kb_[:], ALU.mult, [zis, kb_], [t2])
                    kb.tt("dve", a_[:], t1[:], t2[:], ALU.subtract, [t1, t2], [a_])
                    kb.tt("dve", t1[:], zrs[:], kb_[:], ALU.mult, [zrs, kb_], [t1])
                    kb.tt("dve", t2[:], zis[:], ka_[:], ALU.mult, [zis, ka_], [t2])
                    kb.tt("dve", b_[:], t1[:], t2[:], ALU.add, [t1, t2], [b_])
                if "C" in parts:
                    kb.dma("sp", OA[o, rows, :], a_[:], reads=[a_], writes=[OA])
                if "S" in parts:
                    kb.dma("sp", OB[o, rows, :], b_[:], reads=[b_], writes=[OB])

    def hy_inv(self, seg, Ls, tok0, o, ZT_in, zin_c0, gate_c0, OUT, out_c0, bias_i):
        kb = self.kb
        nt = Ls // 128
        CT = self.inputs["dftCT%d" % seg]
        ST = self.inputs["dftST%d" % seg]
        YR, YI = self.YR[seg], self.YI[seg]
        UT = self.UT
        TBK = min(1024, Ls)
        HB = min(512, TBK)
        nh = TBK // HB
        with ExitStack() as st:
            ct = kb.sbuf(st, "ict", [128, nt, TBK], F32R)
            s_t = kb.sbuf(st, "ist", [128, nt, TBK], F32R)
            yr = [kb.sbuf(st, "iyr%d" % j, [128, nt, 128], F32R) for j in range(2)]
            yi = [kb.sbuf(st, "iyi%d" % j, [128, nt, 128], F32R) for j in range(2)]
            zt = [kb.sbuf(st, "izt%d" % j, [128, TBK]) for j in range(2)]
            gt = [kb.sbuf(st, "igt%d" % j, [128, TBK]) for j in range(2)]
            ot = [kb.sbuf(st, "iot%d" % j, [128, TBK]) for j in range(2)]
            ps = [kb.psum(st, "ips%d" % j, [128, 512]) for j in range(4)]
            n = 0
            for tb in range(Ls // TBK):
                ts_ = slice(tb * TBK, (tb + 1) * TBK)
                tg = slice(tok0 + tb * TBK, tok0 + (tb + 1) * TBK)
                for hh in range(nh):
                    hs = slice(tb * TBK + hh * HB, tb * TBK + (hh + 1) * HB)
                    kb.dma("pool", ct[:, :, hh * HB:(hh + 1) * HB], CT[:, hs].rearrange("(k p) t -> p k t", p=128), reads=[CT], writes=[ct])
                    kb.dma("pool", s_t[:, :, hh * HB:(hh + 1) * HB], ST[:, hs].rearrange("(k p) t -> p k t", p=128), reads=[ST], writes=[s_t])
                for cc in range(8):
                    b = n % 2
                    n += 1
                    kb.dma("pool", yr[b][:], YR[o, :, cc * 128:(cc + 1) * 128].rearrange("(k p) c -> p k c", p=128), reads=[YR], writes=[yr[b]])
                    kb.dma("pool", yi[b][:], YI[o, :, cc * 128:(cc + 1) * 128].rearrange("(k p) c -> p k c", p=128), reads=[YI], writes=[yi[b]])
                    kb.dma("pool", zt[b][:], ZT_in[:, zin_c0 + cc, tg], reads=[ZT_in], writes=[zt[b]])
                    kb.dma("pool", gt[b][:], UT[:, gate_c0 + cc, tg], reads=[UT], writes=[gt[b]])
                    for hh in range(nh):
                        p = ps[(b * 2 + hh) % 4]
                        cs = slice(hh * HB, (hh + 1) * HB)
                        for k in range(nt):
                            kb.mm(p[:, 0:HB], yr[b][:, k, :], ct[:, k, cs], k == 0, False, [yr[b], ct], [p])
                        for k in range(nt):
                            kb.mm(p[:, 0:HB], yi[b][:, k, :], s_t[:, k, cs], False, k == nt - 1, [yi[b], s_t], [p])
                        kb.stt("dve", ot[b][:, cs], zt[b][:, cs], self.hybias[:, bias_i * 8 + cc:bias_i * 8 + cc + 1], p[:, 0:HB], ALU.mult, ALU.add,
                               [zt[b], self.hybias, p], [ot[b]])
                    kb.tt("dve", ot[b][:], ot[b][:], gt[b][:], ALU.mult, [ot[b], gt[b]], [ot[b]])
                    kb.dma("sp", OUT[:, out_c0 + cc, tg], ot[b][:], reads=[ot[b]], writes=[OUT])

    def hy_tok(self, seg, Ls, tok0, SRC, c0, ZTOK):
        kb = self.kb
        nt = Ls // 128
        with ExitStack() as st:
            src = [kb.sbuf(st, "tks%d" % j, [128, Ls]) for j in range(2)]
            ob = [kb.sbuf(st, "tko%d" % j, [128, 4, 128]) for j in range(2)]
            ps = [kb.psum(st, "tkp%d" % j, [128, 4, 128]) for j in range(2)]
            n = 0
            for cc in range(8):
                s_ = src[cc % 2]
                kb.dma("pool", s_[:], SRC[:, c0 + cc, tok0:tok0 + Ls], reads=[SRC], writes=[s_])
                for tg in range(0, nt, 4):
                    ng = min(4, nt - tg)
                    p = ps[n % 2]
                    o = ob[n % 2]
                    n += 1
                    for j in range(ng):
                        kb.tr(p[:, j, :], s_[:, (tg + j) * 128:(tg + j + 1) * 128], self.ident[:], [s_, self.ident], [p])
                    kb.copy("act" if n % 2 else "dve", o[:, 0:ng, :], p[:, 0:ng, :], [p], [o])
                    kb.dma("sp", ZTOK[tg * 128:(tg + ng) * 128, cc * 128:(cc + 1) * 128].rearrange("(j p) c -> p j c", p=128),
                           o[:, 0:ng, :], reads=[o], writes=[ZTOK])

    def stage_hyena(self, i, PH, CATT):
        kb = self.kb
        UT = self.UT
        cw = self.inputs["hy_conv"]
        hbi = self.inputs["hy_biasc"]
        kb.dma("sp", self.hybias[:], hbi[i], reads=[hbi], writes=[self.hybias])
        with ExitStack() as st:
            cwt = kb.sbuf(st, "hcw", [128, 24, 4])
            kb.dma("sp", cwt[:], cw[i], reads=[cw], writes=[cwt])
            xin = [kb.sbuf(st, "hxi%d" % j, [128, T]) for j in range(2)]
            uo = [kb.sbuf(st, "huo%d" % j, [128, T]) for j in range(2)]
            for c in range(24):
                x = xin[c % 2]
                u = uo[c % 2]
                kb.dma("pool", x[:], PH[:, c, :], reads=[PH], writes=[x])
                kb.actf(u[:], x[:], AF.Identity, [x, cwt], [u], scale=cwt[:, c, 1:2], bias=cwt[:, c, 3:4])
                for (a, b_) in ((0, L), (L, T)):
                    kb.stt("dve", u[:, a + 1:b_], x[:, a:b_ - 1], cwt[:, c, 0:1], u[:, a + 1:b_], ALU.mult, ALU.add, [x, cwt, u], [u])
                    kb.stt("pool", u[:, a:b_ - 1], x[:, a + 1:b_], cwt[:, c, 2:3], u[:, a:b_ - 1], ALU.mult, ALU.add, [x, cwt, u], [u])
                kb.dma("sp", UT[:, c, :], u[:], reads=[u], writes=[UT])
        for seg, (Ls, tok0) in enumerate(((L, 0), (LC, L))):
            self.hy_filters(i, Ls, seg)
            self.hy_tok(seg, Ls, tok0, UT, 16, self.ZTOK[seg])
            self.hy_fwd(seg, Ls, [(self.ZTOKv[seg], 0)], (self.SA[seg], self.SB[seg]), self.YR[seg], self.YI[seg], 0, 1.0)
            self.hy_inv(seg, Ls, tok0, 0, UT, 16, 0, self.ZT1, 0, 0)
            self.hy_tok(seg, Ls, tok0, self.ZT1, 0, self.ZTOK[seg])
            self.hy_fwd(seg, Ls, [(self.ZTOKv[seg], 0)], (self.SA[seg], self.SB[seg]), self.YR[seg], self.YI[seg], 1, 1.0)
            self.hy_inv(seg, Ls, tok0, 1, self.ZT1, 0, 8, CATT, 0, 1)

    def build(self, x_name="x"):
        nc = self.nc
        with ExitStack() as st:
            kb = KB(nc, st)
            self.kb = kb
            plans_na, pats = na_structure()
            nuq = pats.shape[0]
            x_in = self.inp("x", [T, D])
            cT = self.inp("cT", [128, 16, 2])
            self.inp("ada_w", [DEPTH, D, 6 * D])
            self.inp("ada_b", [DEPTH, 6 * D])
            self.inp("ln_g", [DEPTH, 2, D])
            self.inp("ln_b", [DEPTH, 2, D])
            self.inp("ev_w_in", [2, D, 4608])
            self.inp("ev_w_out", [2, D, D])
            self.inp("od_w_in", [2, D, 3 * D])
            self.inp("od_w_out", [2, D, D])
            self.inp("hy_conv", [2, 128, 24, 4])
            self.inp("hy_biasc", [2, 128, 16])
            self.inp("hy_f_w1", [2, 33, 64])
            self.inp("hy_f_w2", [2, 2, 64, 64])
            self.inp("hy_f_w3", [2, 64, 4096])
            self.inp("hy_fcol", [2, 64, 4])
            self.inp("swa_sink", [2, 8])
            self.inp("moe_w_router", [DEPTH, D, NEXP])
            self.inp("moe_w1", [DEPTH, NEXP, D, EFF])
            self.inp("moe_w3", [DEPTH, NEXP, D, EFF])
            self.inp("moe_w2", [DEPTH, NEXP, EFF, D])
            ident_in = self.inp("ident", [128, 128])
            ustr_in = self.inp("ustrict", [128, 128])
            iota_in = self.inp("iota", [128, 256])
            sel16_in = self.inp("sel16", [16, 16, 128])
            tokidx_in = self.inp("tokidx", [128, NT])
            ropeR_in = self.inp("ropeR", [128, 128])
            ropeC_in = self.inp("ropeC", [128, L])
            ropeS_in = self.inp("ropeS", [128, L])
            for seg, Ls in enumerate((L, LC)):
                self.inp("zembT%d" % seg, [33, Ls])
                self.inp("decay%d" % seg, [Ls, 1024])
                for nm in ("dftC", "dftS", "dftCT", "dftST"):
                    self.inp("%s%d" % (nm, seg), [Ls, Ls])
            evbt = self.inp("evbt", [1, 128, 2, 128])
            nab = self.inp("nab", [2, 16, 128, nuq, 128])
            self.M = [self.scratch("M%d" % l, [2, 6 * D]) for l in range(DEPTH)]
            HT = self.scratch("HT", [128, 16, T])
            PH = self.scratch("PH", [128, 24, T])
            QT = self.scratch("QT", [128, 16, T])
            KT = self.scratch("KT", [128, 16, T])
            V = self.scratch("V", [T, D])
            CATT = self.scratch("CATT", [128, 16, T])
            Y = self.scratch("Y", [T, D])
            XA = self.scratch("XA", [T, D])
            XB = self.scratch("XB", [T, D])
            H2 = self.scratch("H2", [T, D])
            Fo = self.scratch("Fo", [T, D])
            self.UT = self.scratch("UT", [128, 24, T])
            self.ZT1 = self.scratch("ZT1", [128, 8, T])
            self.ZTOKv, self.ZTOK, self.KP, self.KM, self.SA, self.SB, self.YR, self.YI = [], [], [], [], [], [], [], []
            self.YS, self.SELT = [], []
            for seg, Ls in enumerate((L, LC)):
                z3 = self.scratch("ZTOK%d" % seg, [1, Ls, 1024])
                z2 = Tile("ZTOK2_%d" % seg, z3.t[0], "dram")
                z2.trk = z3.trk
                self.ZTOKv.append(z3)
                self.ZTOK.append(z2)
                for nm, lst in (("KP", self.KP), ("KM", self.KM), ("SA", self.SA), ("SB", self.SB), ("YR", self.YR), ("YI", self.YI)):
                    lst.append(self.scratch("%s%d" % (nm, seg), [2, Ls, 1024]))
                cap = 2 * Ls // NEXP
                nsc_ = (cap + 127) // 128
                self.SELT.append(self.scratch("SELT%d" % seg, [Ls // 128, min(cap, 128), NEXP * nsc_, 128]))
            self.YSm = self.scratch("YSm", [NEXP, 288, D])
            out = self.kb.dram("out", [T, D], F32, kind="ExternalOutput")
            self.ident = kb.sbuf(st, "ident", [128, 128])
            self.ones = kb.sbuf(st, "ones", [128, 128])
            self.ustrict = kb.sbuf(st, "ustrict", [128, 128])
            self.iota = kb.sbuf(st, "iota", [128, 256])
            self.tokidx = kb.sbuf(st, "tokidx", [128, NT])
            kb.dma("sp", self.tokidx[:], tokidx_in[:], reads=[tokidx_in], writes=[self.tokidx])
            self.sel16 = kb.sbuf(st, "sel16", [16, 16, 128])
            kb.dma("sp", self.sel16[:], sel16_in[:], reads=[sel16_in], writes=[self.sel16])
            self.sT = kb.sbuf(st, "sT", [128, 16, 2])
            self.mcol = kb.sbuf(st, "mcol", [128, 2, 96])
            self.mcol1 = kb.sbuf(st, "mcol1", [128, 2, 96])
            self.hybias = kb.sbuf(st, "hybias", [128, 16])
            AFF = kb.sbuf(st, "AFF", [128, NT, NEXP])
            kb.dma("sp", self.ident[:], ident_in[:], reads=[ident_in], writes=[self.ident])
            kb.dma("sp", self.ustrict[:], ustr_in[:], reads=[ustr_in], writes=[self.ustrict])
            kb.dma("sp", self.iota[:], iota_in[:], reads=[iota_in], writes=[self.iota])
            kb.op("dve", lambda e: e.memset(self.ones[:], 1.0), writes=[self.ones])
            kb.dma("sp", self.sT[:], cT[:], reads=[cT], writes=[self.sT])
            kb.actf(self.sT[:], self.sT[:], AF.Silu, [self.sT], [self.sT])

            plans_ev = []
            for n in range(16):
                p = []
                if n >= 1:
                    p.append((n - 1, 0))
                p.append((n, None))
                if n <= 14:
                    p.append((n + 1, 1))
                p += [(16, None), (17, None)]
                plans_ev.append(p)
            plans_ev += [[(16, None), (17, None)]] * 2

            X = x_in
            stop = self.stop_after
            for l in range(self.l0, self.l0 + self.nlayers):
                i = l // 2
                last = (l == self.l0 + self.nlayers - 1)
                self.stage_mod(l)
                self.stage_modT(X, HT, 0, 1)
                if stop == "modT":
                    break
                if l % 2 == 0:
                    self.stage_proj(HT, self.inputs["ev_w_in"], i,
                                    [("fm", 0, 3072, PH), ("fm", 3072, 1024, QT), ("fm", 4096, 256, KT), ("tm", 4352, 256, V)])
                    if stop == "proj":
                        break
                    with ExitStack() as st2:
                        self.ropeR = kb.sbuf(st2, "ropeR", [128, 128], F32R)
                        self.ropeC = kb.sbuf(st2, "ropeC", [128, L])
                        self.ropeS = kb.sbuf(st2, "ropeS", [128, L])
                        kb.dma("pool", self.ropeR[:], ropeR_in[:], reads=[ropeR_in], writes=[self.ropeR])
                        kb.dma("sp", self.ropeC[:], ropeC_in[:], reads=[ropeC_in], writes=[self.ropeC])
                        kb.dma("pool", self.ropeS[:], ropeS_in[:], reads=[ropeS_in], writes=[self.ropeS])
                        self.stage_attn(QT, KT, V, CATT, 8, 8, 4, plans_ev, evbt, 2, False, True, i)
                    if stop == "attn":
                        break
                    self.stage_hyena(i, PH, CATT)
                    if stop == "hyena":
                        break
                    Wout = self.inputs["ev_w_out"]
                else:
                    self.stage_proj(HT, self.inputs["od_w_in"], i,
                                    [("fm", 0, 2048, QT), ("fm", 2048, 2048, KT), ("tm", 4096, 2048, V)])
                    if stop == "proj":
                        break
                    nabl = Tile("nab%d" % i, nab.t[i], "dram")
                    nabl.trk = nab.trk
                    self.stage_attn(QT, KT, V, CATT, 0, 16, 1, plans_na, nabl, nuq, True, False, None)
                    if stop == "attn":
                        break
                    Wout = self.inputs["od_w_out"]
                self.stage_proj(CATT, Wout, i, [("tm", 0, 2048, Y)])
                if stop == "oproj":
                    break
                self.stage_ln(l, 0, 2, X, Y, XA, H2=H2, AFF=AFF)
                if stop == "ln1":
                    break
                self.stage_moe(l, H2, AFF, Fo)
                if stop == "moe":
                    break
                Xn = out if last else XB
                self.stage_ln(l, 1, 5, XA, Fo, Xn)
                X = Xn
            outs = [t for t in [HT, PH, QT, KT, V, CATT, Y, XA, XB, H2, Fo, self.UT, self.ZT1] + self.M + self.KP + self.KM + self.SA
                    + self.SB + self.YR + self.YI + self.SELT + self.ZTOKv if t.name in self.dbg] + [out]
            kb.wait_all("sp", outs)
            kb.wait_all("pool", outs)
            kb.finalize()
            print("ninst", kb.ninst, "dsems", kb.ndsem)
        return nc


def na_structure():
    col = np.arange(64)
    cs = np.clip(col - 8, 0, 48)
    col_ok = (col[None, :] >= cs[:, None]) & (col[None, :] < cs[:, None] + 16)
    dc = np.clip(col[None, :] - col[:, None] + 15, 0, 30)
    uniq = {}
    pats = []
    plans = []
    for n in range(16):
        full = -np.ones((128, 2048), np.int64)
        for rr in range(2):
            r = 2 * n + rr
            rs = min(max(r - 4, 0), 24)
            for kr in range(8):
                krow = rs + kr
                dr = rs - r + kr + 7
                full[rr * 64:(rr + 1) * 64, krow * 64:(krow + 1) * 64] = np.where(col_ok, dr * 31 + dc, -1)
        plan = []
        for j in range(16):
            t = full[:, j * 128:(j + 1) * 128]
            if (t >= 0).any():
                key = t.tobytes()
                if key not in uniq:
                    uniq[key] = len(pats)
                    pats.append(np.ascontiguousarray(t.T))
                plan.append((j, uniq[key]))
        plan += [(16, None), (17, None)]
        plans.append(plan)
    plans += [[(16, None), (17, None)]] * 2
    return plans, np.stack(pats)


_CONST = {}


def host_consts():
    if _CONST:
        return _CONST
    f32 = np.float32
    c = _CONST
    c["ident"] = np.eye(128, dtype=f32)
    c["ustrict"] = np.triu(np.ones((128, 128), f32), 1)
    s16 = np.zeros((16, 16, 128), f32)
    for e in range(16):
        s16[e, e, :] = 1.0
    c["sel16"] = s16
    c["tokidx"] = (np.arange(NT, dtype=f32)[None, :] * 128 + np.arange(128, dtype=f32)[:, None]).astype(f32)
    c["iota"] = np.tile(np.arange(256, dtype=f32)[None, :], (128, 1))
    R = np.zeros((128, 128), f32)
    for m in range(128):
        if (m % 64) < 32:
            R[m + 32, m] = -1.0
        else:
            R[m - 32, m] = 1.0
    c["ropeR"] = R
    t = np.arange(L)
    inv = (10000.0 ** (-2.0 * np.arange(32, dtype=f32) / 64)).astype(f32)
    C = np.zeros((128, L), f32)
    S = np.zeros((128, L), f32)
    for d in range(128):
        pos = (t // GRID_W) if d < 64 else (t % GRID_W)
        ang = pos.astype(f32) * inv[d % 32]
        C[d] = np.cos(ang)
        S[d] = np.sin(ang)
    c["ropeC"] = C
    c["ropeS"] = S
    for seg, Ls in enumerate((L, LC)):
        tt = np.linspace(0.0, 1.0, Ls, dtype=f32)[:, None]
        w = (f32(2.0 * math.pi / Ls) * np.arange(Ls, dtype=f32))[:, None]
        f = np.linspace(1e-4, 15, 16, dtype=f32)[None, :]
        z = np.concatenate([tt, np.cos(f * w), -np.sin(f * w)], axis=-1).astype(f32)
        c["zembT%d" % seg] = np.ascontiguousarray(z.T)
        deltas = np.abs(np.linspace(math.log(1e-2) / 1.5, math.log(1e-2) / 0.3, 1024, dtype=f32))
        c["decay%d" % seg] = np.exp(-tt * deltas[None, :]).astype(f32)
        N2 = 2 * Ls
        tf = np.arange(Ls, dtype=np.float64)
        ph = np.pi * np.outer(tf, 2 * tf + 1) / N2
        Cm = np.cos(ph).astype(f32)
        Sm = (-np.sin(ph)).astype(f32)
        c["dftC%d" % seg] = Cm
        c["dftS%d" % seg] = Sm
        c["dftCT%d" % seg] = np.ascontiguousarray(Cm.T)
        c["dftST%d" % seg] = np.ascontiguousarray(Sm.T)
    a = np.arange(128)
    triA = np.where(a[None, :] <= a[:, None], 0.0, NEG).astype(f32)
    triB = np.where(a[:, None] <= a[None, :], 0.0, NEG).astype(f32)
    c["evbt"] = np.ascontiguousarray(np.stack([triA, triB], axis=1)[None])
    return c


def host_inputs(b, x, c, ctx, c_ctx, ada_w, ada_b, ln_g, ln_b, ev_w_in, ev_w_out, hy_conv_w, hy_conv_b,
                hy_f_w1, hy_f_b1, hy_f_w2, hy_f_b2, hy_f_w3, hy_f_freq, hy_bias, swa_sink,
                od_w_in, od_w_out, na_rpb, moe_w_router, moe_w1, moe_w3, moe_w2):
    f32 = np.float32
    m = dict(host_consts())
    m["x"] = np.ascontiguousarray(np.concatenate([x[b], ctx[b]], axis=0))
    cc = np.stack([c[b], c_ctx], axis=-1)
    m["cT"] = np.ascontiguousarray(cc.reshape(16, 128, 2).transpose(1, 0, 2))
    for k, v in (("ada_w", ada_w), ("ada_b", ada_b), ("ln_g", ln_g), ("ln_b", ln_b), ("ev_w_in", ev_w_in), ("ev_w_out", ev_w_out),
                 ("od_w_in", od_w_in), ("od_w_out", od_w_out), ("hy_f_w1", hy_f_w1), ("hy_f_w2", hy_f_w2), ("hy_f_w3", hy_f_w3),
                 ("swa_sink", swa_sink), ("moe_w_router", moe_w_router), ("moe_w1", moe_w1), ("moe_w3", moe_w3), ("moe_w2", moe_w2)):
        m[k] = v
    cw = np.concatenate([hy_conv_w, hy_conv_b[:, None, :]], axis=1)
    m["hy_conv"] = np.ascontiguousarray(cw.reshape(2, 4, 24, 128).transpose(0, 3, 2, 1))
    m["hy_biasc"] = np.ascontiguousarray(hy_bias.reshape(2, 2, 8, 128).transpose(0, 3, 1, 2).reshape(2, 128, 16))
    m["hy_fcol"] = np.ascontiguousarray(np.stack([hy_f_freq, hy_f_b1, hy_f_b2[:, 0], hy_f_b2[:, 1]], axis=-1))
    plans, pats = na_structure()
    flat = na_rpb.reshape(2, 16, -1)
    g = flat[:, :, np.maximum(pats, 0)]
    g = np.where(pats[None, None] >= 0, g, f32(NEG)).astype(f32)
    m["nab"] = np.ascontiguousarray(g.transpose(0, 1, 3, 2, 4))
    return m


_NC_CACHE = {}


def kernel(**inputs):
    inputs = {k: np.asarray(v) for k, v in inputs.items()}
    if "nc" not in _NC_CACHE:
        _NC_CACHE["nc"] = Prog().build()
    nc = _NC_CACHE["nc"]
    B = inputs["x"].shape[0]
    in_maps = [host_inputs(b, **inputs) for b in range(B)]
    res = run_bass_kernel_spmd(nc, in_maps, core_ids=list(range(B)))
    out = np.stack([np.asarray(r["out"])[:L] for r in res.results], axis=0)
    return out.astype(np.float32)
```

```python
import math
from contextlib import ExitStack
import numpy as np
import concourse.bass as bass
import concourse.mybir as mybir
from concourse.bass_utils import run_bass_kernel_spmd

F32 = mybir.dt.float32
F32R = mybir.dt.float32r
FAST_MM = True


def f32(ap):
    return ap.bitcast(F32) if ap.dtype == F32R else ap
I32 = mybir.dt.int32
AF = mybir.ActivationFunctionType
ALU = mybir.AluOpType
AX = mybir.AxisListType

D = 2048
L = 2048
LC = 256
T = L + LC
NT = T // 128
DEPTH = 4
GRID_W = 64
HY_DIM = 1024
NEXP = 16
EFF = 1024
DN_ALPHA = (2 * DEPTH) ** 0.25
LN_EPS = 1e-5
NEG = -30000.0


class Trk:
    __slots__ = ("name", "lw", "rd", "dsem")

    def __init__(self, name):
        self.name = name
        self.lw = None
        self.rd = []
        self.dsem = None


class Tile:
    def __init__(self, name, t, space):
        self.name = name
        self.t = t
        self.space = space
        self.trk = Trk(name)
        if space == "dram":
            self.trk.lw = {}
            self.trk.rd = {}

    def __getitem__(self, idx):
        return self.t[idx]


class KB:
    ENG = ("pe", "dve", "act", "pool", "sp")

    def __init__(self, nc, stack):
        self.nc = nc
        self.stack = stack
        self.prog = {e: [] for e in self.ENG}
        self.sems = {}
        self.cnt = {}
        self.seen = {e: {} for e in self.ENG}
        for e in ("pe", "dve", "act", "pool"):
            self._mksem("E_" + e)
        self.free_dsems = []
        self.pending = {}
        self.ndsem = 0
        self.ninst = 0
        self.rr = 0

    def _mksem(self, key):
        h = self.stack.enter_context(self.nc.semaphore(key))
        self.sems[key] = h
        self.cnt[key] = 0
        return key

    def _dsem_for(self, trk):
        if trk.dsem is None:
            if self.free_dsems:
                trk.dsem = self.free_dsems.pop()
            else:
                self.ndsem += 1
                trk.dsem = self._mksem("D%d" % self.ndsem)
        return trk.dsem

    def sbuf(self, st, name, shape, dtype=F32):
        self.uid = getattr(self, "uid", 0) + 1
        t = st.enter_context(self.nc.sbuf_tensor("s%d_%s" % (self.uid, name), list(shape), dtype))
        tl = Tile(name, t, "sbuf")
        tl.trk.rd = list(self.pending.items())
        st.callback(self._release, tl)
        return tl

    def psum(self, st, name, shape, dtype=F32):
        self.uid = getattr(self, "uid", 0) + 1
        t = st.enter_context(self.nc.psum_tensor("p%d_%s" % (self.uid, name), list(shape), dtype))
        tl = Tile(name, t, "psum")
        tl.trk.rd = list(self.pending.items())
        st.callback(self._release, tl)
        return tl

    def _release(self, tile):
        for d in [tile.trk.lw] + list(tile.trk.rd):
            if d is not None and d[1] > self.pending.get(d[0], 0):
                self.pending[d[0]] = d[1]
        if tile.trk.dsem is not None:
            self.free_dsems.append(tile.trk.dsem)
            tile.trk.dsem = None

    def dram(self, name, shape, dtype=F32, kind="Internal"):
        t = self.nc.dram_tensor(name, list(shape), dtype, kind=kind)
        return Tile(name, t.ap(), "dram")

    def _waits(self, e, reads, writes, skip_self=False):
        deps = {}

        def add(d):
            if d is None:
                return
            k, c = d
            if c > deps.get(k, 0):
                deps[k] = c
        for r in reads:
            if r.space == "dram":
                for d in r.trk.lw.items():
                    add(d)
            else:
                add(r.trk.lw)
        for w in writes:
            if w.space == "dram":
                for d in w.trk.rd.items():
                    add(d)
                if e not in ("sp", "pool"):
                    for d in w.trk.lw.items():
                        add(d)
            else:
                add(w.trk.lw)
                for d in w.trk.rd:
                    add(d)
        out = []
        seen = self.seen[e]
        for k, c in deps.items():
            if skip_self and k == "E_" + e:
                continue
            if k[0] == "D":
                c = self.cnt[k]
            if seen.get(k, 0) >= c:
                continue
            seen[k] = c
            out.append((k, c))
        return out

    def _commit(self, mark, reads, writes):
        for r in reads:
            if r.space == "dram":
                if mark[1] > r.trk.rd.get(mark[0], 0):
                    r.trk.rd[mark[0]] = mark[1]
                continue
            r.trk.rd.append(mark)
            if len(r.trk.rd) > 64:
                mx = {}
                for k, c in r.trk.rd:
                    if c > mx.get(k, 0):
                        mx[k] = c
                r.trk.rd = list(mx.items())
        for w in writes:
            if w.space == "dram":
                if mark[1] > w.trk.lw.get(mark[0], 0):
                    w.trk.lw[mark[0]] = mark[1]
                continue
            w.trk.lw = mark
            w.trk.rd = []

    def op(self, e, fn, reads=(), writes=()):
        waits = self._waits(e, reads, writes, skip_self=(e == "pe"))
        key = "E_" + e
        self.cnt[key] += 1
        c = self.cnt[key]
        sems = self.sems

        def emit(eng, waits=waits, fn=fn, key=key):
            for k, v in waits:
                eng.wait_ge(sems[k], v)
            fn(eng).then_inc(sems[key], 1)
        self.prog[e].append(emit)
        self._commit((key, c), reads, writes)
        self.ninst += 1

    def dma(self, q, out, in_, reads=(), writes=(), **kw):
        sb = [t for t in list(writes) + list(reads) if t.space != "dram"]
        assert len(sb) >= 1
        key = self._dsem_for(sb[0].trk)
        waits = self._waits(q, reads, writes)
        self.cnt[key] += 16
        c = self.cnt[key]
        sems = self.sems

        def emit(eng, waits=waits, key=key, out=out, in_=in_, kw=kw):
            for k, v in waits:
                eng.wait_ge(sems[k], v)
            eng.dma_start(out=out, in_=in_, **kw).then_inc(sems[key], 16)
        self.prog[q].append(emit)
        self._commit((key, c), reads, writes)
        self.ninst += 1

    def gather(self, dst, n, src, idx):
        key = self._dsem_for(dst.trk)
        waits = self._waits("pool", [idx, src], [dst])
        self.cnt[key] += 16
        c = self.cnt[key]
        sems = self.sems

        def emit(eng, waits=waits, key=key):
            for k, v in waits:
                eng.wait_ge(sems[k], v)
            eng.indirect_dma_start(out=dst[0:n, :], out_offset=None, in_=src[:, :],
                                   in_offset=bass.IndirectOffsetOnAxis(ap=idx[0:n, :], axis=0)).then_inc(sems[key], 16)
        self.prog["pool"].append(emit)
        self._commit((key, c), [idx, src], [dst])
        self.ninst += 1

    def q(self):
        self.rr += 1
        return ("sp", "pool")[self.rr % 2]

    def wait_all(self, e, tiles):
        waits = self._waits(e, tiles, ())
        sems = self.sems

        def emit(eng, waits=waits):
            for k, v in waits:
                eng.wait_ge(sems[k], v)
        self.prog[e].append(emit)

    def finalize(self):
        nc = self.nc
        prog = self.prog
        with nc.Block() as block:
            @block.sync
            def _(eng):
                for f in prog["sp"]:
                    f(eng)

            @block.tensor
            def _(eng):
                for f in prog["pe"]:
                    f(eng)

            @block.vector
            def _(eng):
                for f in prog["dve"]:
                    f(eng)

            @block.scalar
            def _(eng):
                for f in prog["act"]:
                    f(eng)

            @block.gpsimd
            def _(eng):
                for f in prog["pool"]:
                    f(eng)

    def mm(self, out, lhsT, rhs, start, stop, reads, writes):
        fast = (FAST_MM and lhsT.dtype == F32R and rhs.dtype == F32R and tuple(lhsT.shape) == (128, 128)
                and len(rhs.shape) == 2 and rhs.shape[1] % 2 == 0 and rhs.shape[1] >= 32)
        if not fast:
            lhsT = f32(lhsT)
            rhs = f32(rhs)
        self.op("pe", lambda e: e.matmul(out, lhsT=lhsT, rhs=rhs, start=start, stop=stop), reads=reads, writes=writes)

    def tr(self, out, in_, ident, reads, writes):
        in_ = f32(in_)
        self.op("pe", lambda e: e.transpose(out, in_, ident), reads=reads, writes=writes)

    def copy(self, e, out, in_, reads, writes):
        if e == "act":
            self.op("act", lambda g: g.activation(out=out, in_=in_, func=AF.Copy), reads=reads, writes=writes)
        else:
            self.op(e, lambda g: g.tensor_copy(out=out, in_=in_), reads=reads, writes=writes)

    def ts(self, e, out, in0, s1, s2, op0, op1, reads, writes):
        if s2 is None:
            self.op(e, lambda g: g.tensor_scalar(out=out, in0=in0, scalar1=s1, scalar2=None, op0=op0), reads=reads, writes=writes)
        else:
            self.op(e, lambda g: g.tensor_scalar(out=out, in0=in0, scalar1=s1, scalar2=s2, op0=op0, op1=op1), reads=reads, writes=writes)

    def tt(self, e, out, in0, in1, op, reads, writes):
        self.op(e, lambda g: g.tensor_tensor(out=out, in0=in0, in1=in1, op=op), reads=reads, writes=writes)

    def stt(self, e, out, in0, scalar, in1, op0, op1, reads, writes):
        e = "dve"
        self.op(e, lambda g: g.scalar_tensor_tensor(out=out, in0=in0, scalar=scalar, in1=in1, op0=op0, op1=op1),
                reads=reads, writes=writes)

    def actf(self, out, in_, func, reads, writes, scale=None, bias=None, accum_out=None):
        kw = {}
        if scale is not None:
            kw["scale"] = scale
        if bias is not None:
            kw["bias"] = bias
        if accum_out is not None:
            kw["accum_out"] = accum_out
        self.op("act", lambda g: g.activation(out=out, in_=in_, func=func, **kw), reads=reads, writes=writes)


class Prog:
    def __init__(self, dbg=(), nlayers=DEPTH, stop_after=None, l0=0):
        self.dbg = set(dbg)
        self.l0 = l0
        self.nlayers = nlayers
        self.stop_after = stop_after
        self.nc = bass.Bass("TRN2", target_bir_lowering=False)
        self.inputs = {}

    def inp(self, name, shape):
        t = self.nc.dram_tensor(name, list(shape), F32, kind="ExternalInput")
        tl = Tile(name, t.ap(), "dram")
        self.inputs[name] = tl
        return tl

    def scratch(self, name, shape):
        kind = "ExternalOutput" if name in self.dbg else "Internal"
        return self.kb.dram(name, shape, F32, kind=kind)

    def stage_mod(self, l):
        kb = self.kb
        Ml = self.M[l]
        with ExitStack() as st:
            wb = [kb.sbuf(st, "modw%d" % i, [128, 16, 512]) for i in range(2)]
            br = [kb.sbuf(st, "modb%d" % i, [1, 512]) for i in range(2)]
            ms = [kb.sbuf(st, "mods%d" % i, [2, 512]) for i in range(2)]
            ps = [kb.psum(st, "modp%d" % i, [2, 512]) for i in range(2)]
            aw = self.inputs["ada_w"]
            ab = self.inputs["ada_b"]
            for cb in range(24):
                w = wb[cb % 2]
                b = br[cb % 2]
                p = ps[cb % 2]
                m = ms[cb % 2]
                cs = slice(cb * 512, (cb + 1) * 512)
                kb.dma("pool", w[:], aw[l, :, cs].rearrange("(k p) n -> p k n", p=128), reads=[aw], writes=[w])
                kb.dma("pool", b[:], ab[l:l + 1, cs], reads=[ab], writes=[b])
                for k in range(16):
                    kb.mm(p[:], self.sT[:, k, :], w[:, k, :], k == 0, False, [self.sT, w], [p])
                kb.mm(p[:], self.ones[0:1, 0:2], b[:], False, True, [self.ones, b], [p])
                kb.copy("dve", m[:], p[:], [p], [m])
                kb.dma("sp", Ml[:, cs], m[:], reads=[m], writes=[Ml])
            mr = kb.sbuf(st, "modr", [96, 128])
            pc = kb.psum(st, "modpc", [128, 96])
            for r in range(2):
                kb.dma("sp", mr[:], Ml[r, :].rearrange("(c p) -> c p", p=128), reads=[Ml], writes=[mr])
                kb.tr(pc[:], mr[:], self.ident[0:96, 0:96], [mr, self.ident], [pc])
                kb.copy("dve", self.mcol[:, r, :], pc[:], [pc], [self.mcol])
                kb.ts("dve", self.mcol1[:, r, :], pc[:], 1.0, None, ALU.add, None, [pc], [self.mcol1])

    def stage_modT(self, X, HT, s_shift, s_scale):
        kb = self.kb
        with ExitStack() as st:
            xb = [kb.sbuf(st, "mtx%d" % i, [128, D]) for i in range(2)]
            hb = [kb.sbuf(st, "mth%d" % i, [128, 16, 128]) for i in range(2)]
            ps = [kb.psum(st, "mtp%d" % i, [128, 4, 128]) for i in range(2)]
            for i in range(NT):
                r = 0 if i < 16 else 1
                xt = xb[i % 2]
                ht = hb[i % 2]
                kb.dma("sp", xt[:], X[i * 128:(i + 1) * 128, :], reads=[X], writes=[xt])
                for g in range(4):
                    p = ps[g % 2]
                    for j in range(4):
                        c = g * 4 + j
                        kb.tr(p[:, j, :], xt[:, c * 128:(c + 1) * 128], self.ident[:], [xt, self.ident], [p])
                    for j in range(4):
                        c = g * 4 + j
                        sc = self.mcol1[:, r, s_scale * 16 + c:s_scale * 16 + c + 1]
                        sh = self.mcol[:, r, s_shift * 16 + c:s_shift * 16 + c + 1]
                        if j % 2 == 0:
                            kb.actf(ht[:, c, :], p[:, j, :], AF.Identity, [p, self.mcol, self.mcol1], [ht], scale=sc, bias=sh)
                        else:
                            kb.ts("dve", ht[:, c, :], p[:, j, :], sc, sh, ALU.mult, ALU.add, [p, self.mcol, self.mcol1], [ht])
                kb.dma("pool", HT[:, :, i * 128:(i + 1) * 128], ht[:], reads=[ht], writes=[HT])

    def stage_proj(self, HT, W, wsel, specs):
        kb = self.kb
        TB = T // 2
        mv = [(0, 512), (512, 512), (1024, 128)]
        with ExitStack() as st:
            hblk = kb.sbuf(st, "pjh", [128, 16, TB], F32R)
            wpan = [kb.sbuf(st, "pjw%d" % i, [128, 16, 128], F32R) for i in range(2)]
            wp2 = [kb.sbuf(st, "pjv%d" % i, [128, 16, 512], F32R) for i in range(2)]
            ot = [kb.sbuf(st, "pjo%d" % i, [128, TB]) for i in range(2)]
            ot2 = [kb.sbuf(st, "pjq%d" % i, [128, 512]) for i in range(2)]
            ps = [kb.psum(st, "pjp%d" % i, [128, 512]) for i in range(6)]
            n = 0
            n2 = 0
            for tb in range(2):
                t0 = tb * TB
                kb.dma("pool", hblk[:, 0:8, :], HT[:, 0:8, t0:t0 + TB], reads=[HT], writes=[hblk])
                kb.dma("pool", hblk[:, 8:16, :], HT[:, 8:16, t0:t0 + TB], reads=[HT], writes=[hblk])
                for kind, col0, ncols, OUT in specs:
                    if kind == "fm":
                        for cc in range(ncols // 128):
                            wp = wpan[n % 2]
                            o = ot[n % 2]
                            pp = ps[(n % 2) * 3:(n % 2) * 3 + 3]
                            c0 = col0 + cc * 128
                            kb.dma("pool", wp[:], W[wsel, :, c0:c0 + 128].rearrange("(k p) n -> p k n", p=128),
                                   reads=[W], writes=[wp])
                            for k in range(16):
                                for mi, (m0, mn) in enumerate(mv):
                                    kb.mm(pp[mi][:, :mn], wp[:, k, :], hblk[:, k, m0:m0 + mn], k == 0, k == 15,
                                          [wp, hblk], [pp[mi]])
                            for mi, (m0, mn) in enumerate(mv):
                                kb.copy("act" if mi != 1 else "dve", o[:, m0:m0 + mn], pp[mi][:, :mn], [pp[mi]], [o])
                            kb.dma("sp", OUT[:, cc, t0:t0 + TB], o[:], reads=[o], writes=[OUT])
                            n += 1
                    else:
                        PW = 512 if ncols % 512 == 0 else 256
                        for cb in range(ncols // PW):
                            wp = wp2[n2 % 2]
                            c0 = col0 + cb * PW
                            kb.dma("pool", wp[:, :, 0:PW], W[wsel, :, c0:c0 + PW].rearrange("(k p) n -> p k n", p=128),
                                   reads=[W], writes=[wp])
                            n2 += 1
                            for ti in range(TB // 128):
                                p = ps[n % 6]
                                o = ot2[n % 2]
                                for k in range(16):
                                    kb.mm(p[:, :PW], hblk[:, k, ti * 128:(ti + 1) * 128], wp[:, k, 0:PW], k == 0, k == 15,
                                          [hblk, wp], [p])
                                kb.copy("act" if n % 2 == 0 else "dve", o[:, 0:PW], p[:, :PW], [p], [o])
                                kb.dma("sp", OUT[t0 + ti * 128:t0 + (ti + 1) * 128, cb * PW:(cb + 1) * PW], o[:, 0:PW],
                                       reads=[o], writes=[OUT])
                                n += 1

    def stage_attn(self, QT, KT, V, CATT, cat_c0, nheads, gq, plans, BT, nuniq, bt_per_head, rope, sink_l):
        kb = self.kb
        scale = 128.0 ** -0.5
        with ExitStack() as st:
            qh = kb.sbuf(st, "aq", [128, T], F32R)
            kh = kb.sbuf(st, "ak", [128, T], F32R)
            va = kb.sbuf(st, "av", [128, NT, 132], F32R)
            oth = kb.sbuf(st, "ao", [128, T])
            bt = kb.sbuf(st, "abt", [128, nuniq, 128], F32R)
            identR = kb.sbuf(st, "aidr", [128, 128], F32R)
            kb.copy("dve", identR[:], self.ident[:], [self.ident], [identR])
            et = [kb.sbuf(st, "aet%d" % i, [128, 7, 128], F32R) for i in range(2)]
            osb = [kb.sbuf(st, "aos%d" % i, [128, 128]) for i in range(2)]
            rd = [kb.sbuf(st, "ard%d" % i, [128, 2]) for i in range(2)]
            psS = [[kb.psum(st, "aps%d%d" % (i, j), [128, 512]) for j in range(2)] for i in range(2)]
            psO = [kb.psum(st, "apo%d" % i, [128, 512]) for i in range(2)]
            psT = kb.psum(st, "apt", [128, 128])
            psR = kb.psum(st, "apr", [128, 512])
            if rope:
                rt1 = kb.sbuf(st, "art1", [128, 512])
                rt2 = kb.sbuf(st, "art2", [128, 512])
            if sink_l is not None:
                esk = kb.sbuf(st, "aesk", [128, 8])
                sk = self.inputs["swa_sink"]
                kb.dma("sp", esk[:], sk[sink_l:sink_l + 1, :].to_broadcast([128, 8]), reads=[sk], writes=[esk])
                kb.actf(esk[:], esk[:], AF.Exp, [esk], [esk])
            kb.copy("pool", va[:, :, 128:132], self.ones[:, 0:NT * 4].rearrange("p (a b) -> p a b", b=4), [self.ones], [va])
            def load_bt(hh):
                kb.dma("pool", bt[:], BT[hh], reads=[BT], writes=[bt])
                kb.ts("dve", bt[:].rearrange("p a b -> p (a b)"), f32(bt[:].rearrange("p a b -> p (a b)")), float(1.0 / scale), None,
                      ALU.mult, None, [bt], [bt])
            if not bt_per_head:
                load_bt(0)

            def do_rope(tl):
                for blk in range(4):
                    sl = slice(blk * 512, (blk + 1) * 512)
                    kb.mm(psR[:], self.ropeR[:], tl[:, sl], True, True, [self.ropeR, tl], [psR])
                    kb.tt("pool", rt1[:], f32(tl[:, sl]), self.ropeC[:, sl], ALU.mult, [tl, self.ropeC], [rt1])
                    kb.tt("dve", rt2[:], psR[:], self.ropeS[:, sl], ALU.mult, [psR, self.ropeS], [rt2])
                    kb.tt("dve", (tl[:, sl]), rt1[:], rt2[:], ALU.add, [rt1, rt2], [tl])

            cur_kv = -1
            it = 0
            for h in range(nheads):
                kv = h // gq
                kb.dma("pool", qh[:, 0:T // 2], QT[:, h, 0:T // 2], reads=[QT], writes=[qh])
                kb.dma("pool", qh[:, T // 2:T], QT[:, h, T // 2:T], reads=[QT], writes=[qh])
                if bt_per_head:
                    load_bt(h)
                if kv != cur_kv:
                    cur_kv = kv
                    kb.dma("pool", kh[:, 0:T // 2], KT[:, kv, 0:T // 2], reads=[KT], writes=[kh])
                    kb.dma("pool", kh[:, T // 2:T], KT[:, kv, T // 2:T], reads=[KT], writes=[kh])
                    kb.dma("pool", va[:, :, 0:128], V[:, kv * 128:(kv + 1) * 128].rearrange("(i p) d -> p i d", p=128),
                           reads=[V], writes=[va])
                    if rope:
                        do_rope(kh)
                if rope:
                    do_rope(qh)
                def emit_S(n, b):
                    blocks = plans[n]
                    pS = psS[b]
                    for jj, (j, bk) in enumerate(blocks):
                        p = pS[jj // 4]
                        cols = slice((jj % 4) * 128, (jj % 4 + 1) * 128)
                        kb.mm(p[:, cols], kh[:, j * 128:(j + 1) * 128], qh[:, n * 128:(n + 1) * 128],
                              True, bk is None, [kh, qh], [p])
                        if bk is not None:
                            kb.mm(p[:, cols], identR[:], bt[:, bk, :], False, True, [identR, bt], [p])

                def emit_exp(n, b):
                    nb = len(plans[n])
                    e_t = et[b]
                    pS = psS[b]
                    for bank in range((nb + 3) // 4):
                        w = min(4, nb - bank * 4)
                        kb.actf(e_t[:, bank * 4:bank * 4 + w, :], pS[bank][:, 0:w * 128].rearrange("p (a b) -> p a b", b=128),
                                AF.Exp, [pS[bank]], [e_t], scale=scale)

                def emit_PV(n, b):
                    blocks = plans[n]
                    nb = len(blocks)
                    for jj, (j, bk) in enumerate(blocks):
                        kb.mm(psO[b][:, 0:130], et[b][:, jj, :], va[:, j, 0:130], jj == 0, jj == nb - 1, [et[b], va], [psO[b]])

                def emit_norm(n, b):
                    r = rd[b]
                    o = osb[b]
                    pO = psO[b]
                    if sink_l is not None:
                        kb.tt("dve", r[:, 0:1], pO[:, 128:129], esk[:, h:h + 1], ALU.add, [pO, esk], [r])
                        kb.op("dve", lambda e, r=r: e.reciprocal(out=r[:, 1:2], in_=r[:, 0:1]), reads=[r], writes=[r])
                    else:
                        kb.op("dve", lambda e, r=r, pO=pO: e.reciprocal(out=r[:, 1:2], in_=pO[:, 128:129]), reads=[pO], writes=[r])
                    kb.ts("dve", o[:], pO[:, 0:128], r[:, 1:2], None, ALU.mult, None, [pO, r], [o])

                def emit_tr(n, b):
                    kb.tr(psT[:], osb[b][:], self.ident[:], [osb[b], self.ident], [psT])
                    kb.copy("act", oth[:, n * 128:(n + 1) * 128], psT[:], [psT], [oth])

                b0 = it % 2
                emit_S(0, b0)
                for n in range(NT):
                    b = (b0 + n) % 2
                    emit_exp(n, b)
                    if n + 1 < NT:
                        emit_S(n + 1, 1 - b)
                    emit_PV(n, b)
                    emit_norm(n, b)
                    if n >= 1:
                        emit_tr(n - 1, 1 - b)
                emit_tr(NT - 1, (b0 + NT - 1) % 2)
                it += NT
                kb.dma("sp", CATT[:, cat_c0 + h, :], oth[:], reads=[oth], writes=[CATT])

    def stage_ln(self, l, which, s_gate, X, Y, Xout, H2=None, AFF=None):
        kb = self.kb
        Ml = self.M[l]
        lg = self.inputs["ln_g"]
        lb = self.inputs["ln_b"]
        with ExitStack() as st:
            mg = [kb.sbuf(st, "lmg%d" % r, [128, D]) for r in range(2)]
            gb = kb.sbuf(st, "lgb", [128, D])
            bb = kb.sbuf(st, "lbb", [128, D])
            for r in range(2):
                kb.dma("pool", mg[r][:], Ml[r:r + 1, s_gate * D:(s_gate + 1) * D].to_broadcast([128, D]), reads=[Ml], writes=[mg[r]])
            kb.dma("pool", gb[:], lg[l, which:which + 1, :].to_broadcast([128, D]), reads=[lg], writes=[gb])
            kb.dma("pool", bb[:], lb[l, which:which + 1, :].to_broadcast([128, D]), reads=[lb], writes=[bb])
            if H2 is not None:
                m3 = [kb.sbuf(st, "lm3%d" % r, [128, D]) for r in range(2)]
                m4 = [kb.sbuf(st, "lm4%d" % r, [128, D]) for r in range(2)]
                for r in range(2):
                    kb.dma("pool", m3[r][:], Ml[r:r + 1, 3 * D:4 * D].to_broadcast([128, D]), reads=[Ml], writes=[m3[r]])
                    kb.dma("pool", m4[r][:], Ml[r:r + 1, 4 * D:5 * D].to_broadcast([128, D]), reads=[Ml], writes=[m4[r]])
                    kb.ts("pool", m4[r][:], m4[r][:], 1.0, None, ALU.add, None, [m4[r]], [m4[r]])
                wr = kb.sbuf(st, "lwr", [128, 16, NEXP])
                wrin = self.inputs["moe_w_router"]
                kb.dma("sp", wr[:], wrin[l].rearrange("(k p) e -> p k e", p=128), reads=[wrin], writes=[wr])
                h2b = [kb.sbuf(st, "lh2%d" % i, [128, D]) for i in range(2)]
                h2T = kb.sbuf(st, "lh2T", [128, 16, 128])
                psT = [kb.psum(st, "lpt%d" % i, [128, 4, 128]) for i in range(2)]
                psL = kb.psum(st, "lpl", [128, NEXP])
                lgt = kb.sbuf(st, "llg", [128, NEXP])
                sm = kb.sbuf(st, "lsm", [128, 4])
            xb = [kb.sbuf(st, "lx%d" % i, [128, D]) for i in range(2)]
            yb = [kb.sbuf(st, "ly%d" % i, [128, D]) for i in range(2)]
            stt_ = kb.sbuf(st, "lst", [128, 4, 6])
            mv = kb.sbuf(st, "lmv", [128, 4])
            def router(i):
                h2 = h2b[i % 2]
                for g in range(4):
                    p = psT[g % 2]
                    for j in range(4):
                        c = g * 4 + j
                        kb.tr(p[:, j, :], h2[:, c * 128:(c + 1) * 128], self.ident[:], [h2, self.ident], [p])
                    kb.copy("act", h2T[:, g * 4:(g + 1) * 4, :], p[:], [p], [h2T])
                for c in range(16):
                    kb.mm(psL[:], h2T[:, c, :], wr[:, c, :], c == 0, c == 15, [h2T, wr], [psL])
                kb.op("dve", lambda e: e.reduce_max(out=sm[:, 0:1], in_=psL[:], axis=AX.X), reads=[psL], writes=[sm])
                kb.ts("dve", sm[:, 1:2], sm[:, 0:1], -1.0, None, ALU.mult, None, [sm], [sm])
                kb.actf(lgt[:], psL[:], AF.Exp, [psL, sm], [lgt, sm], bias=sm[:, 1:2], accum_out=sm[:, 2:3])
                kb.op("dve", lambda e: e.reciprocal(out=sm[:, 3:4], in_=sm[:, 2:3]), reads=[sm], writes=[sm])
                kb.ts("dve", AFF[:, i, :], lgt[:], sm[:, 3:4], None, ALU.mult, None, [lgt, sm], [AFF])

            for i in range(NT):
                r = 0 if i < 16 else 1
                x = xb[i % 2]
                y = yb[i % 2]
                rows = slice(i * 128, (i + 1) * 128)
                if i == 0:
                    kb.dma("sp", x[:], X[rows, :], reads=[X], writes=[x])
                    kb.dma("sp", y[:], Y[rows, :], reads=[Y], writes=[y])
                if i + 1 < NT:
                    rn = slice((i + 1) * 128, (i + 2) * 128)
                    kb.dma("sp", xb[(i + 1) % 2][:], X[rn, :], reads=[X], writes=[xb[(i + 1) % 2]])
                    kb.dma("sp", yb[(i + 1) % 2][:], Y[rn, :], reads=[Y], writes=[yb[(i + 1) % 2]])
                kb.tt("dve", y[:], y[:], mg[r][:], ALU.mult, [y, mg[r]], [y])
                kb.stt("pool", y[:], x[:], float(DN_ALPHA), y[:], ALU.mult, ALU.add, [x, y], [y])
                for c in range(4):
                    kb.op("dve", lambda e, c=c, y=y: e.bn_stats(out=stt_[:, c, :], in_=y[:, c * 512:(c + 1) * 512]), reads=[y], writes=[stt_])
                kb.op("dve", lambda e: e.bn_aggr(out=mv[:, 0:2], in_=stt_[:].rearrange("p a b -> p (a b)")), reads=[stt_], writes=[mv])
                kb.ts("dve", mv[:, 2:3], mv[:, 1:2], float(LN_EPS), None, ALU.add, None, [mv], [mv])
                kb.actf(mv[:, 2:3], mv[:, 2:3], AF.Sqrt, [mv], [mv])
                kb.op("dve", lambda e: e.reciprocal(out=mv[:, 3:4], in_=mv[:, 2:3]), reads=[mv], writes=[mv])
                kb.ts("dve", y[:], y[:], mv[:, 0:1], mv[:, 3:4], ALU.subtract, ALU.mult, [y, mv], [y])
                kb.tt("pool", y[:], y[:], gb[:], ALU.mult, [y, gb], [y])
                kb.tt("dve", x[:], y[:], bb[:], ALU.add, [y, bb], [x])
                kb.dma("sp", Xout[rows, :], x[:], reads=[x], writes=[Xout])
                if H2 is not None:
                    h2 = h2b[i % 2]
                    kb.tt("pool", h2[:], x[:], m4[r][:], ALU.mult, [x, m4[r]], [h2])
                    kb.tt("pool", h2[:], h2[:], m3[r][:], ALU.add, [h2, m3[r]], [h2])
                    kb.dma("sp", H2[rows, :], h2[:], reads=[h2], writes=[H2])
                    if i >= 1:
                        router(i - 1)
            if H2 is not None:
                router(NT - 1)

    def stage_moe(self, l, H2, AFF, Fo):
        kb = self.kb
        W1 = self.inputs["moe_w1"]
        W3 = self.inputs["moe_w3"]
        W2 = self.inputs["moe_w2"]
        SEGS = [(0, 16, 256, 0), (16, 2, 32, 256)]
        CAPT = 288
        YS = self.YSm
        with ExitStack() as st:
            masks = [kb.sbuf(st, "gmask%d" % g, [128, SEGS[g][1], 16]) for g in range(2)]
            poss = [kb.sbuf(st, "gpos%d" % g, [128, SEGS[g][1], 16]) for g in range(2)]
            gate = kb.sbuf(st, "ggate", [128, 8])
            psAs = [kb.psum(st, "gpa%d" % i, [128, 512]) for i in range(2)]
            psUs = [kb.psum(st, "gpu%d" % i, [128, 512]) for i in range(2)]
            psYs = [kb.psum(st, "gpy%d" % i, [128, 512]) for i in range(2)]
            psM = kb.psum(st, "gpm", [128, 512])
            psG1 = kb.psum(st, "gpg", [128, 512])
            psG = psAs + psUs
            for g, (tile0, ntile, cap, soff) in enumerate(SEGS):
                N = ntile * 128
                mask, pos = masks[g], poss[g]
                st2 = ExitStack()
                afT = kb.sbuf(st2, "gafT", [16, N])
                bc = kb.sbuf(st2, "gbc", [128, N])
                junk = kb.sbuf(st2, "gjunk", [128, N])
                rank = kb.sbuf(st2, "grank", [128, ntile, 16])
                for i in range(ntile):
                    kb.tr(psM[0:16, 0:128], AFF[:, tile0 + i, :], self.ident[:], [AFF, self.ident], [psM])
                    kb.copy("dve", afT[:, i * 128:(i + 1) * 128], psM[0:16, 0:128], [psM], [afT])
                npb = 0
                for e_ in range(NEXP):
                    for blk in range(0, N, 512):
                        n_ = min(512, N - blk)
                        pb = psG[npb % 4]
                        npb += 1
                        kb.mm(pb[:, 0:n_], self.sel16[:, e_, :], afT[:, blk:blk + n_], True, True, [self.sel16, afT], [pb])
                        kb.copy("act", bc[:, blk:blk + n_], pb[:, 0:n_], [pb], [bc])
                    for i in range(ntile):
                        kb.op("dve", lambda g_, i=i, e_=e_, tile0=tile0, junk=junk, bc=bc, rank=rank: g_.tensor_scalar(
                            out=junk[:], in0=bc[:], scalar1=AFF[:, tile0 + i, e_:e_ + 1], scalar2=None, op0=ALU.is_gt, op1=ALU.add,
                            accum_out=rank[:, i, e_:e_ + 1]), reads=[bc, AFF], writes=[junk, rank])
                kb.ts("dve", mask[:].rearrange("p a b -> p (a b)"), rank[:].rearrange("p a b -> p (a b)"), float(cap), None, ALU.is_lt, None,
                      [rank], [mask])
                for i in range(ntile):
                    for i2 in range(i):
                        kb.mm(psM[:, 0:16], self.ones[:], mask[:, i2, :], i2 == 0, False, [self.ones, mask], [psM])
                    kb.mm(psM[:, 0:16], self.ustrict[:], mask[:, i, :], i == 0, True, [self.ustrict, mask], [psM])
                    kb.copy("dve", pos[:, i, :], psM[:, 0:16], [psM], [pos])
                st2.close()
            sels = [kb.sbuf(st, "gsel%d" % g, [128, SEGS[g][1], SEGS[g][2]], F32R) for g in range(2)]
            xsT = kb.sbuf(st, "gxsT", [128, 16, CAPT], F32R)
            gT = kb.sbuf(st, "ggT", [128, 8, 384], F32R)
            zpad = kb.sbuf(st, "gzp", [128, 96])
            kb.op("dve", lambda e: e.memset(zpad[:], 0.0), writes=[zpad])
            for fc_ in range(8):
                kb.copy("dve", gT[:, fc_, 288:384], zpad[:], [zpad], [gT])
            stg = [kb.sbuf(st, "gstg%d" % i, [128, 512]) for i in range(2)]
            xtok = [kb.sbuf(st, "gxk%d" % i, [128, D]) for i in range(3)]
            idxs = [kb.sbuf(st, "gix%d" % i, [128, 1], I32) for i in range(6)]
            ngp = 0
            wbuf = [kb.sbuf(st, "gw%d" % i, [128, 8192], F32R) for i in range(3)]
            sas = [kb.sbuf(st, "gsa%d" % i, [128, CAPT]) for i in range(2)]
            chunks = [(0, 0, 128, 0, 0), (0, 128, 128, 128, 1), (1, 0, 32, 256, 2)]
            cnt = {"nst": 0, "nw": 0, "ngp": 0, "ny": 0}
            gi = kb.sbuf(st, "ggi", [128, NT, 2])
            kb.copy("dve", gi[:, :, 1:2], self.tokidx[:].rearrange("p (a b) -> p a b", b=1), [self.tokidx], [gi])

            def prepA(e_):
                for g, (tile0, ntile, cap, soff) in enumerate(SEGS):
                    for i in range(ntile):
                        kb.ts("dve", sels[g][:, i, :], self.iota[:, 0:cap], poss[g][:, i, e_:e_ + 1], masks[g][:, i, e_:e_ + 1],
                              ALU.is_equal, ALU.mult, [self.iota, poss[g], masks[g]], [sels[g]])
                kb.copy("dve", gi[:, :, 0:1], AFF[:, :, e_:e_ + 1], [AFF], [gi])
                for ci, (g, s0, ns, gs0, gc) in enumerate(chunks):
                    tile0, ntile, cap, soff = SEGS[g]
                    for i in range(ntile):
                        kb.mm(psM[0:ns, 0:2], sels[g][:, i, s0:s0 + ns], gi[:, tile0 + i, :], i == 0, i == ntile - 1,
                              [sels[g], gi], [psM])
                    ix = idxs[(e_ * 3 + ci) % 6]
                    kb.copy("dve", ix[0:ns, :], psM[0:ns, 1:2], [psM], [ix])
                    gcol = (e_ % 2) * 4 + gc
                    kb.copy("dve", gate[0:ns, gcol:gcol + 1], psM[0:ns, 0:1], [psM], [gate])
                    kb.gather(xtok[ci], ns, H2, ix)
                for (g, s0, ns, gs0, gc) in chunks:
                    tile0, ntile, cap, soff = SEGS[g]
                    nsc = (cap + 127) // 128
                    sc = s0 // 128
                    for ig in range(0, ntile, 4):
                        ng = min(4, ntile - ig)
                        for i in range(ig, ig + ng):
                            kb.tr(psM[0:ns, (i - ig) * 128:(i - ig + 1) * 128], sels[g][:, i, s0:s0 + ns], self.ident[:],
                                  [sels[g], self.ident], [psM])
                        sg = stg[cnt["nst"] % 2]
                        cnt["nst"] += 1
                        kb.copy("act", sg[0:ns, 0:ng * 128], psM[0:ns, 0:ng * 128], [psM], [sg])
                        kb.dma("sp", self.SELT[g][ig:ig + ng, 0:ns, e_ * nsc + sc, :].rearrange("i p t -> p i t"),
                               sg[0:ns, 0:ng * 128].rearrange("p (i t) -> p i t", t=128), reads=[sg], writes=[self.SELT[g]])
            def prepB(e_):
                for ci, (g, s0, ns, gs0, gc) in enumerate(chunks):
                    xk = xtok[ci]
                    for dg in range(4):
                        p = psG1
                        for j in range(4):
                            dc = dg * 4 + j
                            kb.tr(p[:, j * 128:j * 128 + ns], xk[0:ns, dc * 128:(dc + 1) * 128], self.ident[0:ns, 0:ns], [xk, self.ident], [p])
                        kb.copy("act" if dg % 2 == 0 else "dve", xsT[:, dg * 4:(dg + 1) * 4, gs0:gs0 + ns],
                                p[:].rearrange("p (a b) -> p a b", b=128)[:, :, 0:ns], [p], [xsT])

            def ffn_up(e_):
                for fh in range(2):
                    w1 = wbuf[cnt["nw"] % 3]
                    cnt["nw"] += 1
                    w3 = wbuf[cnt["nw"] % 3]
                    cnt["nw"] += 1
                    kb.dma("pool", w1[:].rearrange("p (k n) -> p k n", n=512),
                           W1[l, e_, :, fh * 512:(fh + 1) * 512].rearrange("(k p) n -> p k n", p=128), reads=[W1], writes=[w1])
                    kb.dma("pool", w3[:].rearrange("p (k n) -> p k n", n=512),
                           W3[l, e_, :, fh * 512:(fh + 1) * 512].rearrange("(k p) n -> p k n", p=128), reads=[W3], writes=[w3])
                    for fc in range(4):
                        psA = psAs[fc % 2]
                        psU = psUs[fc % 2]
                        sa_ = sas[fc % 2]
                        for dc in range(16):
                            kb.mm(psA[:, 0:CAPT], w1[:, dc * 512 + fc * 128:dc * 512 + (fc + 1) * 128], xsT[:, dc, :], dc == 0, dc == 15,
                                  [w1, xsT], [psA])
                        for dc in range(16):
                            kb.mm(psU[:, 0:CAPT], w3[:, dc * 512 + fc * 128:dc * 512 + (fc + 1) * 128], xsT[:, dc, :], dc == 0, dc == 15,
                                  [w3, xsT], [psU])
                        kb.actf(sa_[:], psA[:, 0:CAPT], AF.Silu, [psA], [sa_])
                        kb.tt("dve", gT[:, fh * 4 + fc, 0:CAPT], sa_[:], psU[:, 0:CAPT], ALU.mult, [sa_, psU], [gT])

            def ffn_down(e_):
                for dh in range(2):
                    w2 = wbuf[cnt["nw"] % 3]
                    cnt["nw"] += 1
                    kb.dma("pool", w2[:].rearrange("p (k n) -> p k n", n=1024),
                           W2[l, e_, :, dh * 1024:(dh + 1) * 1024].rearrange("(k p) n -> p k n", p=128), reads=[W2], writes=[w2])
                    for (g, s0, ns, gs0, gc) in chunks:
                        gcol = (e_ % 2) * 4 + gc
                        for db in range(2):
                            psY = psYs[cnt["ny"] % 2]
                            cnt["ny"] += 1
                            mrows = 128
                            for fc in range(8):
                                kb.mm(psY[0:mrows, :], gT[:, fc, gs0:gs0 + mrows], w2[:, fc * 1024 + db * 512:fc * 1024 + (db + 1) * 512],
                                      fc == 0, fc == 7, [gT, w2], [psY])
                            sg = stg[cnt["nst"] % 2]
                            cnt["nst"] += 1
                            kb.actf(sg[0:ns, :], psY[0:ns, :], AF.Identity, [psY, gate], [sg], scale=gate[0:ns, gcol:gcol + 1])
                            c0 = dh * 1024 + db * 512
                            kb.dma("sp", YS[e_, gs0:gs0 + ns, c0:c0 + 512], sg[0:ns, :], reads=[sg], writes=[YS])

            prepA(0)
            prepB(0)
            for e_ in range(NEXP):
                if e_ + 1 < NEXP:
                    prepA(e_ + 1)
                ffn_up(e_)
                if e_ + 1 < NEXP:
                    prepB(e_ + 1)
                ffn_down(e_)
        for g, (tile0, ntile, cap, soff) in enumerate(SEGS):
            nsc = (cap + 127) // 128
            sl_ = min(cap, 128)
            with ExitStack() as st:
                ysall = kb.sbuf(st, "cys", [128, NEXP * nsc, 512], F32R)
                selt = [kb.sbuf(st, "cse%d" % i, [128, NEXP * nsc, 128], F32R) for i in range(2)]
                ob = [kb.sbuf(st, "cob%d" % i, [128, 512]) for i in range(2)]
                ps = [kb.psum(st, "cps%d" % i, [128, 512]) for i in range(2)]
                n = 0
                for db in range(4):
                    for e_ in range(NEXP):
                        kb.dma("pool", ysall[0:sl_, e_ * nsc:(e_ + 1) * nsc, :],
                               YS[e_, soff:soff + cap, db * 512:(db + 1) * 512].rearrange("(s p) d -> p s d", p=sl_), reads=[YS], writes=[ysall])
                    for i in range(ntile):
                        se = selt[n % 2]
                        p = ps[n % 2]
                        o = ob[n % 2]
                        n += 1
                        kb.dma("pool", se[0:sl_, :, :], self.SELT[g][i, 0:sl_, :, :], reads=[self.SELT[g]], writes=[se])
                        for q_ in range(NEXP * nsc):
                            kb.mm(p[:], se[0:sl_, q_, :], ysall[0:sl_, q_, :], q_ == 0, q_ == NEXP * nsc - 1, [se, ysall], [p])
                        kb.copy("act", o[:], p[:], [p], [o])
                        kb.dma("sp", Fo[(tile0 + i) * 128:(tile0 + i + 1) * 128, db * 512:(db + 1) * 512], o[:], reads=[o], writes=[Fo])

    def hy_filters(self, i, Ls, seg):
        kb = self.kb
        nt = Ls // 128
        zT = self.inputs["zembT%d" % seg]
        dec = self.inputs["decay%d" % seg]
        fw1 = self.inputs["hy_f_w1"]
        fw2 = self.inputs["hy_f_w2"]
        fw3 = self.inputs["hy_f_w3"]
        fcol = self.inputs["hy_fcol"]
        KP, KM = self.KP[seg], self.KM[seg]
        with ExitStack() as st:
            z = kb.sbuf(st, "fz", [33, Ls])
            w1 = kb.sbuf(st, "fw1", [33, 64])
            w2 = kb.sbuf(st, "fw2", [64, 2, 64])
            w3 = kb.sbuf(st, "fw3", [64, 4096])
            fc = kb.sbuf(st, "fc", [64, 8])
            ha = kb.sbuf(st, "fha", [64, Ls])
            hb_ = kb.sbuf(st, "fhb", [64, Ls])
            wr_ = kb.sbuf(st, "fwr", [64, 512])
            wa_ = kb.sbuf(st, "fwa", [64, 512])
            wb_ = kb.sbuf(st, "fwb", [64, 512])
            dt = [kb.sbuf(st, "fdt%d" % j, [128, 1024]) for j in range(2)]
            ft = [kb.sbuf(st, "fft%d" % j, [128, 4096]) for j in range(2)]
            kp = [kb.sbuf(st, "fkp%d" % j, [128, 2048]) for j in range(2)]
            km = [kb.sbuf(st, "fkm%d" % j, [128, 2048]) for j in range(2)]
            ps = [kb.psum(st, "fps%d" % j, [128, 512]) for j in range(4)]
            kb.dma("sp", z[:], zT[:], reads=[zT], writes=[z])
            kb.dma("sp", w1[:], fw1[i], reads=[fw1], writes=[w1])
            kb.dma("sp", w2[:], fw2[i].rearrange("a k n -> k a n"), reads=[fw2], writes=[w2])
            kb.dma("pool", w3[:], fw3[i], reads=[fw3], writes=[w3])
            kb.dma("sp", fc[:, 0:4], fcol[i], reads=[fcol], writes=[fc])
            for j in range(3):
                kb.tt("dve", fc[:, 4 + j:5 + j], fc[:, 1 + j:2 + j], fc[:, 0:1], ALU.mult, [fc], [fc])
            src = z
            srcK = 33
            cur = ha
            for layer in range(3):
                wl = w1[:, :] if layer == 0 else w2[:, layer - 1, :]
                for blk in range(0, Ls, 512):
                    n_ = min(512, Ls - blk)
                    p = ps[(blk // 512) % 4]
                    kb.mm(p[0:64, 0:n_], wl, src[0:srcK, blk:blk + n_], True, True, [w1, w2, src], [p])
                    kb.ts("dve", wr_[:, 0:n_], p[0:64, 0:n_], fc[:, 0:1], fc[:, 4 + layer:5 + layer], ALU.mult, ALU.add, [p, fc], [wr_])
                    kb.ts("dve", wa_[:, 0:n_], wr_[:, 0:n_], -math.pi, 2 * math.pi, ALU.is_lt, ALU.mult, [wr_], [wa_])
                    kb.ts("pool", wb_[:, 0:n_], wr_[:, 0:n_], math.pi, 2 * math.pi, ALU.is_gt, ALU.mult, [wr_], [wb_])
                    kb.tt("dve", wr_[:, 0:n_], wr_[:, 0:n_], wa_[:, 0:n_], ALU.add, [wr_, wa_], [wr_])
                    kb.tt("dve", wr_[:, 0:n_], wr_[:, 0:n_], wb_[:, 0:n_], ALU.subtract, [wr_, wb_], [wr_])
                    kb.actf(cur[:, blk:blk + n_], wr_[:, 0:n_], AF.Sin, [wr_], [cur])
                src = cur
                srcK = 64
                cur = hb_ if cur is ha else ha
            hfin = src
            for tc in range(nt):
                d_ = dt[tc % 2]
                f_ = ft[tc % 2]
                kb.dma("sp", d_[:], dec[tc * 128:(tc + 1) * 128, :], reads=[dec], writes=[d_])
                for cb in range(8):
                    p = ps[cb % 4]
                    kb.mm(p[:], hfin[:, tc * 128:(tc + 1) * 128], w3[:, cb * 512:(cb + 1) * 512], True, True, [hfin, w3], [p])
                    kb.tt("dve", f_[:, cb * 512:(cb + 1) * 512], p[:], d_[:, (cb % 2) * 512:(cb % 2 + 1) * 512], ALU.mult, [p, d_], [f_])
                a = kp[tc % 2]
                m = km[tc % 2]
                for o in range(2):
                    kb.tt("pool", a[:, o * 1024:(o + 1) * 1024], f_[:, o * 2048:o * 2048 + 1024], f_[:, o * 2048 + 1024:(o + 1) * 2048],
                          ALU.add, [f_], [a])
                    kb.tt("pool", m[:, o * 1024:(o + 1) * 1024], f_[:, o * 2048:o * 2048 + 1024], f_[:, o * 2048 + 1024:(o + 1) * 2048],
                          ALU.subtract, [f_], [m])
                    kb.dma("sp", KP[o, tc * 128:(tc + 1) * 128, :], a[:, o * 1024:(o + 1) * 1024], reads=[a], writes=[KP])
                    kb.dma("sp", KM[o, tc * 128:(tc + 1) * 128, :], m[:, o * 1024:(o + 1) * 1024], reads=[m], writes=[KM])
        for o in range(2):
            self.hy_fwd(seg, Ls, [(self.KP[seg], o)], None, self.SA[seg], self.SB[seg], o, 1.0 / Ls, parts="C")
            self.hy_fwd(seg, Ls, [(self.KM[seg], o)], None, self.SA[seg], self.SB[seg], o, 1.0 / Ls, parts="S")

    def hy_fwd(self, seg, Ls, srcs, spec, OA, OB, o, scale, parts="CS"):
        kb = self.kb
        nt = Ls // 128
        Cm = self.inputs["dftC%d" % seg]
        Sm = self.inputs["dftS%d" % seg]
        with ExitStack() as st:
            zr = kb.sbuf(st, "dz0", [128, nt, 1024], F32R)
            zi = zr
            kb.dma("pool", zr[:, 0:nt // 2, :], srcs[0][0][srcs[0][1], 0:Ls // 2, :].rearrange("(k p) c -> p k c", p=128), reads=[srcs[0][0]], writes=[zr])
            kb.dma("pool", zr[:, nt // 2:nt, :], srcs[0][0][srcs[0][1], Ls // 2:Ls, :].rearrange("(k p) c -> p k c", p=128), reads=[srcs[0][0]], writes=[zr])
            cp = [kb.sbuf(st, "dcp%d" % j, [128, nt, 128], F32R) for j in range(2)]
            sp = [kb.sbuf(st, "dsp%d" % j, [128, nt, 128], F32R) for j in range(2)]
            ps = [kb.psum(st, "dps%d" % j, [128, 512]) for j in range(8)]
            ra = [kb.sbuf(st, "dra%d" % j, [128, 1024]) for j in range(2)]
            rb = [kb.sbuf(st, "drb%d" % j, [128, 1024]) for j in range(2)]
            if spec is not None:
                ka = [kb.sbuf(st, "dka%d" % j, [128, 1024]) for j in range(2)]
                kbb = [kb.sbuf(st, "dkb%d" % j, [128, 1024]) for j in range(2)]
                t1 = kb.sbuf(st, "dt1", [128, 1024])
                t2 = kb.sbuf(st, "dt2", [128, 1024])
                zrs = kb.sbuf(st, "dzr", [128, 1024])
                zis = kb.sbuf(st, "dzi", [128, 1024])
            for fc in range(nt):
                c_ = cp[fc % 2]
                s_ = sp[fc % 2]
                pp = ps[(fc % 2) * 4:(fc % 2) * 4 + 4]
                if "C" in parts:
                    kb.dma("pool", c_[:], Cm[:, fc * 128:(fc + 1) * 128].rearrange("(k p) f -> p k f", p=128), reads=[Cm], writes=[c_])
                    for k in range(nt):
                        for hcol in range(2):
                            kb.mm(pp[hcol][:], c_[:, k, :], zr[:, k, hcol * 512:(hcol + 1) * 512], k == 0, k == nt - 1, [c_, zr], [pp[hcol]])
                if "S" in parts:
                    kb.dma("pool", s_[:], Sm[:, fc * 128:(fc + 1) * 128].rearrange("(k p) f -> p k f", p=128), reads=[Sm], writes=[s_])
                    for k in range(nt):
                        for hcol in range(2):
                            kb.mm(pp[2 + hcol][:], s_[:, k, :], zi[:, k, hcol * 512:(hcol + 1) * 512], k == 0, k == nt - 1, [s_, zi], [pp[2 + hcol]])
                a_ = ra[fc % 2]
                b_ = rb[fc % 2]
                rows = slice(fc * 128, (fc + 1) * 128)
                if spec is None:
                    for hcol in range(2):
                        if "C" in parts:
                            kb.actf(a_[:, hcol * 512:(hcol + 1) * 512], pp[hcol][:], AF.Copy, [pp[hcol]], [a_], scale=float(scale))
                        if "S" in parts:
                            kb.ts("dve", b_[:, hcol * 512:(hcol + 1) * 512], pp[2 + hcol][:], float(scale), None, ALU.mult, None, [pp[2 + hcol]], [b_])
                else:
                    A, B = spec
                    ka_ = ka[fc % 2]
                    kb_ = kbb[fc % 2]
                    kb.dma("pool", ka_[:], A[o, rows, :], reads=[A], writes=[ka_])
                    kb.dma("pool", kb_[:], B[o, rows, :], reads=[B], writes=[kb_])
                    for hcol in range(2):
                        kb.copy("act", zrs[:, hcol * 512:(hcol + 1) * 512], pp[hcol][:], [pp[hcol]], [zrs])
                        kb.copy("act", zis[:, hcol * 512:(hcol + 1) * 512], pp[2 + hcol][:], [pp[2 + hcol]], [zis])
                    kb.tt("dve", t1[:], zrs[:], ka_[:], ALU.mult, [zrs, ka_], [t1])
                    kb.tt("dve", t2[:], zis[:], kb_[:], ALU.mult, [zis, kb_], [t2])
                    kb.tt("dve", a_[:], t1[:], t2[:], ALU.subtract, [t1, t2], [a_])
                    kb.tt("dve", t1[:], zrs[:], kb_[:], ALU.mult, [zrs, kb_], [t1])
                    kb.tt("dve", t2[:], zis[:], ka_[:], ALU.mult, [zis, ka_], [t2])
                    kb.tt("dve", b_[:], t1[:], t2[:], ALU.add, [t1, t2], [b_])
                if "C" in parts:
                    kb.dma("sp", OA[o, rows, :], a_[:], reads=[a_], writes=[OA])
                if "S" in parts:
                    kb.dma("sp", OB[o, rows, :], b_[:], reads=[b_], writes=[OB])

    def hy_inv(self, seg, Ls, tok0, o, ZT_in, zin_c0, gate_c0, OUT, out_c0, bias_i):
        kb = self.kb
        nt = Ls // 128
        CT = self.inputs["dftCT%d" % seg]
        ST = self.inputs["dftST%d" % seg]
        YR, YI = self.YR[seg], self.YI[seg]
        UT = self.UT
        TBK = min(1024, Ls)
        HB = min(512, TBK)
        nh = TBK // HB
        with ExitStack() as st:
            ct = kb.sbuf(st, "ict", [128, nt, TBK], F32R)
            s_t = kb.sbuf(st, "ist", [128, nt, TBK], F32R)
            yr = [kb.sbuf(st, "iyr%d" % j, [128, nt, 128], F32R) for j in range(2)]
            yi = [kb.sbuf(st, "iyi%d" % j, [128, nt, 128], F32R) for j in range(2)]
            zt = [kb.sbuf(st, "izt%d" % j, [128, TBK]) for j in range(2)]
            gt = [kb.sbuf(st, "igt%d" % j, [128, TBK]) for j in range(2)]
            ot = [kb.sbuf(st, "iot%d" % j, [128, TBK]) for j in range(2)]
            ps = [kb.psum(st, "ips%d" % j, [128, 512]) for j in range(4)]
            n = 0
            for tb in range(Ls // TBK):
                ts_ = slice(tb * TBK, (tb + 1) * TBK)
                tg = slice(tok0 + tb * TBK, tok0 + (tb + 1) * TBK)
                for hh in range(nh):
                    hs = slice(tb * TBK + hh * HB, tb * TBK + (hh + 1) * HB)
                    kb.dma("pool", ct[:, :, hh * HB:(hh + 1) * HB], CT[:, hs].rearrange("(k p) t -> p k t", p=128), reads=[CT], writes=[ct])
                    kb.dma("pool", s_t[:, :, hh * HB:(hh + 1) * HB], ST[:, hs].rearrange("(k p) t -> p k t", p=128), reads=[ST], writes=[s_t])
                for cc in range(8):
                    b = n % 2
                    n += 1
                    kb.dma("pool", yr[b][:], YR[o, :, cc * 128:(cc + 1) * 128].rearrange("(k p) c -> p k c", p=128), reads=[YR], writes=[yr[b]])
                    kb.dma("pool", yi[b][:], YI[o, :, cc * 128:(cc + 1) * 128].rearrange("(k p) c -> p k c", p=128), reads=[YI], writes=[yi[b]])
                    kb.dma("pool", zt[b][:], ZT_in[:, zin_c0 + cc, tg], reads=[ZT_in], writes=[zt[b]])
                    kb.dma("pool", gt[b][:], UT[:, gate_c0 + cc, tg], reads=[UT], writes=[gt[b]])
                    for hh in range(nh):
                        p = ps[(b * 2 + hh) % 4]
                        cs = slice(hh * HB, (hh + 1) * HB)
                        for k in range(nt):
                            kb.mm(p[:, 0:HB], yr[b][:, k, :], ct[:, k, cs], k == 0, False, [yr[b], ct], [p])
                        for k in range(nt):
                            kb.mm(p[:, 0:HB], yi[b][:, k, :], s_t[:, k, cs], False, k == nt - 1, [yi[b], s_t], [p])
                        kb.stt("dve", ot[b][:, cs], zt[b][:, cs], self.hybias[:, bias_i * 8 + cc:bias_i * 8 + cc + 1], p[:, 0:HB], ALU.mult, ALU.add,
                               [zt[b], self.hybias, p], [ot[b]])
                    kb.tt("dve", ot[b][:], ot[b][:], gt[b][:], ALU.mult, [ot[b], gt[b]], [ot[b]])
                    kb.dma("sp", OUT[:, out_c0 + cc, tg], ot[b][:], reads=[ot[b]], writes=[OUT])

    def hy_tok(self, seg, Ls, tok0, SRC, c0, ZTOK):
        kb = self.kb
        nt = Ls // 128
        with ExitStack() as st:
            src = [kb.sbuf(st, "tks%d" % j, [128, Ls]) for j in range(2)]
            ob = [kb.sbuf(st, "tko%d" % j, [128, 4, 128]) for j in range(2)]
            ps = [kb.psum(st, "tkp%d" % j, [128, 4, 128]) for j in range(2)]
            n = 0
            for cc in range(8):
                s_ = src[cc % 2]
                kb.dma("pool", s_[:], SRC[:, c0 + cc, tok0:tok0 + Ls], reads=[SRC], writes=[s_])
                for tg in range(0, nt, 4):
                    ng = min(4, nt - tg)
                    p = ps[n % 2]
                    o = ob[n % 2]
                    n += 1
                    for j in range(ng):
                        kb.tr(p[:, j, :], s_[:, (tg + j) * 128:(tg + j + 1) * 128], self.ident[:], [s_, self.ident], [p])
                    kb.copy("act" if n % 2 else "dve", o[:, 0:ng, :], p[:, 0:ng, :], [p], [o])
                    kb.dma("sp", ZTOK[tg * 128:(tg + ng) * 128, cc * 128:(cc + 1) * 128].rearrange("(j p) c -> p j c", p=128),
                           o[:, 0:ng, :], reads=[o], writes=[ZTOK])

    def stage_hyena(self, i, PH, CATT):
        kb = self.kb
        UT = self.UT
        cw = self.inputs["hy_conv"]
        hbi = self.inputs["hy_biasc"]
        kb.dma("sp", self.hybias[:], hbi[i], reads=[hbi], writes=[self.hybias])
        with ExitStack() as st:
            cwt = kb.sbuf(st, "hcw", [128, 24, 4])
            kb.dma("sp", cwt[:], cw[i], reads=[cw], writes=[cwt])
            xin = [kb.sbuf(st, "hxi%d" % j, [128, T]) for j in range(2)]
            uo = [kb.sbuf(st, "huo%d" % j, [128, T]) for j in range(2)]
            for c in range(24):
                x = xin[c % 2]
                u = uo[c % 2]
                kb.dma("pool", x[:], PH[:, c, :], reads=[PH], writes=[x])
                kb.actf(u[:], x[:], AF.Identity, [x, cwt], [u], scale=cwt[:, c, 1:2], bias=cwt[:, c, 3:4])
                for (a, b_) in ((0, L), (L, T)):
                    kb.stt("dve", u[:, a + 1:b_], x[:, a:b_ - 1], cwt[:, c, 0:1], u[:, a + 1:b_], ALU.mult, ALU.add, [x, cwt, u], [u])
                    kb.stt("pool", u[:, a:b_ - 1], x[:, a + 1:b_], cwt[:, c, 2:3], u[:, a:b_ - 1], ALU.mult, ALU.add, [x, cwt, u], [u])
                kb.dma("sp", UT[:, c, :], u[:], reads=[u], writes=[UT])
        for seg, (Ls, tok0) in enumerate(((L, 0), (LC, L))):
            self.hy_filters(i, Ls, seg)
            self.hy_tok(seg, Ls, tok0, UT, 16, self.ZTOK[seg])
            self.hy_fwd(seg, Ls, [(self.ZTOKv[seg], 0)], (self.SA[seg], self.SB[seg]), self.YR[seg], self.YI[seg], 0, 1.0)
            self.hy_inv(seg, Ls, tok0, 0, UT, 16, 0, self.ZT1, 0, 0)
            self.hy_tok(seg, Ls, tok0, self.ZT1, 0, self.ZTOK[seg])
            self.hy_fwd(seg, Ls, [(self.ZTOKv[seg], 0)], (self.SA[seg], self.SB[seg]), self.YR[seg], self.YI[seg], 1, 1.0)
            self.hy_inv(seg, Ls, tok0, 1, self.ZT1, 0, 8, CATT, 0, 1)

    def build(self, x_name="x"):
        nc = self.nc
        with ExitStack() as st:
            kb = KB(nc, st)
            self.kb = kb
            plans_na, pats = na_structure()
            nuq = pats.shape[0]
            x_in = self.inp("x", [T, D])
            cT = self.inp("cT", [128, 16, 2])
            self.inp("ada_w", [DEPTH, D, 6 * D])
            self.inp("ada_b", [DEPTH, 6 * D])
            self.inp("ln_g", [DEPTH, 2, D])
            self.inp("ln_b", [DEPTH, 2, D])
            self.inp("ev_w_in", [2, D, 4608])
            self.inp("ev_w_out", [2, D, D])
            self.inp("od_w_in", [2, D, 3 * D])
            self.inp("od_w_out", [2, D, D])
            self.inp("hy_conv", [2, 128, 24, 4])
            self.inp("hy_biasc", [2, 128, 16])
            self.inp("hy_f_w1", [2, 33, 64])
            self.inp("hy_f_w2", [2, 2, 64, 64])
            self.inp("hy_f_w3", [2, 64, 4096])
            self.inp("hy_fcol", [2, 64, 4])
            self.inp("swa_sink", [2, 8])
            self.inp("moe_w_router", [DEPTH, D, NEXP])
            self.inp("moe_w1", [DEPTH, NEXP, D, EFF])
            self.inp("moe_w3", [DEPTH, NEXP, D, EFF])
            self.inp("moe_w2", [DEPTH, NEXP, EFF, D])
            ident_in = self.inp("ident", [128, 128])
            ustr_in = self.inp("ustrict", [128, 128])
            iota_in = self.inp("iota", [128, 256])
            sel16_in = self.inp("sel16", [16, 16, 128])
            tokidx_in = self.inp("tokidx", [128, NT])
            ropeR_in = self.inp("ropeR", [128, 128])
            ropeC_in = self.inp("ropeC", [128, L])
            ropeS_in = self.inp("ropeS", [128, L])
            for seg, Ls in enumerate((L, LC)):
                self.inp("zembT%d" % seg, [33, Ls])
                self.inp("decay%d" % seg, [Ls, 1024])
                for nm in ("dftC", "dftS", "dftCT", "dftST"):
                    self.inp("%s%d" % (nm, seg), [Ls, Ls])
            evbt = self.inp("evbt", [1, 128, 2, 128])
            nab = self.inp("nab", [2, 16, 128, nuq, 128])
            self.M = [self.scratch("M%d" % l, [2, 6 * D]) for l in range(DEPTH)]
            HT = self.scratch("HT", [128, 16, T])
            PH = self.scratch("PH", [128, 24, T])
            QT = self.scratch("QT", [128, 16, T])
            KT = self.scratch("KT", [128, 16, T])
            V = self.scratch("V", [T, D])
            CATT = self.scratch("CATT", [128, 16, T])
            Y = self.scratch("Y", [T, D])
            XA = self.scratch("XA", [T, D])
            XB = self.scratch("XB", [T, D])
            H2 = self.scratch("H2", [T, D])
            Fo = self.scratch("Fo", [T, D])
            self.UT = self.scratch("UT", [128, 24, T])
            self.ZT1 = self.scratch("ZT1", [128, 8, T])
            self.ZTOKv, self.ZTOK, self.KP, self.KM, self.SA, self.SB, self.YR, self.YI = [], [], [], [], [], [], [], []
            self.YS, self.SELT = [], []
            for seg, Ls in enumerate((L, LC)):
                z3 = self.scratch("ZTOK%d" % seg, [1, Ls, 1024])
                z2 = Tile("ZTOK2_%d" % seg, z3.t[0], "dram")
                z2.trk = z3.trk
                self.ZTOKv.append(z3)
                self.ZTOK.append(z2)
                for nm, lst in (("KP", self.KP), ("KM", self.KM), ("SA", self.SA), ("SB", self.SB), ("YR", self.YR), ("YI", self.YI)):
                    lst.append(self.scratch("%s%d" % (nm, seg), [2, Ls, 1024]))
                cap = 2 * Ls // NEXP
                nsc_ = (cap + 127) // 128
                self.SELT.append(self.scratch("SELT%d" % seg, [Ls // 128, min(cap, 128), NEXP * nsc_, 128]))
            self.YSm = self.scratch("YSm", [NEXP, 288, D])
            out = self.kb.dram("out", [T, D], F32, kind="ExternalOutput")
            self.ident = kb.sbuf(st, "ident", [128, 128])
            self.ones = kb.sbuf(st, "ones", [128, 128])
            self.ustrict = kb.sbuf(st, "ustrict", [128, 128])
            self.iota = kb.sbuf(st, "iota", [128, 256])
            self.tokidx = kb.sbuf(st, "tokidx", [128, NT])
            kb.dma("sp", self.tokidx[:], tokidx_in[:], reads=[tokidx_in], writes=[self.tokidx])
            self.sel16 = kb.sbuf(st, "sel16", [16, 16, 128])
            kb.dma("sp", self.sel16[:], sel16_in[:], reads=[sel16_in], writes=[self.sel16])
            self.sT = kb.sbuf(st, "sT", [128, 16, 2])
            self.mcol = kb.sbuf(st, "mcol", [128, 2, 96])
            self.mcol1 = kb.sbuf(st, "mcol1", [128, 2, 96])
            self.hybias = kb.sbuf(st, "hybias", [128, 16])
            AFF = kb.sbuf(st, "AFF", [128, NT, NEXP])
            kb.dma("sp", self.ident[:], ident_in[:], reads=[ident_in], writes=[self.ident])
            kb.dma("sp", self.ustrict[:], ustr_in[:], reads=[ustr_in], writes=[self.ustrict])
            kb.dma("sp", self.iota[:], iota_in[:], reads=[iota_in], writes=[self.iota])
            kb.op("dve", lambda e: e.memset(self.ones[:], 1.0), writes=[self.ones])
            kb.dma("sp", self.sT[:], cT[:], reads=[cT], writes=[self.sT])
            kb.actf(self.sT[:], self.sT[:], AF.Silu, [self.sT], [self.sT])

            plans_ev = []
            for n in range(16):
                p = []
                if n >= 1:
                    p.append((n - 1, 0))
                p.append((n, None))
                if n <= 14:
                    p.append((n + 1, 1))
                p += [(16, None), (17, None)]
                plans_ev.append(p)
            plans_ev += [[(16, None), (17, None)]] * 2

            X = x_in
            stop = self.stop_after
            for l in range(self.l0, self.l0 + self.nlayers):
                i = l // 2
                last = (l == self.l0 + self.nlayers - 1)
                self.stage_mod(l)
                self.stage_modT(X, HT, 0, 1)
                if stop == "modT":
                    break
                if l % 2 == 0:
                    self.stage_proj(HT, self.inputs["ev_w_in"], i,
                                    [("fm", 0, 3072, PH), ("fm", 3072, 1024, QT), ("fm", 4096, 256, KT), ("tm", 4352, 256, V)])
                    if stop == "proj":
                        break
                    with ExitStack() as st2:
                        self.ropeR = kb.sbuf(st2, "ropeR", [128, 128], F32R)
                        self.ropeC = kb.sbuf(st2, "ropeC", [128, L])
                        self.ropeS = kb.sbuf(st2, "ropeS", [128, L])
                        kb.dma("pool", self.ropeR[:], ropeR_in[:], reads=[ropeR_in], writes=[self.ropeR])
                        kb.dma("sp", self.ropeC[:], ropeC_in[:], reads=[ropeC_in], writes=[self.ropeC])
                        kb.dma("pool", self.ropeS[:], ropeS_in[:], reads=[ropeS_in], writes=[self.ropeS])
                        self.stage_attn(QT, KT, V, CATT, 8, 8, 4, plans_ev, evbt, 2, False, True, i)
                    if stop == "attn":
                        break
                    self.stage_hyena(i, PH, CATT)
                    if stop == "hyena":
                        break
                    Wout = self.inputs["ev_w_out"]
                else:
                    self.stage_proj(HT, self.inputs["od_w_in"], i,
                                    [("fm", 0, 2048, QT), ("fm", 2048, 2048, KT), ("tm", 4096, 2048, V)])
                    if stop == "proj":
                        break
                    nabl = Tile("nab%d" % i, nab.t[i], "dram")
                    nabl.trk = nab.trk
                    self.stage_attn(QT, KT, V, CATT, 0, 16, 1, plans_na, nabl, nuq, True, False, None)
                    if stop == "attn":
                        break
                    Wout = self.inputs["od_w_out"]
                self.stage_proj(CATT, Wout, i, [("tm", 0, 2048, Y)])
                if stop == "oproj":
                    break
                self.stage_ln(l, 0, 2, X, Y, XA, H2=H2, AFF=AFF)
                if stop == "ln1":
                    break
                self.stage_moe(l, H2, AFF, Fo)
                if stop == "moe":
                    break
                Xn = out if last else XB
                self.stage_ln(l, 1, 5, XA, Fo, Xn)
                X = Xn
            outs = [t for t in [HT, PH, QT, KT, V, CATT, Y, XA, XB, H2, Fo, self.UT, self.ZT1] + self.M + self.KP + self.KM + self.SA
                    + self.SB + self.YR + self.YI + self.SELT + self.ZTOKv if t.name in self.dbg] + [out]
            kb.wait_all("sp", outs)
            kb.wait_all("pool", outs)
            kb.finalize()
            print("ninst", kb.ninst, "dsems", kb.ndsem)
        return nc


def na_structure():
    col = np.arange(64)
    cs = np.clip(col - 8, 0, 48)
    col_ok = (col[None, :] >= cs[:, None]) & (col[None, :] < cs[:, None] + 16)
    dc = np.clip(col[None, :] - col[:, None] + 15, 0, 30)
    uniq = {}
    pats = []
    plans = []
    for n in range(16):
        full = -np.ones((128, 2048), np.int64)
        for rr in range(2):
            r = 2 * n + rr
            rs = min(max(r - 4, 0), 24)
            for kr in range(8):
                krow = rs + kr
                dr = rs - r + kr + 7
                full[rr * 64:(rr + 1) * 64, krow * 64:(krow + 1) * 64] = np.where(col_ok, dr * 31 + dc, -1)
        plan = []
        for j in range(16):
            t = full[:, j * 128:(j + 1) * 128]
            if (t >= 0).any():
                key = t.tobytes()
                if key not in uniq:
                    uniq[key] = len(pats)
                    pats.append(np.ascontiguousarray(t.T))
                plan.append((j, uniq[key]))
        plan += [(16, None), (17, None)]
        plans.append(plan)
    plans += [[(16, None), (17, None)]] * 2
    return plans, np.stack(pats)


_CONST = {}


def host_consts():
    if _CONST:
        return _CONST
    f32 = np.float32
    c = _CONST
    c["ident"] = np.eye(128, dtype=f32)
    c["ustrict"] = np.triu(np.ones((128, 128), f32), 1)
    s16 = np.zeros((16, 16, 128), f32)
    for e in range(16):
        s16[e, e, :] = 1.0
    c["sel16"] = s16
    c["tokidx"] = (np.arange(NT, dtype=f32)[None, :] * 128 + np.arange(128, dtype=f32)[:, None]).astype(f32)
    c["iota"] = np.tile(np.arange(256, dtype=f32)[None, :], (128, 1))
    R = np.zeros((128, 128), f32)
    for m in range(128):
        if (m % 64) < 32:
            R[m + 32, m] = -1.0
        else:
            R[m - 32, m] = 1.0
    c["ropeR"] = R
    t = np.arange(L)
    inv = (10000.0 ** (-2.0 * np.arange(32, dtype=f32) / 64)).astype(f32)
    C = np.zeros((128, L), f32)
    S = np.zeros((128, L), f32)
    for d in range(128):
        pos = (t // GRID_W) if d < 64 else (t % GRID_W)
        ang = pos.astype(f32) * inv[d % 32]
        C[d] = np.cos(ang)
        S[d] = np.sin(ang)
    c["ropeC"] = C
    c["ropeS"] = S
    for seg, Ls in enumerate((L, LC)):
        tt = np.linspace(0.0, 1.0, Ls, dtype=f32)[:, None]
        w = (f32(2.0 * math.pi / Ls) * np.arange(Ls, dtype=f32))[:, None]
        f = np.linspace(1e-4, 15, 16, dtype=f32)[None, :]
        z = np.concatenate([tt, np.cos(f * w), -np.sin(f * w)], axis=-1).astype(f32)
        c["zembT%d" % seg] = np.ascontiguousarray(z.T)
        deltas = np.abs(np.linspace(math.log(1e-2) / 1.5, math.log(1e-2) / 0.3, 1024, dtype=f32))
        c["decay%d" % seg] = np.exp(-tt * deltas[None, :]).astype(f32)
        N2 = 2 * Ls
        tf = np.arange(Ls, dtype=np.float64)
        ph = np.pi * np.outer(tf, 2 * tf + 1) / N2
        Cm = np.cos(ph).astype(f32)
        Sm = (-np.sin(ph)).astype(f32)
        c["dftC%d" % seg] = Cm
        c["dftS%d" % seg] = Sm
        c["dftCT%d" % seg] = np.ascontiguousarray(Cm.T)
        c["dftST%d" % seg] = np.ascontiguousarray(Sm.T)
    a = np.arange(128)
    triA = np.where(a[None, :] <= a[:, None], 0.0, NEG).astype(f32)
    triB = np.where(a[:, None] <= a[None, :], 0.0, NEG).astype(f32)
    c["evbt"] = np.ascontiguousarray(np.stack([triA, triB], axis=1)[None])
    return c


def host_inputs(b, x, c, ctx, c_ctx, ada_w, ada_b, ln_g, ln_b, ev_w_in, ev_w_out, hy_conv_w, hy_conv_b,
                hy_f_w1, hy_f_b1, hy_f_w2, hy_f_b2, hy_f_w3, hy_f_freq, hy_bias, swa_sink,
                od_w_in, od_w_out, na_rpb, moe_w_router, moe_w1, moe_w3, moe_w2):
    f32 = np.float32
    m = dict(host_consts())
    m["x"] = np.ascontiguousarray(np.concatenate([x[b], ctx[b]], axis=0))
    cc = np.stack([c[b], c_ctx], axis=-1)
    m["cT"] = np.ascontiguousarray(cc.reshape(16, 128, 2).transpose(1, 0, 2))
    for k, v in (("ada_w", ada_w), ("ada_b", ada_b), ("ln_g", ln_g), ("ln_b", ln_b), ("ev_w_in", ev_w_in), ("ev_w_out", ev_w_out),
                 ("od_w_in", od_w_in), ("od_w_out", od_w_out), ("hy_f_w1", hy_f_w1), ("hy_f_w2", hy_f_w2), ("hy_f_w3", hy_f_w3),
                 ("swa_sink", swa_sink), ("moe_w_router", moe_w_router), ("moe_w1", moe_w1), ("moe_w3", moe_w3), ("moe_w2", moe_w2)):
        m[k] = v
    cw = np.concatenate([hy_conv_w, hy_conv_b[:, None, :]], axis=1)
    m["hy_conv"] = np.ascontiguousarray(cw.reshape(2, 4, 24, 128).transpose(0, 3, 2, 1))
    m["hy_biasc"] = np.ascontiguousarray(hy_bias.reshape(2, 2, 8, 128).transpose(0, 3, 1, 2).reshape(2, 128, 16))
    m["hy_fcol"] = np.ascontiguousarray(np.stack([hy_f_freq, hy_f_b1, hy_f_b2[:, 0], hy_f_b2[:, 1]], axis=-1))
    plans, pats = na_structure()
    flat = na_rpb.reshape(2, 16, -1)
    g = flat[:, :, np.maximum(pats, 0)]
    g = np.where(pats[None, None] >= 0, g, f32(NEG)).astype(f32)
    m["nab"] = np.ascontiguousarray(g.transpose(0, 1, 3, 2, 4))
    return m


_NC_CACHE = {}


def kernel(**inputs):
    inputs = {k: np.asarray(v) for k, v in inputs.items()}
    if "nc" not in _NC_CACHE:
        _NC_CACHE["nc"] = Prog().build()
    nc = _NC_CACHE["nc"]
    B = inputs["x"].shape[0]
    in_maps = [host_inputs(b, **inputs) for b in range(B)]
    res = run_bass_kernel_spmd(nc, in_maps, core_ids=list(range(B)))
    out = np.stack([np.asarray(r["out"])[:L] for r in res.results], axis=0)
    return out.astype(np.float32)
```

```python
import math
from contextlib import ExitStack
import numpy as np
import concourse.bass as bass
import concourse.mybir as mybir
from concourse.bass_utils import run_bass_kernel_spmd

F32 = mybir.dt.float32
F32R = mybir.dt.float32r
FAST_MM = True


def f32(ap):
    return ap.bitcast(F32) if ap.dtype == F32R else ap
I32 = mybir.dt.int32
AF = mybir.ActivationFunctionType
ALU = mybir.AluOpType
AX = mybir.AxisListType

D = 2048
L = 2048
LC = 256
T = L + LC
NT = T // 128
DEPTH = 4
GRID_W = 64
HY_DIM = 1024
NEXP = 16
EFF = 1024
DN_ALPHA = (2 * DEPTH) ** 0.25
LN_EPS = 1e-5
NEG = -30000.0


class Trk:
    __slots__ = ("name", "lw", "rd", "dsem")

    def __init__(self, name):
        self.name = name
        self.lw = None
        self.rd = []
        self.dsem = None


class Tile:
    def __init__(self, name, t, space):
        self.name = name
        self.t = t
        self.space = space
        self.trk = Trk(name)
        if space == "dram":
            self.trk.lw = {}
            self.trk.rd = {}

    def __getitem__(self, idx):
        return self.t[idx]


class KB:
    ENG = ("pe", "dve", "act", "pool", "sp")

    def __init__(self, nc, stack):
        self.nc = nc
        self.stack = stack
        self.prog = {e: [] for e in self.ENG}
        self.sems = {}
        self.cnt = {}
        self.seen = {e: {} for e in self.ENG}
        for e in ("pe", "dve", "act", "pool"):
            self._mksem("E_" + e)
        self.free_dsems = []
        self.pending = {}
        self.ndsem = 0
        self.ninst = 0
        self.rr = 0

    def _mksem(self, key):
        h = self.stack.enter_context(self.nc.semaphore(key))
        self.sems[key] = h
        self.cnt[key] = 0
        return key

    def _dsem_for(self, trk):
        if trk.dsem is None:
            if self.free_dsems:
                trk.dsem = self.free_dsems.pop()
            else:
                self.ndsem += 1
                trk.dsem = self._mksem("D%d" % self.ndsem)
        return trk.dsem

    def sbuf(self, st, name, shape, dtype=F32):
        self.uid = getattr(self, "uid", 0) + 1
        t = st.enter_context(self.nc.sbuf_tensor("s%d_%s" % (self.uid, name), list(shape), dtype))
        tl = Tile(name, t, "sbuf")
        tl.trk.rd = list(self.pending.items())
        st.callback(self._release, tl)
        return tl

    def psum(self, st, name, shape, dtype=F32):
        self.uid = getattr(self, "uid", 0) + 1
        t = st.enter_context(self.nc.psum_tensor("p%d_%s" % (self.uid, name), list(shape), dtype))
        tl = Tile(name, t, "psum")
        tl.trk.rd = list(self.pending.items())
        st.callback(self._release, tl)
        return tl

    def _release(self, tile):
        for d in [tile.trk.lw] + list(tile.trk.rd):
            if d is not None and d[1] > self.pending.get(d[0], 0):
                self.pending[d[0]] = d[1]
        if tile.trk.dsem is not None:
            self.free_dsems.append(tile.trk.dsem)
            tile.trk.dsem = None

    def dram(self, name, shape, dtype=F32, kind="Internal"):
        t = self.nc.dram_tensor(name, list(shape), dtype, kind=kind)
        return Tile(name, t.ap(), "dram")

    def _waits(self, e, reads, writes, skip_self=False):
        deps = {}

        def add(d):
            if d is None:
                return
            k, c = d
            if c > deps.get(k, 0):
                deps[k] = c
        for r in reads:
            if r.space == "dram":
                for d in r.trk.lw.items():
                    add(d)
            else:
                add(r.trk.lw)
        for w in writes:
            if w.space == "dram":
                for d in w.trk.rd.items():
                    add(d)
                if e not in ("sp", "pool"):
                    for d in w.trk.lw.items():
                        add(d)
            else:
                add(w.trk.lw)
                for d in w.trk.rd:
                    add(d)
        out = []
        seen = self.seen[e]
        for k, c in deps.items():
            if skip_self and k == "E_" + e:
                continue
            if k[0] == "D":
                c = self.cnt[k]
            if seen.get(k, 0) >= c:
                continue
            seen[k] = c
            out.append((k, c))
        return out

    def _commit(self, mark, reads, writes):
        for r in reads:
            if r.space == "dram":
                if mark[1] > r.trk.rd.get(mark[0], 0):
                    r.trk.rd[mark[0]] = mark[1]
                continue
            r.trk.rd.append(mark)
            if len(r.trk.rd) > 64:
                mx = {}
                for k, c in r.trk.rd:
                    if c > mx.get(k, 0):
                        mx[k] = c
                r.trk.rd = list(mx.items())
        for w in writes:
            if w.space == "dram":
                if mark[1] > w.trk.lw.get(mark[0], 0):
                    w.trk.lw[mark[0]] = mark[1]
                continue
            w.trk.lw = mark
            w.trk.rd = []

    def op(self, e, fn, reads=(), writes=()):
        waits = self._waits(e, reads, writes, skip_self=(e == "pe"))
        key = "E_" + e
        self.cnt[key] += 1
        c = self.cnt[key]
        sems = self.sems

        def emit(eng, waits=waits, fn=fn, key=key):
            for k, v in waits:
                eng.wait_ge(sems[k], v)
            fn(eng).then_inc(sems[key], 1)
        self.prog[e].append(emit)
        self._commit((key, c), reads, writes)
        self.ninst += 1

    def dma(self, q, out, in_, reads=(), writes=(), **kw):
        sb = [t for t in list(writes) + list(reads) if t.space != "dram"]
        assert len(sb) >= 1
        key = self._dsem_for(sb[0].trk)
        waits = self._waits(q, reads, writes)
        self.cnt[key] += 16
        c = self.cnt[key]
        sems = self.sems

        def emit(eng, waits=waits, key=key, out=out, in_=in_, kw=kw):
            for k, v in waits:
                eng.wait_ge(sems[k], v)
            eng.dma_start(out=out, in_=in_, **kw).then_inc(sems[key], 16)
        self.prog[q].append(emit)
        self._commit((key, c), reads, writes)
        self.ninst += 1

    def gather(self, dst, n, src, idx):
        key = self._dsem_for(dst.trk)
        waits = self._waits("pool", [idx, src], [dst])
        self.cnt[key] += 16
        c = self.cnt[key]
        sems = self.sems

        def emit(eng, waits=waits, key=key):
            for k, v in waits:
                eng.wait_ge(sems[k], v)
            eng.indirect_dma_start(out=dst[0:n, :], out_offset=None, in_=src[:, :],
                                   in_offset=bass.IndirectOffsetOnAxis(ap=idx[0:n, :], axis=0)).then_inc(sems[key], 16)
        self.prog["pool"].append(emit)
        self._commit((key, c), [idx, src], [dst])
        self.ninst += 1

    def q(self):
        self.rr += 1
        return ("sp", "pool")[self.rr % 2]

    def wait_all(self, e, tiles):
        waits = self._waits(e, tiles, ())
        sems = self.sems

        def emit(eng, waits=waits):
            for k, v in waits:
                eng.wait_ge(sems[k], v)
        self.prog[e].append(emit)

    def finalize(self):
        nc = self.nc
        prog = self.prog
        with nc.Block() as block:
            @block.sync
            def _(eng):
                for f in prog["sp"]:
                    f(eng)

            @block.tensor
            def _(eng):
                for f in prog["pe"]:
                    f(eng)

            @block.vector
            def _(eng):
                for f in prog["dve"]:
                    f(eng)

            @block.scalar
            def _(eng):
                for f in prog["act"]:
                    f(eng)

            @block.gpsimd
            def _(eng):
                for f in prog["pool"]:
                    f(eng)

    def mm(self, out, lhsT, rhs, start, stop, reads, writes):
        fast = (FAST_MM and lhsT.dtype == F32R and rhs.dtype == F32R and tuple(lhsT.shape) == (128, 128)
                and len(rhs.shape) == 2 and rhs.shape[1] % 2 == 0 and rhs.shape[1] >= 32)
        if not fast:
            lhsT = f32(lhsT)
            rhs = f32(rhs)
        self.op("pe", lambda e: e.matmul(out, lhsT=lhsT, rhs=rhs, start=start, stop=stop), reads=reads, writes=writes)

    def tr(self, out, in_, ident, reads, writes):
        in_ = f32(in_)
        self.op("pe", lambda e: e.transpose(out, in_, ident), reads=reads, writes=writes)

    def copy(self, e, out, in_, reads, writes):
        if e == "act":
            self.op("act", lambda g: g.activation(out=out, in_=in_, func=AF.Copy), reads=reads, writes=writes)
        else:
            self.op(e, lambda g: g.tensor_copy(out=out, in_=in_), reads=reads, writes=writes)

    def ts(self, e, out, in0, s1, s2, op0, op1, reads, writes):
        if s2 is None:
            self.op(e, lambda g: g.tensor_scalar(out=out, in0=in0, scalar1=s1, scalar2=None, op0=op0), reads=reads, writes=writes)
        else:
            self.op(e, lambda g: g.tensor_scalar(out=out, in0=in0, scalar1=s1, scalar2=s2, op0=op0, op1=op1), reads=reads, writes=writes)

    def tt(self, e, out, in0, in1, op, reads, writes):
        self.op(e, lambda g: g.tensor_tensor(out=out, in0=in0, in1=in1, op=op), reads=reads, writes=writes)

    def stt(self, e, out, in0, scalar, in1, op0, op1, reads, writes):
        e = "dve"
        self.op(e, lambda g: g.scalar_tensor_tensor(out=out, in0=in0, scalar=scalar, in1=in1, op0=op0, op1=op1),
                reads=reads, writes=writes)

    def actf(self, out, in_, func, reads, writes, scale=None, bias=None, accum_out=None):
        kw = {}
        if scale is not None:
            kw["scale"] = scale
        if bias is not None:
            kw["bias"] = bias
        if accum_out is not None:
            kw["accum_out"] = accum_out
        self.op("act", lambda g: g.activation(out=out, in_=in_, func=func, **kw), reads=reads, writes=writes)


class Prog:
    def __init__(self, dbg=(), nlayers=DEPTH, stop_after=None, l0=0):
        self.dbg = set(dbg)
        self.l0 = l0
        self.nlayers = nlayers
        self.stop_after = stop_after
        self.nc = bass.Bass("TRN2", target_bir_lowering=False)
        self.inputs = {}

    def inp(self, name, shape):
        t = self.nc.dram_tensor(name, list(shape), F32, kind="ExternalInput")
        tl = Tile(name, t.ap(), "dram")
        self.inputs[name] = tl
        return tl

    def scratch(self, name, shape):
        kind = "ExternalOutput" if name in self.dbg else "Internal"
        return self.kb.dram(name, shape, F32, kind=kind)

    def stage_mod(self, l):
        kb = self.kb
        Ml = self.M[l]
        with ExitStack() as st:
            wb = [kb.sbuf(st, "modw%d" % i, [128, 16, 512]) for i in range(2)]
            br = [kb.sbuf(st, "modb%d" % i, [1, 512]) for i in range(2)]
            ms = [kb.sbuf(st, "mods%d" % i, [2, 512]) for i in range(2)]
            ps = [kb.psum(st, "modp%d" % i, [2, 512]) for i in range(2)]
            aw = self.inputs["ada_w"]
            ab = self.inputs["ada_b"]
            for cb in range(24):
                w = wb[cb % 2]
                b = br[cb % 2]
                p = ps[cb % 2]
                m = ms[cb % 2]
                cs = slice(cb * 512, (cb + 1) * 512)
                kb.dma("pool", w[:], aw[l, :, cs].rearrange("(k p) n -> p k n", p=128), reads=[aw], writes=[w])
                kb.dma("pool", b[:], ab[l:l + 1, cs], reads=[ab], writes=[b])
                for k in range(16):
                    kb.mm(p[:], self.sT[:, k, :], w[:, k, :], k == 0, False, [self.sT, w], [p])
                kb.mm(p[:], self.ones[0:1, 0:2], b[:], False, True, [self.ones, b], [p])
                kb.copy("dve", m[:], p[:], [p], [m])
                kb.dma("sp", Ml[:, cs], m[:], reads=[m], writes=[Ml])
            mr = kb.sbuf(st, "modr", [96, 128])
            pc = kb.psum(st, "modpc", [128, 96])
            for r in range(2):
                kb.dma("sp", mr[:], Ml[r, :].rearrange("(c p) -> c p", p=128), reads=[Ml], writes=[mr])
                kb.tr(pc[:], mr[:], self.ident[0:96, 0:96], [mr, self.ident], [pc])
                kb.copy("dve", self.mcol[:, r, :], pc[:], [pc], [self.mcol])
                kb.ts("dve", self.mcol1[:, r, :], pc[:], 1.0, None, ALU.add, None, [pc], [self.mcol1])

    def stage_modT(self, X, HT, s_shift, s_scale):
        kb = self.kb
        with ExitStack() as st:
            xb = [kb.sbuf(st, "mtx%d" % i, [128, D]) for i in range(2)]
            hb = [kb.sbuf(st, "mth%d" % i, [128, 16, 128]) for i in range(2)]
            ps = [kb.psum(st, "mtp%d" % i, [128, 4, 128]) for i in range(2)]
            for i in range(NT):
                r = 0 if i < 16 else 1
                xt = xb[i % 2]
                ht = hb[i % 2]
                kb.dma("sp", xt[:], X[i * 128:(i + 1) * 128, :], reads=[X], writes=[xt])
                for g in range(4):
                    p = ps[g % 2]
                    for j in range(4):
                        c = g * 4 + j
                        kb.tr(p[:, j, :], xt[:, c * 128:(c + 1) * 128], self.ident[:], [xt, self.ident], [p])
                    for j in range(4):
                        c = g * 4 + j
                        sc = self.mcol1[:, r, s_scale * 16 + c:s_scale * 16 + c + 1]
                        sh = self.mcol[:, r, s_shift * 16 + c:s_shift * 16 + c + 1]
                        if j % 2 == 0:
                            kb.actf(ht[:, c, :], p[:, j, :], AF.Identity, [p, self.mcol, self.mcol1], [ht], scale=sc, bias=sh)
                        else:
                            kb.ts("dve", ht[:, c, :], p[:, j, :], sc, sh, ALU.mult, ALU.add, [p, self.mcol, self.mcol1], [ht])
                kb.dma("pool", HT[:, :, i * 128:(i + 1) * 128], ht[:], reads=[ht], writes=[HT])

    def stage_proj(self, HT, W, wsel, specs):
        kb = self.kb
        TB = T // 2
        mv = [(0, 512), (512, 512), (1024, 128)]
        with ExitStack() as st:
            hblk = kb.sbuf(st, "pjh", [128, 16, TB], F32R)
            wpan = [kb.sbuf(st, "pjw%d" % i, [128, 16, 128], F32R) for i in range(2)]
            wp2 = [kb.sbuf(st, "pjv%d" % i, [128, 16, 512], F32R) for i in range(2)]
            ot = [kb.sbuf(st, "pjo%d" % i, [128, TB]) for i in range(2)]
            ot2 = [kb.sbuf(st, "pjq%d" % i, [128, 512]) for i in range(2)]
            ps = [kb.psum(st, "pjp%d" % i, [128, 512]) for i in range(6)]
            n = 0
            n2 = 0
            for tb in range(2):
                t0 = tb * TB
                kb.dma("pool", hblk[:, 0:8, :], HT[:, 0:8, t0:t0 + TB], reads=[HT], writes=[hblk])
                kb.dma("pool", hblk[:, 8:16, :], HT[:, 8:16, t0:t0 + TB], reads=[HT], writes=[hblk])
                for kind, col0, ncols, OUT in specs:
                    if kind == "fm":
                        for cc in range(ncols // 128):
                            wp = wpan[n % 2]
                            o = ot[n % 2]
                            pp = ps[(n % 2) * 3:(n % 2) * 3 + 3]
                            c0 = col0 + cc * 128
                            kb.dma("pool", wp[:], W[wsel, :, c0:c0 + 128].rearrange("(k p) n -> p k n", p=128),
                                   reads=[W], writes=[wp])
                            for k in range(16):
                                for mi, (m0, mn) in enumerate(mv):
                                    kb.mm(pp[mi][:, :mn], wp[:, k, :], hblk[:, k, m0:m0 + mn], k == 0, k == 15,
                                          [wp, hblk], [pp[mi]])
                            for mi, (m0, mn) in enumerate(mv):
                                kb.copy("act" if mi != 1 else "dve", o[:, m0:m0 + mn], pp[mi][:, :mn], [pp[mi]], [o])
                            kb.dma("sp", OUT[:, cc, t0:t0 + TB], o[:], reads=[o], writes=[OUT])
                            n += 1
                    else:
                        PW = 512 if ncols % 512 == 0 else 256
                        for cb in range(ncols // PW):
                            wp = wp2[n2 % 2]
                            c0 = col0 + cb * PW
                            kb.dma("pool", wp[:, :, 0:PW], W[wsel, :, c0:c0 + PW].rearrange("(k p) n -> p k n", p=128),
                                   reads=[W], writes=[wp])
                            n2 += 1
                            for ti in range(TB // 128):
                                p = ps[n % 6]
                                o = ot2[n % 2]
                                for k in range(16):
                                    kb.mm(p[:, :PW], hblk[:, k, ti * 128:(ti + 1) * 128], wp[:, k, 0:PW], k == 0, k == 15,
                                          [hblk, wp], [p])
                                kb.copy("act" if n % 2 == 0 else "dve", o[:, 0:PW], p[:, :PW], [p], [o])
                                kb.dma("sp", OUT[t0 + ti * 128:t0 + (ti + 1) * 128, cb * PW:(cb + 1) * PW], o[:, 0:PW],
                                       reads=[o], writes=[OUT])
                                n += 1

    def stage_attn(self, QT, KT, V, CATT, cat_c0, nheads, gq, plans, BT, nuniq, bt_per_head, rope, sink_l):
        kb = self.kb
        scale = 128.0 ** -0.5
        with ExitStack() as st:
            qh = kb.sbuf(st, "aq", [128, T], F32R)
            kh = kb.sbuf(st, "ak", [128, T], F32R)
            va = kb.sbuf(st, "av", [128, NT, 132], F32R)
            oth = kb.sbuf(st, "ao", [128, T])
            bt = kb.sbuf(st, "abt", [128, nuniq, 128], F32R)
            identR = kb.sbuf(st, "aidr", [128, 128], F32R)
            kb.copy("dve", identR[:], self.ident[:], [self.ident], [identR])
            et = [kb.sbuf(st, "aet%d" % i, [128, 7, 128], F32R) for i in range(2)]
            osb = [kb.sbuf(st, "aos%d" % i, [128, 128]) for i in range(2)]
            rd = [kb.sbuf(st, "ard%d" % i, [128, 2]) for i in range(2)]
            psS = [[kb.psum(st, "aps%d%d" % (i, j), [128, 512]) for j in range(2)] for i in range(2)]
            psO = [kb.psum(st, "apo%d" % i, [128, 512]) for i in range(2)]
            psT = kb.psum(st, "apt", [128, 128])
            psR = kb.psum(st, "apr", [128, 512])
            if rope:
                rt1 = kb.sbuf(st, "art1", [128, 512])
                rt2 = kb.sbuf(st, "art2", [128, 512])
            if sink_l is not None:
                esk = kb.sbuf(st, "aesk", [128, 8])
                sk = self.inputs["swa_sink"]
                kb.dma("sp", esk[:], sk[sink_l:sink_l + 1, :].to_broadcast([128, 8]), reads=[sk], writes=[esk])
                kb.actf(esk[:], esk[:], AF.Exp, [esk], [esk])
            kb.copy("pool", va[:, :, 128:132], self.ones[:, 0:NT * 4].rearrange("p (a b) -> p a b", b=4), [self.ones], [va])
            def load_bt(hh):
                kb.dma("pool", bt[:], BT[hh], reads=[BT], writes=[bt])
                kb.ts("dve", bt[:].rearrange("p a b -> p (a b)"), f32(bt[:].rearrange("p a b -> p (a b)")), float(1.0 / scale), None,
                      ALU.mult, None, [bt], [bt])
            if not bt_per_head:
                load_bt(0)

            def do_rope(tl):
                for blk in range(4):
                    sl = slice(blk * 512, (blk + 1) * 512)
                    kb.mm(psR[:], self.ropeR[:], tl[:, sl], True, True, [self.ropeR, tl], [psR])
                    kb.tt("pool", rt1[:], f32(tl[:, sl]), self.ropeC[:, sl], ALU.mult, [tl, self.ropeC], [rt1])
                    kb.tt("dve", rt2[:], psR[:], self.ropeS[:, sl], ALU.mult, [psR, self.ropeS], [rt2])
                    kb.tt("dve", (tl[:, sl]), rt1[:], rt2[:], ALU.add, [rt1, rt2], [tl])

            cur_kv = -1
            it = 0
            for h in range(nheads):
                kv = h // gq
                kb.dma("pool", qh[:, 0:T // 2], QT[:, h, 0:T // 2], reads=[QT], writes=[qh])
                kb.dma("pool", qh[:, T // 2:T], QT[:, h, T // 2:T], reads=[QT], writes=[qh])
                if bt_per_head:
                    load_bt(h)
                if kv != cur_kv:
                    cur_kv = kv
                    kb.dma("pool", kh[:, 0:T // 2], KT[:, kv, 0:T // 2], reads=[KT], writes=[kh])
                    kb.dma("pool", kh[:, T // 2:T], KT[:, kv, T // 2:T], reads=[KT], writes=[kh])
                    kb.dma("pool", va[:, :, 0:128], V[:, kv * 128:(kv + 1) * 128].rearrange("(i p) d -> p i d", p=128),
                           reads=[V], writes=[va])
                    if rope:
                        do_rope(kh)
                if rope:
                    do_rope(qh)
                def emit_S(n, b):
                    blocks = plans[n]
                    pS = psS[b]
                    for jj, (j, bk) in enumerate(blocks):
                        p = pS[jj // 4]
                        cols = slice((jj % 4) * 128, (jj % 4 + 1) * 128)
                        kb.mm(p[:, cols], kh[:, j * 128:(j + 1) * 128], qh[:, n * 128:(n + 1) * 128],
                              True, bk is None, [kh, qh], [p])
                        if bk is not None:
                            kb.mm(p[:, cols], identR[:], bt[:, bk, :], False, True, [identR, bt], [p])

                def emit_exp(n, b):
                    nb = len(plans[n])
                    e_t = et[b]
                    pS = psS[b]
                    for bank in range((nb + 3) // 4):
                        w = min(4, nb - bank * 4)
                        kb.actf(e_t[:, bank * 4:bank * 4 + w, :], pS[bank][:, 0:w * 128].rearrange("p (a b) -> p a b", b=128),
                                AF.Exp, [pS[bank]], [e_t], scale=scale)

                def emit_PV(n, b):
                    blocks = plans[n]
                    nb = len(blocks)
                    for jj, (j, bk) in enumerate(blocks):
                        kb.mm(psO[b][:, 0:130], et[b][:, jj, :], va[:, j, 0:130], jj == 0, jj == nb - 1, [et[b], va], [psO[b]])

                def emit_norm(n, b):
                    r = rd[b]
                    o = osb[b]
                    pO = psO[b]
                    if sink_l is not None:
                        kb.tt("dve", r[:, 0:1], pO[:, 128:129], esk[:, h:h + 1], ALU.add, [pO, esk], [r])
                        kb.op("dve", lambda e, r=r: e.reciprocal(out=r[:, 1:2], in_=r[:, 0:1]), reads=[r], writes=[r])
                    else:
                        kb.op("dve", lambda e, r=r, pO=pO: e.reciprocal(out=r[:, 1:2], in_=pO[:, 128:129]), reads=[pO], writes=[r])
                    kb.ts("dve", o[:], pO[:, 0:128], r[:, 1:2], None, ALU.mult, None, [pO, r], [o])

                def emit_tr(n, b):
                    kb.tr(psT[:], osb[b][:], self.ident[:], [osb[b], self.ident], [psT])
                    kb.copy("act", oth[:, n * 128:(n + 1) * 128], psT[:], [psT], [oth])

                b0 = it % 2
                emit_S(0, b0)
                for n in range(NT):
                    b = (b0 + n) % 2
                    emit_exp(n, b)
                    if n + 1 < NT:
                        emit_S(n + 1, 1 - b)
                    emit_PV(n, b)
                    emit_norm(n, b)
                    if n >= 1:
                        emit_tr(n - 1, 1 - b)
                emit_tr(NT - 1, (b0 + NT - 1) % 2)
                it += NT
                kb.dma("sp", CATT[:, cat_c0 + h, :], oth[:], reads=[oth], writes=[CATT])

    def stage_ln(self, l, which, s_gate, X, Y, Xout, H2=None, AFF=None):
        kb = self.kb
        Ml = self.M[l]
        lg = self.inputs["ln_g"]
        lb = self.inputs["ln_b"]
        with ExitStack() as st:
            mg = [kb.sbuf(st, "lmg%d" % r, [128, D]) for r in range(2)]
            gb = kb.sbuf(st, "lgb", [128, D])
            bb = kb.sbuf(st, "lbb", [128, D])
            for r in range(2):
                kb.dma("pool", mg[r][:], Ml[r:r + 1, s_gate * D:(s_gate + 1) * D].to_broadcast([128, D]), reads=[Ml], writes=[mg[r]])
            kb.dma("pool", gb[:], lg[l, which:which + 1, :].to_broadcast([128, D]), reads=[lg], writes=[gb])
            kb.dma("pool", bb[:], lb[l, which:which + 1, :].to_broadcast([128, D]), reads=[lb], writes=[bb])
            if H2 is not None:
                m3 = [kb.sbuf(st, "lm3%d" % r, [128, D]) for r in range(2)]
                m4 = [kb.sbuf(st, "lm4%d" % r, [128, D]) for r in range(2)]
                for r in range(2):
                    kb.dma("pool", m3[r][:], Ml[r:r + 1, 3 * D:4 * D].to_broadcast([128, D]), reads=[Ml], writes=[m3[r]])
                    kb.dma("pool", m4[r][:], Ml[r:r + 1, 4 * D:5 * D].to_broadcast([128, D]), reads=[Ml], writes=[m4[r]])
                    kb.ts("pool", m4[r][:], m4[r][:], 1.0, None, ALU.add, None, [m4[r]], [m4[r]])
                wr = kb.sbuf(st, "lwr", [128, 16, NEXP])
                wrin = self.inputs["moe_w_router"]
                kb.dma("sp", wr[:], wrin[l].rearrange("(k p) e -> p k e", p=128), reads=[wrin], writes=[wr])
                h2b = [kb.sbuf(st, "lh2%d" % i, [128, D]) for i in range(2)]
                h2T = kb.sbuf(st, "lh2T", [128, 16, 128])
                psT = [kb.psum(st, "lpt%d" % i, [128, 4, 128]) for i in range(2)]
                psL = kb.psum(st, "lpl", [128, NEXP])
                lgt = kb.sbuf(st, "llg", [128, NEXP])
                sm = kb.sbuf(st, "lsm", [128, 4])
            xb = [kb.sbuf(st, "lx%d" % i, [128, D]) for i in range(2)]
            yb = [kb.sbuf(st, "ly%d" % i, [128, D]) for i in range(2)]
            stt_ = kb.sbuf(st, "lst", [128, 4, 6])
            mv = kb.sbuf(st, "lmv", [128, 4])
            def router(i):
                h2 = h2b[i % 2]
                for g in range(4):
                    p = psT[g % 2]
                    for j in range(4):
                        c = g * 4 + j
                        kb.tr(p[:, j, :], h2[:, c * 128:(c + 1) * 128], self.ident[:], [h2, self.ident], [p])
                    kb.copy("act", h2T[:, g * 4:(g + 1) * 4, :], p[:], [p], [h2T])
                for c in range(16):
                    kb.mm(psL[:], h2T[:, c, :], wr[:, c, :], c == 0, c == 15, [h2T, wr], [psL])
                kb.op("dve", lambda e: e.reduce_max(out=sm[:, 0:1], in_=psL[:], axis=AX.X), reads=[psL], writes=[sm])
                kb.ts("dve", sm[:, 1:2], sm[:, 0:1], -1.0, None, ALU.mult, None, [sm], [sm])
                kb.actf(lgt[:], psL[:], AF.Exp, [psL, sm], [lgt, sm], bias=sm[:, 1:2], accum_out=sm[:, 2:3])
                kb.op("dve", lambda e: e.reciprocal(out=sm[:, 3:4], in_=sm[:, 2:3]), reads=[sm], writes=[sm])
                kb.ts("dve", AFF[:, i, :], lgt[:], sm[:, 3:4], None, ALU.mult, None, [lgt, sm], [AFF])

            for i in range(NT):
                r = 0 if i < 16 else 1
                x = xb[i % 2]
                y = yb[i % 2]
                rows = slice(i * 128, (i + 1) * 128)
                if i == 0:
                    kb.dma("sp", x[:], X[rows, :], reads=[X], writes=[x])
                    kb.dma("sp", y[:], Y[rows, :], reads=[Y], writes=[y])
                if i + 1 < NT:
                    rn = slice((i + 1) * 128, (i + 2) * 128)
                    kb.dma("sp", xb[(i + 1) % 2][:], X[rn, :], reads=[X], writes=[xb[(i + 1) % 2]])
                    kb.dma("sp", yb[(i + 1) % 2][:], Y[rn, :], reads=[Y], writes=[yb[(i + 1) % 2]])
                kb.tt("dve", y[:], y[:], mg[r][:], ALU.mult, [y, mg[r]], [y])
                kb.stt("pool", y[:], x[:], float(DN_ALPHA), y[:], ALU.mult, ALU.add, [x, y], [y])
                for c in range(4):
                    kb.op("dve", lambda e, c=c, y=y: e.bn_stats(out=stt_[:, c, :], in_=y[:, c * 512:(c + 1) * 512]), reads=[y], writes=[stt_])
                kb.op("dve", lambda e: e.bn_aggr(out=mv[:, 0:2], in_=stt_[:].rearrange("p a b -> p (a b)")), reads=[stt_], writes=[mv])
                kb.ts("dve", mv[:, 2:3], mv[:, 1:2], float(LN_EPS), None, ALU.add, None, [mv], [mv])
                kb.actf(mv[:, 2:3], mv[:, 2:3], AF.Sqrt, [mv], [mv])
                kb.op("dve", lambda e: e.reciprocal(out=mv[:, 3:4], in_=mv[:, 2:3]), reads=[mv], writes=[mv])
                kb.ts("dve", y[:], y[:], mv[:, 0:1], mv[:, 3:4], ALU.subtract, ALU.mult, [y, mv], [y])
                kb.tt("pool", y[:], y[:], gb[:], ALU.mult, [y, gb], [y])
                kb.tt("dve", x[:], y[:], bb[:], ALU.add, [y, bb], [x])
                kb.dma("sp", Xout[rows, :], x[:], reads=[x], writes=[Xout])
                if H2 is not None:
                    h2 = h2b[i % 2]
                    kb.tt("pool", h2[:], x[:], m4[r][:], ALU.mult, [x, m4[r]], [h2])
                    kb.tt("pool", h2[:], h2[:], m3[r][:], ALU.add, [h2, m3[r]], [h2])
                    kb.dma("sp", H2[rows, :], h2[:], reads=[h2], writes=[H2])
                    if i >= 1:
                        router(i - 1)
            if H2 is not None:
                router(NT - 1)

    def stage_moe(self, l, H2, AFF, Fo):
        kb = self.kb
        W1 = self.inputs["moe_w1"]
        W3 = self.inputs["moe_w3"]
        W2 = self.inputs["moe_w2"]
        SEGS = [(0, 16, 256, 0), (16, 2, 32, 256)]
        CAPT = 288
        YS = self.YSm
        with ExitStack() as st:
            masks = [kb.sbuf(st, "gmask%d" % g, [128, SEGS[g][1], 16]) for g in range(2)]
            poss = [kb.sbuf(st, "gpos%d" % g, [128, SEGS[g][1], 16]) for g in range(2)]
            gate = kb.sbuf(st, "ggate", [128, 8])
            psAs = [kb.psum(st, "gpa%d" % i, [128, 512]) for i in range(2)]
            psUs = [kb.psum(st, "gpu%d" % i, [128, 512]) for i in range(2)]
            psYs = [kb.psum(st, "gpy%d" % i, [128, 512]) for i in range(2)]
            psM = kb.psum(st, "gpm", [128, 512])
            psG1 = kb.psum(st, "gpg", [128, 512])
            psG = psAs + psUs
            for g, (tile0, ntile, cap, soff) in enumerate(SEGS):
                N = ntile * 128
                mask, pos = masks[g], poss[g]
                st2 = ExitStack()
                afT = kb.sbuf(st2, "gafT", [16, N])
                bc = kb.sbuf(st2, "gbc", [128, N])
                junk = kb.sbuf(st2, "gjunk", [128, N])
                rank = kb.sbuf(st2, "grank", [128, ntile, 16])
                for i in range(ntile):
                    kb.tr(psM[0:16, 0:128], AFF[:, tile0 + i, :], self.ident[:], [AFF, self.ident], [psM])
                    kb.copy("dve", afT[:, i * 128:(i + 1) * 128], psM[0:16, 0:128], [psM], [afT])
                npb = 0
                for e_ in range(NEXP):
                    for blk in range(0, N, 512):
                        n_ = min(512, N - blk)
                        pb = psG[npb % 4]
                        npb += 1
                        kb.mm(pb[:, 0:n_], self.sel16[:, e_, :], afT[:, blk:blk + n_], True, True, [self.sel16, afT], [pb])
                        kb.copy("act", bc[:, blk:blk + n_], pb[:, 0:n_], [pb], [bc])
                    for i in range(ntile):
                        kb.op("dve", lambda g_, i=i, e_=e_, tile0=tile0, junk=junk, bc=bc, rank=rank: g_.tensor_scalar(
                            out=junk[:], in0=bc[:], scalar1=AFF[:, tile0 + i, e_:e_ + 1], scalar2=None, op0=ALU.is_gt, op1=ALU.add,
                            accum_out=rank[:, i, e_:e_ + 1]), reads=[bc, AFF], writes=[junk, rank])
                kb.ts("dve", mask[:].rearrange("p a b -> p (a b)"), rank[:].rearrange("p a b -> p (a b)"), float(cap), None, ALU.is_lt, None,
                      [rank], [mask])
                for i in range(ntile):
                    for i2 in range(i):
                        kb.mm(psM[:, 0:16], self.ones[:], mask[:, i2, :], i2 == 0, False, [self.ones, mask], [psM])
                    kb.mm(psM[:, 0:16], self.ustrict[:], mask[:, i, :], i == 0, True, [self.ustrict, mask], [psM])
                    kb.copy("dve", pos[:, i, :], psM[:, 0:16], [psM], [pos])
                st2.close()
            sels = [kb.sbuf(st, "gsel%d" % g, [128, SEGS[g][1], SEGS[g][2]], F32R) for g in range(2)]
            xsT = kb.sbuf(st, "gxsT", [128, 16, CAPT], F32R)
            gT = kb.sbuf(st, "ggT", [128, 8, 384], F32R)
            zpad = kb.sbuf(st, "gzp", [128, 96])
            kb.op("dve", lambda e: e.memset(zpad[:], 0.0), writes=[zpad])
            for fc_ in range(8):
                kb.copy("dve", gT[:, fc_, 288:384], zpad[:], [zpad], [gT])
            stg = [kb.sbuf(st, "gstg%d" % i, [128, 512]) for i in range(2)]
            xtok = [kb.sbuf(st, "gxk%d" % i, [128, D]) for i in range(3)]
            idxs = [kb.sbuf(st, "gix%d" % i, [128, 1], I32) for i in range(6)]
            ngp = 0
            wbuf = [kb.sbuf(st, "gw%d" % i, [128, 8192], F32R) for i in range(3)]
            sas = [kb.sbuf(st, "gsa%d" % i, [128, CAPT]) for i in range(2)]
            chunks = [(0, 0, 128, 0, 0), (0, 128, 128, 128, 1), (1, 0, 32, 256, 2)]
            cnt = {"nst": 0, "nw": 0, "ngp": 0, "ny": 0}
            gi = kb.sbuf(st, "ggi", [128, NT, 2])
            kb.copy("dve", gi[:, :, 1:2], self.tokidx[:].rearrange("p (a b) -> p a b", b=1), [self.tokidx], [gi])

            def prepA(e_):
                for g, (tile0, ntile, cap, soff) in enumerate(SEGS):
                    for i in range(ntile):
                        kb.ts("dve", sels[g][:, i, :], self.iota[:, 0:cap], poss[g][:, i, e_:e_ + 1], masks[g][:, i, e_:e_ + 1],
                              ALU.is_equal, ALU.mult, [self.iota, poss[g], masks[g]], [sels[g]])
                kb.copy("dve", gi[:, :, 0:1], AFF[:, :, e_:e_ + 1], [AFF], [gi])
                for ci, (g, s0, ns, gs0, gc) in enumerate(chunks):
                    tile0, ntile, cap, soff = SEGS[g]
                    for i in range(ntile):
                        kb.mm(psM[0:ns, 0:2], sels[g][:, i, s0:s0 + ns], gi[:, tile0 + i, :], i == 0, i == ntile - 1,
                              [sels[g], gi], [psM])
                    ix = idxs[(e_ * 3 + ci) % 6]
                    kb.copy("dve", ix[0:ns, :], psM[0:ns, 1:2], [psM], [ix])
                    gcol = (e_ % 2) * 4 + gc
                    kb.copy("dve", gate[0:ns, gcol:gcol + 1], psM[0:ns, 0:1], [psM], [gate])
                    kb.gather(xtok[ci], ns, H2, ix)
                for (g, s0, ns, gs0, gc) in chunks:
                    tile0, ntile, cap, soff = SEGS[g]
                    nsc = (cap + 127) // 128
                    sc = s0 // 128
                    for ig in range(0, ntile, 4):
                        ng = min(4, ntile - ig)
                        for i in range(ig, ig + ng):
                            kb.tr(psM[0:ns, (i - ig) * 128:(i - ig + 1) * 128], sels[g][:, i, s0:s0 + ns], self.ident[:],
                                  [sels[g], self.ident], [psM])
                        sg = stg[cnt["nst"] % 2]
                        cnt["nst"] += 1
                        kb.copy("act", sg[0:ns, 0:ng * 128], psM[0:ns, 0:ng * 128], [psM], [sg])
                        kb.dma("sp", self.SELT[g][ig:ig + ng, 0:ns, e_ * nsc + sc, :].rearrange("i p t -> p i t"),
                               sg[0:ns, 0:ng * 128].rearrange("p (i t) -> p i t", t=128), reads=[sg], writes=[self.SELT[g]])
            def prepB(e_):
                for ci, (g, s0, ns, gs0, gc) in enumerate(chunks):
                    xk = xtok[ci]
                    for dg in range(4):
                        p = psG1 if dg % 2 == 0 else psM
                        for j in range(4):
                            dc = dg * 4 + j
                            kb.tr(p[:, j * 128:j * 128 + ns], xk[0:ns, dc * 128:(dc + 1) * 128], self.ident[0:ns, 0:ns], [xk, self.ident], [p])
                        kb.copy("act" if dg % 2 == 0 else "dve", xsT[:, dg * 4:(dg + 1) * 4, gs0:gs0 + ns],
                                p[:].rearrange("p (a b) -> p a b", b=128)[:, :, 0:ns], [p], [xsT])

            def ffn_up(e_):
                for fh in range(2):
                    w1 = wbuf[cnt["nw"] % 3]
                    cnt["nw"] += 1
                    w3 = wbuf[cnt["nw"] % 3]
                    cnt["nw"] += 1
                    kb.dma("pool", w1[:].rearrange("p (k n) -> p k n", n=512),
                           W1[l, e_, :, fh * 512:(fh + 1) * 512].rearrange("(k p) n -> p k n", p=128), reads=[W1], writes=[w1])
                    kb.dma("pool", w3[:].rearrange("p (k n) -> p k n", n=512),
                           W3[l, e_, :, fh * 512:(fh + 1) * 512].rearrange("(k p) n -> p k n", p=128), reads=[W3], writes=[w3])
                    for fc in range(4):
                        psA = psAs[fc % 2]
                        psU = psUs[fc % 2]
                        sa_ = sas[fc % 2]
                        for dc in range(16):
                            kb.mm(psA[:, 0:CAPT], w1[:, dc * 512 + fc * 128:dc * 512 + (fc + 1) * 128], xsT[:, dc, :], dc == 0, dc == 15,
                                  [w1, xsT], [psA])
                        for dc in range(16):
                            kb.mm(psU[:, 0:CAPT], w3[:, dc * 512 + fc * 128:dc * 512 + (fc + 1) * 128], xsT[:, dc, :], dc == 0, dc == 15,
                                  [w3, xsT], [psU])
                        kb.actf(sa_[:], psA[:, 0:CAPT], AF.Silu, [psA], [sa_])
                        kb.tt("dve", gT[:, fh * 4 + fc, 0:CAPT], sa_[:], psU[:, 0:CAPT], ALU.mult, [sa_, psU], [gT])

            def ffn_down(e_):
                for dh in range(2):
                    w2 = wbuf[cnt["nw"] % 3]
                    cnt["nw"] += 1
                    kb.dma("pool", w2[:].rearrange("p (k n) -> p k n", n=1024),
                           W2[l, e_, :, dh * 1024:(dh + 1) * 1024].rearrange("(k p) n -> p k n", p=128), reads=[W2], writes=[w2])
                    for (g, s0, ns, gs0, gc) in chunks:
                        gcol = (e_ % 2) * 4 + gc
                        for db in range(2):
                            psY = psYs[cnt["ny"] % 2]
                            cnt["ny"] += 1
                            mrows = 128
                            for fc in range(8):
                                kb.mm(psY[0:mrows, :], gT[:, fc, gs0:gs0 + mrows], w2[:, fc * 1024 + db * 512:fc * 1024 + (db + 1) * 512],
                                      fc == 0, fc == 7, [gT, w2], [psY])
                            sg = stg[cnt["nst"] % 2]
                            cnt["nst"] += 1
                            kb.actf(sg[0:ns, :], psY[0:ns, :], AF.Identity, [psY, gate], [sg], scale=gate[0:ns, gcol:gcol + 1])
                            c0 = dh * 1024 + db * 512
                            kb.dma("sp", YS[e_, gs0:gs0 + ns, c0:c0 + 512], sg[0:ns, :], reads=[sg], writes=[YS])

            prepA(0)
            prepB(0)
            for e_ in range(NEXP):
                if e_ + 1 < NEXP:
                    prepA(e_ + 1)
                ffn_up(e_)
                if e_ + 1 < NEXP:
                    prepB(e_ + 1)
                ffn_down(e_)
        for g, (tile0, ntile, cap, soff) in enumerate(SEGS):
            nsc = (cap + 127) // 128
            sl_ = min(cap, 128)
            with ExitStack() as st:
                ysall = kb.sbuf(st, "cys", [128, NEXP * nsc, 512], F32R)
                selt = [kb.sbuf(st, "cse%d" % i, [128, NEXP * nsc, 128], F32R) for i in range(2)]
                ob = [kb.sbuf(st, "cob%d" % i, [128, 512]) for i in range(2)]
                ps = [kb.psum(st, "cps%d" % i, [128, 512]) for i in range(2)]
                n = 0
                for db in range(4):
                    for e_ in range(NEXP):
                        kb.dma("pool", ysall[0:sl_, e_ * nsc:(e_ + 1) * nsc, :],
                               YS[e_, soff:soff + cap, db * 512:(db + 1) * 512].rearrange("(s p) d -> p s d", p=sl_), reads=[YS], writes=[ysall])
                    for i in range(ntile):
                        se = selt[n % 2]
                        p = ps[n % 2]
                        o = ob[n % 2]
                        n += 1
                        kb.dma("pool", se[0:sl_, :, :], self.SELT[g][i, 0:sl_, :, :], reads=[self.SELT[g]], writes=[se])
                        for q_ in range(NEXP * nsc):
                            kb.mm(p[:], se[0:sl_, q_, :], ysall[0:sl_, q_, :], q_ == 0, q_ == NEXP * nsc - 1, [se, ysall], [p])
                        kb.copy("act", o[:], p[:], [p], [o])
                        kb.dma("sp", Fo[(tile0 + i) * 128:(tile0 + i + 1) * 128, db * 512:(db + 1) * 512], o[:], reads=[o], writes=[Fo])

    def hy_filters(self, i, Ls, seg):
        kb = self.kb
        nt = Ls // 128
        zT = self.inputs["zembT%d" % seg]
        dec = self.inputs["decay%d" % seg]
        fw1 = self.inputs["hy_f_w1"]
        fw2 = self.inputs["hy_f_w2"]
        fw3 = self.inputs["hy_f_w3"]
        fcol = self.inputs["hy_fcol"]
        KP, KM = self.KP[seg], self.KM[seg]
        with ExitStack() as st:
            z = kb.sbuf(st, "fz", [33, Ls])
            w1 = kb.sbuf(st, "fw1", [33, 64])
            w2 = kb.sbuf(st, "fw2", [64, 2, 64])
            w3 = kb.sbuf(st, "fw3", [64, 4096])
            fc = kb.sbuf(st, "fc", [64, 8])
            ha = kb.sbuf(st, "fha", [64, Ls])
            hb_ = kb.sbuf(st, "fhb", [64, Ls])
            wr_ = kb.sbuf(st, "fwr", [64, 512])
            wa_ = kb.sbuf(st, "fwa", [64, 512])
            wb_ = kb.sbuf(st, "fwb", [64, 512])
            dt = [kb.sbuf(st, "fdt%d" % j, [128, 1024]) for j in range(2)]
            ft = [kb.sbuf(st, "fft%d" % j, [128, 4096]) for j in range(2)]
            kp = [kb.sbuf(st, "fkp%d" % j, [128, 2048]) for j in range(2)]
            km = [kb.sbuf(st, "fkm%d" % j, [128, 2048]) for j in range(2)]
            ps = [kb.psum(st, "fps%d" % j, [128, 512]) for j in range(4)]
            kb.dma("sp", z[:], zT[:], reads=[zT], writes=[z])
            kb.dma("sp", w1[:], fw1[i], reads=[fw1], writes=[w1])
            kb.dma("sp", w2[:], fw2[i].rearrange("a k n -> k a n"), reads=[fw2], writes=[w2])
            kb.dma("pool", w3[:], fw3[i], reads=[fw3], writes=[w3])
            kb.dma("sp", fc[:, 0:4], fcol[i], reads=[fcol], writes=[fc])
            for j in range(3):
                kb.tt("dve", fc[:, 4 + j:5 + j], fc[:, 1 + j:2 + j], fc[:, 0:1], ALU.mult, [fc], [fc])
            src = z
            srcK = 33
            cur = ha
            for layer in range(3):
                wl = w1[:, :] if layer == 0 else w2[:, layer - 1, :]
                for blk in range(0, Ls, 512):
                    n_ = min(512, Ls - blk)
                    p = ps[(blk // 512) % 4]
                    kb.mm(p[0:64, 0:n_], wl, src[0:srcK, blk:blk + n_], True, True, [w1, w2, src], [p])
                    kb.ts("dve", wr_[:, 0:n_], p[0:64, 0:n_], fc[:, 0:1], fc[:, 4 + layer:5 + layer], ALU.mult, ALU.add, [p, fc], [wr_])
                    kb.ts("dve", wa_[:, 0:n_], wr_[:, 0:n_], -math.pi, 2 * math.pi, ALU.is_lt, ALU.mult, [wr_], [wa_])
                    kb.ts("pool", wb_[:, 0:n_], wr_[:, 0:n_], math.pi, 2 * math.pi, ALU.is_gt, ALU.mult, [wr_], [wb_])
                    kb.tt("dve", wr_[:, 0:n_], wr_[:, 0:n_], wa_[:, 0:n_], ALU.add, [wr_, wa_], [wr_])
                    kb.tt("dve", wr_[:, 0:n_], wr_[:, 0:n_], wb_[:, 0:n_], ALU.subtract, [wr_, wb_], [wr_])
                    kb.actf(cur[:, blk:blk + n_], wr_[:, 0:n_], AF.Sin, [wr_], [cur])
                src = cur
                srcK = 64
                cur = hb_ if cur is ha else ha
            hfin = src
            for tc in range(nt):
                d_ = dt[tc % 2]
                f_ = ft[tc % 2]
                kb.dma("sp", d_[:], dec[tc * 128:(tc + 1) * 128, :], reads=[dec], writes=[d_])
                for cb in range(8):
                    p = ps[cb % 4]
                    kb.mm(p[:], hfin[:, tc * 128:(tc + 1) * 128], w3[:, cb * 512:(cb + 1) * 512], True, True, [hfin, w3], [p])
                    kb.tt("dve", f_[:, cb * 512:(cb + 1) * 512], p[:], d_[:, (cb % 2) * 512:(cb % 2 + 1) * 512], ALU.mult, [p, d_], [f_])
                a = kp[tc % 2]
                m = km[tc % 2]
                for o in range(2):
                    kb.tt("pool", a[:, o * 1024:(o + 1) * 1024], f_[:, o * 2048:o * 2048 + 1024], f_[:, o * 2048 + 1024:(o + 1) * 2048],
                          ALU.add, [f_], [a])
                    kb.tt("pool", m[:, o * 1024:(o + 1) * 1024], f_[:, o * 2048:o * 2048 + 1024], f_[:, o * 2048 + 1024:(o + 1) * 2048],
                          ALU.subtract, [f_], [m])
                    kb.dma("sp", KP[o, tc * 128:(tc + 1) * 128, :], a[:, o * 1024:(o + 1) * 1024], reads=[a], writes=[KP])
                    kb.dma("sp", KM[o, tc * 128:(tc + 1) * 128, :], m[:, o * 1024:(o + 1) * 1024], reads=[m], writes=[KM])
        for o in range(2):
            self.hy_fwd(seg, Ls, [(self.KP[seg], o)], None, self.SA[seg], self.SB[seg], o, 1.0 / Ls, parts="C")
            self.hy_fwd(seg, Ls, [(self.KM[seg], o)], None, self.SA[seg], self.SB[seg], o, 1.0 / Ls, parts="S")

    def hy_fwd(self, seg, Ls, srcs, spec, OA, OB, o, scale, parts="CS"):
        kb = self.kb
        nt = Ls // 128
        Cm = self.inputs["dftC%d" % seg]
        Sm = self.inputs["dftS%d" % seg]
        with ExitStack() as st:
            zr = kb.sbuf(st, "dz0", [128, nt, 1024], F32R)
            zi = zr
            kb.dma("pool", zr[:, 0:nt // 2, :], srcs[0][0][srcs[0][1], 0:Ls // 2, :].rearrange("(k p) c -> p k c", p=128), reads=[srcs[0][0]], writes=[zr])
            kb.dma("pool", zr[:, nt // 2:nt, :], srcs[0][0][srcs[0][1], Ls // 2:Ls, :].rearrange("(k p) c -> p k c", p=128), reads=[srcs[0][0]], writes=[zr])
            cp = [kb.sbuf(st, "dcp%d" % j, [128, nt, 128], F32R) for j in range(2)]
            sp = [kb.sbuf(st, "dsp%d" % j, [128, nt, 128], F32R) for j in range(2)]
            ps = [kb.psum(st, "dps%d" % j, [128, 512]) for j in range(8)]
            ra = [kb.sbuf(st, "dra%d" % j, [128, 1024]) for j in range(2)]
            rb = [kb.sbuf(st, "drb%d" % j, [128, 1024]) for j in range(2)]
            if spec is not None:
                ka = [kb.sbuf(st, "dka%d" % j, [128, 1024]) for j in range(2)]
                kbb = [kb.sbuf(st, "dkb%d" % j, [128, 1024]) for j in range(2)]
                t1 = kb.sbuf(st, "dt1", [128, 1024])
                t2 = kb.sbuf(st, "dt2", [128, 1024])
                zrs = kb.sbuf(st, "dzr", [128, 1024])
                zis = kb.sbuf(st, "dzi", [128, 1024])
            for fc in range(nt):
                c_ = cp[fc % 2]
                s_ = sp[fc % 2]
                pp = ps[(fc % 2) * 4:(fc % 2) * 4 + 4]
                if "C" in parts:
                    kb.dma("pool", c_[:], Cm[:, fc * 128:(fc + 1) * 128].rearrange("(k p) f -> p k f", p=128), reads=[Cm], writes=[c_])
                    for k in range(nt):
                        for hcol in range(2):
                            kb.mm(pp[hcol][:], c_[:, k, :], zr[:, k, hcol * 512:(hcol + 1) * 512], k == 0, k == nt - 1, [c_, zr], [pp[hcol]])
                if "S" in parts:
                    kb.dma("pool", s_[:], Sm[:, fc * 128:(fc + 1) * 128].rearrange("(k p) f -> p k f", p=128), reads=[Sm], writes=[s_])
                    for k in range(nt):
                        for hcol in range(2):
                            kb.mm(pp[2 + hcol][:], s_[:, k, :], zi[:, k, hcol * 512:(hcol + 1) * 512], k == 0, k == nt - 1, [s_, zi], [pp[2 + hcol]])
                a_ = ra[fc % 2]
                b_ = rb[fc % 2]
                rows = slice(fc * 128, (fc + 1) * 128)
                if spec is None:
                    for hcol in range(2):
                        if "C" in parts:
                            kb.actf(a_[:, hcol * 512:(hcol + 1) * 512], pp[hcol][:], AF.Copy, [pp[hcol]], [a_], scale=float(scale))
                        if "S" in parts:
                            kb.ts("dve", b_[:, hcol * 512:(hcol + 1) * 512], pp[2 + hcol][:], float(scale), None, ALU.mult, None, [pp[2 + hcol]], [b_])
                else:
                    A, B = spec
                    ka_ = ka[fc % 2]
                    kb_ = kbb[fc % 2]
                    kb.dma("pool", ka_[:], A[o, rows, :], reads=[A], writes=[ka_])
                    kb.dma("pool", kb_[:], B[o, rows, :], reads=[B], writes=[kb_])
                    for hcol in range(2):
                        kb.copy("act", zrs[:, hcol * 512:(hcol + 1) * 512], pp[hcol][:], [pp[hcol]], [zrs])
                        kb.copy("act", zis[:, hcol * 512:(hcol + 1) * 512], pp[2 + hcol][:], [pp[2 + hcol]], [zis])
                    kb.tt("dve", t1[:], zrs[:], ka_[:], ALU.mult, [zrs, ka_], [t1])
                    kb.tt("dve", t2[:], zis[:], kb_[:], ALU.mult, [zis, kb_], [t2])
                    kb.tt("dve", a_[:], t1[:], t2[:], ALU.subtract, [t1, t2], [a_])
                    kb.tt("dve", t1[:], zrs[:], kb_[:], ALU.mult, [zrs, kb_], [t1])
                    kb.tt("dve", t2[:], zis[:], ka_[:], ALU.mult, [zis, ka_], [t2])
                    kb.tt("dve", b_[:], t1[:], t2[:], ALU.add, [t1, t2], [b_])
                if "C" in parts:
                    kb.dma("sp", OA[o, rows, :], a_[:], reads=[a_], writes=[OA])
                if "S" in parts:
                    kb.dma("sp", OB[o, rows, :], b_[:], reads=[b_], writes=[OB])

    def hy_inv(self, seg, Ls, tok0, o, ZT_in, zin_c0, gate_c0, OUT, out_c0, bias_i):
        kb = self.kb
        nt = Ls // 128
        CT = self.inputs["dftCT%d" % seg]
        ST = self.inputs["dftST%d" % seg]
        YR, YI = self.YR[seg], self.YI[seg]
        UT = self.UT
        TBK = min(1024, Ls)
        HB = min(512, TBK)
        nh = TBK // HB
        with ExitStack() as st:
            ct = kb.sbuf(st, "ict", [128, nt, TBK], F32R)
            s_t = kb.sbuf(st, "ist", [128, nt, TBK], F32R)
            yr = [kb.sbuf(st, "iyr%d" % j, [128, nt, 128], F32R) for j in range(2)]
            yi = [kb.sbuf(st, "iyi%d" % j, [128, nt, 128], F32R) for j in range(2)]
            zt = [kb.sbuf(st, "izt%d" % j, [128, TBK]) for j in range(2)]
            gt = [kb.sbuf(st, "igt%d" % j, [128, TBK]) for j in range(2)]
            ot = [kb.sbuf(st, "iot%d" % j, [128, TBK]) for j in range(2)]
            ps = [kb.psum(st, "ips%d" % j, [128, 512]) for j in range(4)]
            n = 0
            for tb in range(Ls // TBK):
                ts_ = slice(tb * TBK, (tb + 1) * TBK)
                tg = slice(tok0 + tb * TBK, tok0 + (tb + 1) * TBK)
                for hh in range(nh):
                    hs = slice(tb * TBK + hh * HB, tb * TBK + (hh + 1) * HB)
                    kb.dma("pool", ct[:, :, hh * HB:(hh + 1) * HB], CT[:, hs].rearrange("(k p) t -> p k t", p=128), reads=[CT], writes=[ct])
                    kb.dma("pool", s_t[:, :, hh * HB:(hh + 1) * HB], ST[:, hs].rearrange("(k p) t -> p k t", p=128), reads=[ST], writes=[s_t])
                for cc in range(8):
                    b = n % 2
                    n += 1
                    kb.dma("pool", yr[b][:], YR[o, :, cc * 128:(cc + 1) * 128].rearrange("(k p) c -> p k c", p=128), reads=[YR], writes=[yr[b]])
                    kb.dma("pool", yi[b][:], YI[o, :, cc * 128:(cc + 1) * 128].rearrange("(k p) c -> p k c", p=128), reads=[YI], writes=[yi[b]])
                    kb.dma("pool", zt[b][:], ZT_in[:, zin_c0 + cc, tg], reads=[ZT_in], writes=[zt[b]])
                    kb.dma("pool", gt[b][:], UT[:, gate_c0 + cc, tg], reads=[UT], writes=[gt[b]])
                    for hh in range(nh):
                        p = ps[(b * 2 + hh) % 4]
                        cs = slice(hh * HB, (hh + 1) * HB)
                        for k in range(nt):
                            kb.mm(p[:, 0:HB], yr[b][:, k, :], ct[:, k, cs], k == 0, False, [yr[b], ct], [p])
                        for k in range(nt):
                            kb.mm(p[:, 0:HB], yi[b][:, k, :], s_t[:, k, cs], False, k == nt - 1, [yi[b], s_t], [p])
                        kb.stt("dve", ot[b][:, cs], zt[b][:, cs], self.hybias[:, bias_i * 8 + cc:bias_i * 8 + cc + 1], p[:, 0:HB], ALU.mult, ALU.add,
                               [zt[b], self.hybias, p], [ot[b]])
                    kb.tt("dve", ot[b][:], ot[b][:], gt[b][:], ALU.mult, [ot[b], gt[b]], [ot[b]])
                    kb.dma("sp", OUT[:, out_c0 + cc, tg], ot[b][:], reads=[ot[b]], writes=[OUT])

    def hy_tok(self, seg, Ls, tok0, SRC, c0, ZTOK):
        kb = self.kb
        nt = Ls // 128
        with ExitStack() as st:
            src = [kb.sbuf(st, "tks%d" % j, [128, Ls]) for j in range(2)]
            ob = [kb.sbuf(st, "tko%d" % j, [128, 4, 128]) for j in range(2)]
            ps = [kb.psum(st, "tkp%d" % j, [128, 4, 128]) for j in range(2)]
            n = 0
            for cc in range(8):
                s_ = src[cc % 2]
                kb.dma("pool", s_[:], SRC[:, c0 + cc, tok0:tok0 + Ls], reads=[SRC], writes=[s_])
                for tg in range(0, nt, 4):
                    ng = min(4, nt - tg)
                    p = ps[n % 2]
                    o = ob[n % 2]
                    n += 1
                    for j in range(ng):
                        kb.tr(p[:, j, :], s_[:, (tg + j) * 128:(tg + j + 1) * 128], self.ident[:], [s_, self.ident], [p])
                    kb.copy("act" if n % 2 else "dve", o[:, 0:ng, :], p[:, 0:ng, :], [p], [o])
                    kb.dma("sp", ZTOK[tg * 128:(tg + ng) * 128, cc * 128:(cc + 1) * 128].rearrange("(j p) c -> p j c", p=128),
                           o[:, 0:ng, :], reads=[o], writes=[ZTOK])

    def stage_hyena(self, i, PH, CATT):
        kb = self.kb
        UT = self.UT
        cw = self.inputs["hy_conv"]
        hbi = self.inputs["hy_biasc"]
        kb.dma("sp", self.hybias[:], hbi[i], reads=[hbi], writes=[self.hybias])
        with ExitStack() as st:
            cwt = kb.sbuf(st, "hcw", [128, 24, 4])
            kb.dma("sp", cwt[:], cw[i], reads=[cw], writes=[cwt])
            xin = [kb.sbuf(st, "hxi%d" % j, [128, T]) for j in range(2)]
            uo = [kb.sbuf(st, "huo%d" % j, [128, T]) for j in range(2)]
            for c in range(24):
                x = xin[c % 2]
                u = uo[c % 2]
                kb.dma("pool", x[:], PH[:, c, :], reads=[PH], writes=[x])
                kb.actf(u[:], x[:], AF.Identity, [x, cwt], [u], scale=cwt[:, c, 1:2], bias=cwt[:, c, 3:4])
                for (a, b_) in ((0, L), (L, T)):
                    kb.stt("dve", u[:, a + 1:b_], x[:, a:b_ - 1], cwt[:, c, 0:1], u[:, a + 1:b_], ALU.mult, ALU.add, [x, cwt, u], [u])
                    kb.stt("pool", u[:, a:b_ - 1], x[:, a + 1:b_], cwt[:, c, 2:3], u[:, a:b_ - 1], ALU.mult, ALU.add, [x, cwt, u], [u])
                kb.dma("sp", UT[:, c, :], u[:], reads=[u], writes=[UT])
        for seg, (Ls, tok0) in enumerate(((L, 0), (LC, L))):
            self.hy_filters(i, Ls, seg)
            self.hy_tok(seg, Ls, tok0, UT, 16, self.ZTOK[seg])
            self.hy_fwd(seg, Ls, [(self.ZTOKv[seg], 0)], (self.SA[seg], self.SB[seg]), self.YR[seg], self.YI[seg], 0, 1.0)
            self.hy_inv(seg, Ls, tok0, 0, UT, 16, 0, self.ZT1, 0, 0)
            self.hy_tok(seg, Ls, tok0, self.ZT1, 0, self.ZTOK[seg])
            self.hy_fwd(seg, Ls, [(self.ZTOKv[seg], 0)], (self.SA[seg], self.SB[seg]), self.YR[seg], self.YI[seg], 1, 1.0)
            self.hy_inv(seg, Ls, tok0, 1, self.ZT1, 0, 8, CATT, 0, 1)

    def build(self, x_name="x"):
        nc = self.nc
        with ExitStack() as st:
            kb = KB(nc, st)
            self.kb = kb
            plans_na, pats = na_structure()
            nuq = pats.shape[0]
            x_in = self.inp("x", [T, D])
            cT = self.inp("cT", [128, 16, 2])
            self.inp("ada_w", [DEPTH, D, 6 * D])
            self.inp("ada_b", [DEPTH, 6 * D])
            self.inp("ln_g", [DEPTH, 2, D])
            self.inp("ln_b", [DEPTH, 2, D])
            self.inp("ev_w_in", [2, D, 4608])
            self.inp("ev_w_out", [2, D, D])
            self.inp("od_w_in", [2, D, 3 * D])
            self.inp("od_w_out", [2, D, D])
            self.inp("hy_conv", [2, 128, 24, 4])
            self.inp("hy_biasc", [2, 128, 16])
            self.inp("hy_f_w1", [2, 33, 64])
            self.inp("hy_f_w2", [2, 2, 64, 64])
            self.inp("hy_f_w3", [2, 64, 4096])
            self.inp("hy_fcol", [2, 64, 4])
            self.inp("swa_sink", [2, 8])
            self.inp("moe_w_router", [DEPTH, D, NEXP])
            self.inp("moe_w1", [DEPTH, NEXP, D, EFF])
            self.inp("moe_w3", [DEPTH, NEXP, D, EFF])
            self.inp("moe_w2", [DEPTH, NEXP, EFF, D])
            ident_in = self.inp("ident", [128, 128])
            ustr_in = self.inp("ustrict", [128, 128])
            iota_in = self.inp("iota", [128, 256])
            sel16_in = self.inp("sel16", [16, 16, 128])
            tokidx_in = self.inp("tokidx", [128, NT])
            ropeR_in = self.inp("ropeR", [128, 128])
            ropeC_in = self.inp("ropeC", [128, L])
            ropeS_in = self.inp("ropeS", [128, L])
            for seg, Ls in enumerate((L, LC)):
                self.inp("zembT%d" % seg, [33, Ls])
                self.inp("decay%d" % seg, [Ls, 1024])
                for nm in ("dftC", "dftS", "dftCT", "dftST"):
                    self.inp("%s%d" % (nm, seg), [Ls, Ls])
            evbt = self.inp("evbt", [1, 128, 2, 128])
            nab = self.inp("nab", [2, 16, 128, nuq, 128])
            self.M = [self.scratch("M%d" % l, [2, 6 * D]) for l in range(DEPTH)]
            HT = self.scratch("HT", [128, 16, T])
            PH = self.scratch("PH", [128, 24, T])
            QT = self.scratch("QT", [128, 16, T])
            KT = self.scratch("KT", [128, 16, T])
            V = self.scratch("V", [T, D])
            CATT = self.scratch("CATT", [128, 16, T])
            Y = self.scratch("Y", [T, D])
            XA = self.scratch("XA", [T, D])
            XB = self.scratch("XB", [T, D])
            H2 = self.scratch("H2", [T, D])
            Fo = self.scratch("Fo", [T, D])
            self.UT = self.scratch("UT", [128, 24, T])
            self.ZT1 = self.scratch("ZT1", [128, 8, T])
            self.ZTOKv, self.ZTOK, self.KP, self.KM, self.SA, self.SB, self.YR, self.YI = [], [], [], [], [], [], [], []
            self.YS, self.SELT = [], []
            for seg, Ls in enumerate((L, LC)):
                z3 = self.scratch("ZTOK%d" % seg, [1, Ls, 1024])
                z2 = Tile("ZTOK2_%d" % seg, z3.t[0], "dram")
                z2.trk = z3.trk
                self.ZTOKv.append(z3)
                self.ZTOK.append(z2)
                for nm, lst in (("KP", self.KP), ("KM", self.KM), ("SA", self.SA), ("SB", self.SB), ("YR", self.YR), ("YI", self.YI)):
                    lst.append(self.scratch("%s%d" % (nm, seg), [2, Ls, 1024]))
                cap = 2 * Ls // NEXP
                nsc_ = (cap + 127) // 128
                self.SELT.append(self.scratch("SELT%d" % seg, [Ls // 128, min(cap, 128), NEXP * nsc_, 128]))
            self.YSm = self.scratch("YSm", [NEXP, 288, D])
            out = self.kb.dram("out", [T, D], F32, kind="ExternalOutput")
            self.ident = kb.sbuf(st, "ident", [128, 128])
            self.ones = kb.sbuf(st, "ones", [128, 128])
            self.ustrict = kb.sbuf(st, "ustrict", [128, 128])
            self.iota = kb.sbuf(st, "iota", [128, 256])
            self.tokidx = kb.sbuf(st, "tokidx", [128, NT])
            kb.dma("sp", self.tokidx[:], tokidx_in[:], reads=[tokidx_in], writes=[self.tokidx])
            self.sel16 = kb.sbuf(st, "sel16", [16, 16, 128])
            kb.dma("sp", self.sel16[:], sel16_in[:], reads=[sel16_in], writes=[self.sel16])
            self.sT = kb.sbuf(st, "sT", [128, 16, 2])
            self.mcol = kb.sbuf(st, "mcol", [128, 2, 96])
            self.mcol1 = kb.sbuf(st, "mcol1", [128, 2, 96])
            self.hybias = kb.sbuf(st, "hybias", [128, 16])
            AFF = kb.sbuf(st, "AFF", [128, NT, NEXP])
            kb.dma("sp", self.ident[:], ident_in[:], reads=[ident_in], writes=[self.ident])
            kb.dma("sp", self.ustrict[:], ustr_in[:], reads=[ustr_in], writes=[self.ustrict])
            kb.dma("sp", self.iota[:], iota_in[:], reads=[iota_in], writes=[self.iota])
            kb.op("dve", lambda e: e.memset(self.ones[:], 1.0), writes=[self.ones])
            kb.dma("sp", self.sT[:], cT[:], reads=[cT], writes=[self.sT])
            kb.actf(self.sT[:], self.sT[:], AF.Silu, [self.sT], [self.sT])

            plans_ev = []
            for n in range(16):
                p = []
                if n >= 1:
                    p.append((n - 1, 0))
                p.append((n, None))
                if n <= 14:
                    p.append((n + 1, 1))
                p += [(16, None), (17, None)]
                plans_ev.append(p)
            plans_ev += [[(16, None), (17, None)]] * 2

            X = x_in
            stop = self.stop_after
            for l in range(self.l0, self.l0 + self.nlayers):
                i = l // 2
                last = (l == self.l0 + self.nlayers - 1)
                self.stage_mod(l)
                self.stage_modT(X, HT, 0, 1)
                if stop == "modT":
                    break
                if l % 2 == 0:
                    self.stage_proj(HT, self.inputs["ev_w_in"], i,
                                    [("fm", 0, 3072, PH), ("fm", 3072, 1024, QT), ("fm", 4096, 256, KT), ("tm", 4352, 256, V)])
                    if stop == "proj":
                        break
                    with ExitStack() as st2:
                        self.ropeR = kb.sbuf(st2, "ropeR", [128, 128], F32R)
                        self.ropeC = kb.sbuf(st2, "ropeC", [128, L])
                        self.ropeS = kb.sbuf(st2, "ropeS", [128, L])
                        kb.dma("pool", self.ropeR[:], ropeR_in[:], reads=[ropeR_in], writes=[self.ropeR])
                        kb.dma("sp", self.ropeC[:], ropeC_in[:], reads=[ropeC_in], writes=[self.ropeC])
                        kb.dma("pool", self.ropeS[:], ropeS_in[:], reads=[ropeS_in], writes=[self.ropeS])
                        self.stage_attn(QT, KT, V, CATT, 8, 8, 4, plans_ev, evbt, 2, False, True, i)
                    if stop == "attn":
                        break
                    self.stage_hyena(i, PH, CATT)
                    if stop == "hyena":
                        break
                    Wout = self.inputs["ev_w_out"]
                else:
                    self.stage_proj(HT, self.inputs["od_w_in"], i,
                                    [("fm", 0, 2048, QT), ("fm", 2048, 2048, KT), ("tm", 4096, 2048, V)])
                    if stop == "proj":
                        break
                    nabl = Tile("nab%d" % i, nab.t[i], "dram")
                    nabl.trk = nab.trk
                    self.stage_attn(QT, KT, V, CATT, 0, 16, 1, plans_na, nabl, nuq, True, False, None)
                    if stop == "attn":
                        break
                    Wout = self.inputs["od_w_out"]
                self.stage_proj(CATT, Wout, i, [("tm", 0, 2048, Y)])
                if stop == "oproj":
                    break
                self.stage_ln(l, 0, 2, X, Y, XA, H2=H2, AFF=AFF)
                if stop == "ln1":
                    break
                self.stage_moe(l, H2, AFF, Fo)
                if stop == "moe":
                    break
                Xn = out if last else XB
                self.stage_ln(l, 1, 5, XA, Fo, Xn)
                X = Xn
            outs = [t for t in [HT, PH, QT, KT, V, CATT, Y, XA, XB, H2, Fo, self.UT, self.ZT1] + self.M + self.KP + self.KM + self.SA
                    + self.SB + self.YR + self.YI + self.SELT + self.ZTOKv if t.name in self.dbg] + [out]
            kb.wait_all("sp", outs)
            kb.wait_all("pool", outs)
            kb.finalize()
            print("ninst", kb.ninst, "dsems", kb.ndsem)
        return nc


def na_structure():
    col = np.arange(64)
    cs = np.clip(col - 8, 0, 48)
    col_ok = (col[None, :] >= cs[:, None]) & (col[None, :] < cs[:, None] + 16)
    dc = np.clip(col[None, :] - col[:, None] + 15, 0, 30)
    uniq = {}
    pats = []
    plans = []
    for n in range(16):
        full = -np.ones((128, 2048), np.int64)
        for rr in range(2):
            r = 2 * n + rr
            rs = min(max(r - 4, 0), 24)
            for kr in range(8):
                krow = rs + kr
                dr = rs - r + kr + 7
                full[rr * 64:(rr + 1) * 64, krow * 64:(krow + 1) * 64] = np.where(col_ok, dr * 31 + dc, -1)
        plan = []
        for j in range(16):
            t = full[:, j * 128:(j + 1) * 128]
            if (t >= 0).any():
                key = t.tobytes()
                if key not in uniq:
                    uniq[key] = len(pats)
                    pats.append(np.ascontiguousarray(t.T))
                plan.append((j, uniq[key]))
        plan += [(16, None), (17, None)]
        plans.append(plan)
    plans += [[(16, None), (17, None)]] * 2
    return plans, np.stack(pats)


_CONST = {}


def host_consts():
    if _CONST:
        return _CONST
    f32 = np.float32
    c = _CONST
    c["ident"] = np.eye(128, dtype=f32)
    c["ustrict"] = np.triu(np.ones((128, 128), f32), 1)
    s16 = np.zeros((16, 16, 128), f32)
    for e in range(16):
        s16[e, e, :] = 1.0
    c["sel16"] = s16
    c["tokidx"] = (np.arange(NT, dtype=f32)[None, :] * 128 + np.arange(128, dtype=f32)[:, None]).astype(f32)
    c["iota"] = np.tile(np.arange(256, dtype=f32)[None, :], (128, 1))
    R = np.zeros((128, 128), f32)
    for m in range(128):
        if (m % 64) < 32:
            R[m + 32, m] = -1.0
        else:
            R[m - 32, m] = 1.0
    c["ropeR"] = R
    t = np.arange(L)
    inv = (10000.0 ** (-2.0 * np.arange(32, dtype=f32) / 64)).astype(f32)
    C = np.zeros((128, L), f32)
    S = np.zeros((128, L), f32)
    for d in range(128):
        pos = (t // GRID_W) if d < 64 else (t % GRID_W)
        ang = pos.astype(f32) * inv[d % 32]
        C[d] = np.cos(ang)
        S[d] = np.sin(ang)
    c["ropeC"] = C
    c["ropeS"] = S
    for seg, Ls in enumerate((L, LC)):
        tt = np.linspace(0.0, 1.0, Ls, dtype=f32)[:, None]
        w = (f32(2.0 * math.pi / Ls) * np.arange(Ls, dtype=f32))[:, None]
        f = np.linspace(1e-4, 15, 16, dtype=f32)[None, :]
        z = np.concatenate([tt, np.cos(f * w), -np.sin(f * w)], axis=-1).astype(f32)
        c["zembT%d" % seg] = np.ascontiguousarray(z.T)
        deltas = np.abs(np.linspace(math.log(1e-2) / 1.5, math.log(1e-2) / 0.3, 1024, dtype=f32))
        c["decay%d" % seg] = np.exp(-tt * deltas[None, :]).astype(f32)
        N2 = 2 * Ls
        tf = np.arange(Ls, dtype=np.float64)
        ph = np.pi * np.outer(tf, 2 * tf + 1) / N2
        Cm = np.cos(ph).astype(f32)
        Sm = (-np.sin(ph)).astype(f32)
        c["dftC%d" % seg] = Cm
        c["dftS%d" % seg] = Sm
        c["dftCT%d" % seg] = np.ascontiguousarray(Cm.T)
        c["dftST%d" % seg] = np.ascontiguousarray(Sm.T)
    a = np.arange(128)
    triA = np.where(a[None, :] <= a[:, None], 0.0, NEG).astype(f32)
    triB = np.where(a[:, None] <= a[None, :], 0.0, NEG).astype(f32)
    c["evbt"] = np.ascontiguousarray(np.stack([triA, triB], axis=1)[None])
    return c


def host_inputs(b, x, c, ctx, c_ctx, ada_w, ada_b, ln_g, ln_b, ev_w_in, ev_w_out, hy_conv_w, hy_conv_b,
                hy_f_w1, hy_f_b1, hy_f_w2, hy_f_b2, hy_f_w3, hy_f_freq, hy_bias, swa_sink,
                od_w_in, od_w_out, na_rpb, moe_w_router, moe_w1, moe_w3, moe_w2):
    f32 = np.float32
    m = dict(host_consts())
    m["x"] = np.ascontiguousarray(np.concatenate([x[b], ctx[b]], axis=0))
    cc = np.stack([c[b], c_ctx], axis=-1)
    m["cT"] = np.ascontiguousarray(cc.reshape(16, 128, 2).transpose(1, 0, 2))
    for k, v in (("ada_w", ada_w), ("ada_b", ada_b), ("ln_g", ln_g), ("ln_b", ln_b), ("ev_w_in", ev_w_in), ("ev_w_out", ev_w_out),
                 ("od_w_in", od_w_in), ("od_w_out", od_w_out), ("hy_f_w1", hy_f_w1), ("hy_f_w2", hy_f_w2), ("hy_f_w3", hy_f_w3),
                 ("swa_sink", swa_sink), ("moe_w_router", moe_w_router), ("moe_w1", moe_w1), ("moe_w3", moe_w3), ("moe_w2", moe_w2)):
        m[k] = v
    cw = np.concatenate([hy_conv_w, hy_conv_b[:, None, :]], axis=1)
    m["hy_conv"] = np.ascontiguousarray(cw.reshape(2, 4, 24, 128).transpose(0, 3, 2, 1))
    m["hy_biasc"] = np.ascontiguousarray(hy_bias.reshape(2, 2, 8, 128).transpose(0, 3, 1, 2).reshape(2, 128, 16))
    m["hy_fcol"] = np.ascontiguousarray(np.stack([hy_f_freq, hy_f_b1, hy_f_b2[:, 0], hy_f_b2[:, 1]], axis=-1))
    plans, pats = na_structure()
    flat = na_rpb.reshape(2, 16, -1)
    g = flat[:, :, np.maximum(pats, 0)]
    g = np.where(pats[None, None] >= 0, g, f32(NEG)).astype(f32)
    m["nab"] = np.ascontiguousarray(g.transpose(0, 1, 3, 2, 4))
    return m


_NC_CACHE = {}


def kernel(**inputs):
    inputs = {k: np.asarray(v) for k, v in inputs.items()}
    if "nc" not in _NC_CACHE:
        _NC_CACHE["nc"] = Prog().build()
    nc = _NC_CACHE["nc"]
    B = inputs["x"].shape[0]
    in_maps = [host_inputs(b, **inputs) for b in range(B)]
    res = run_bass_kernel_spmd(nc, in_maps, core_ids=list(range(B)))
    out = np.stack([np.asarray(r["out"])[:L] for r in res.results], axis=0)
    return out.astype(np.float32)
```
